# Optimizing a Trainium2 kernel written in Bass

```python
import math
import jax, jax.numpy as jnp
from jax import lax
import numpy as np

D_MODEL = 1024
BATCH = 16
SEQ = 4096
DEPTH = 4

CTX_LEN = 256
GRID_W = 64
EPS = 1e-6

LRU_WIDTH = 1024
LRU_BLOCKS = 8
LRU_BLOCK_W = LRU_WIDTH // LRU_BLOCKS
LRU_CONV = 4
LRU_C = 8.0
ATT_HEADS = 8
ATT_KV_HEADS = 2
HEAD_DIM = 128
ATT_WIDTH = ATT_HEADS * HEAD_DIM
KV_WIDTH = ATT_KV_HEADS * HEAD_DIM
WINDOW = 128
BLOCK = 128
ROPE_BASE = 10000.0
HY_WIDTH = 1024
HY_ORDER = 2
HY_CONV = 3
HY_EMB = 33
HY_FILTER_DIM = 64
HY_DECAY_TARGET = 1e-2
HY_FAST_PCT = 0.3
HY_SLOW_PCT = 1.5
RET_HEADS = 8
RET_DK = 128
RET_DV = 128
RET_WIDTH = RET_HEADS * RET_DV
RET_CHUNK = 128

EVEN_IN = 2 * LRU_WIDTH + ATT_WIDTH + 2 * KV_WIDTH + ATT_WIDTH
EVEN_MIX = LRU_WIDTH + ATT_WIDTH
ODD_IN = (HY_ORDER + 1) * HY_WIDTH + HY_WIDTH + 2 * RET_HEADS * RET_DK + 2 * RET_WIDTH
ODD_MIX = HY_WIDTH + RET_WIDTH
N_EVEN = (DEPTH + 1) // 2
N_ODD = DEPTH // 2

kernel_name = 'hybrid_lru_swa_hyena_retention_dit'

F32 = jnp.float32


def _split(t, sizes):
    return jnp.split(t, np.cumsum(sizes)[:-1].tolist(), axis=-1)


def _flip(t):
    return jnp.flip(t, axis=1)


def _ident(t):
    return t


def rmsnorm(x, g):
    x32 = x.astype(F32)
    y = x32 * lax.rsqrt(jnp.mean(x32 * x32, axis=-1, keepdims=True) + EPS)
    return (y * g.astype(F32)).astype(x.dtype)


def depthwise_conv(x, w, b, left):
    K = w.shape[0]
    L = x.shape[1]
    xp = jnp.pad(x, ((0, 0), (left, K - 1 - left), (0, 0)))
    y = xp[:, 0:L] * w[0]
    for kk in range(1, K):
        y = y + xp[:, kk:kk + L] * w[kk]
    return y + b


def axial_rope_angles(seq):
    n_rows = seq // GRID_W
    row = jnp.repeat(jnp.arange(n_rows, dtype=F32), GRID_W)
    col = jnp.tile(jnp.arange(GRID_W, dtype=F32), n_rows)
    half = HEAD_DIM // 2
    inv = ROPE_BASE ** (-jnp.arange(0, half, 2, dtype=F32) / half)
    return row[:, None] * inv, col[:, None] * inv


def rope1d(x, ang):
    n = ang.shape[-1]
    cos = jnp.cos(ang)[:, None, :]
    sin = jnp.sin(ang)[:, None, :]
    x1, x2 = x[..., :n], x[..., n:]
    return jnp.concatenate([x1 * cos - x2 * sin, x2 * cos + x1 * sin], axis=-1)


def rope_2d(x, ang_r, ang_c):
    half = HEAD_DIM // 2
    y = jnp.concatenate([rope1d(x[..., :half], ang_r), rope1d(x[..., half:], ang_c)], axis=-1)
    return y.astype(x.dtype)


def _lin_combine(e1, e2):
    a1, b1 = e1
    a2, b2 = e2
    return a1 * a2, a2 * b1 + b2


def linear_scan(a, b, h0):
    a_cum, b_cum = lax.associative_scan(_lin_combine, (a, b), axis=1)
    return a_cum * h0[:, None] + b_cum


def blockdiag(x, w):
    Bn, L, W = x.shape
    xb = x.reshape(Bn, L, LRU_BLOCKS, LRU_BLOCK_W)
    return jnp.einsum('blnc,ncd->blnd', xb, w).reshape(Bn, L, W)


def rglru_coeffs(u, wa, ba, wx, bx, lam):
    r = jax.nn.sigmoid(blockdiag(u, wa) + ba)
    i = jax.nn.sigmoid(blockdiag(u, wx) + bx)
    log_a = -LRU_C * r * jax.nn.softplus(-lam)
    a = jnp.exp(log_a)
    b = jnp.sqrt(-jnp.expm1(2.0 * log_a)) * (i * u)
    return a.astype(F32), b.astype(F32)


def rglru_mixer(xa, xac, conv_w, conv_b, wa, ba, wx, bx, lam):
    left = LRU_CONV // 2
    u = depthwise_conv(xa, conv_w, conv_b, left).astype(F32)
    uc = depthwise_conv(xac, conv_w, conv_b, left).astype(F32)
    zero = jnp.zeros((uc.shape[0], LRU_WIDTH), F32)
    y = 0.0
    yc = 0.0
    for d in range(2):
        fl = _flip if d == 1 else _ident
        a, b = rglru_coeffs(fl(uc), wa[d], ba[d], wx[d], bx[d], lam[d])
        hc = linear_scan(a, b, zero)
        a, b = rglru_coeffs(fl(u), wa[d], ba[d], wx[d], bx[d], lam[d])
        hl = linear_scan(a, b, hc[:, -1])
        y = y + fl(hl)
        yc = yc + fl(hc)
    return y.astype(xa.dtype), yc.astype(xa.dtype)


def softmax_with_sink(scores, sink):
    m = sink
    for s in scores:
        m = jnp.maximum(m, jnp.max(s, axis=-1, keepdims=True))
    ps = [jnp.exp(s - m) for s in scores]
    denom = jnp.exp(sink - m)
    for p in ps:
        denom = denom + jnp.sum(p, axis=-1, keepdims=True)
    return [p / denom for p in ps]


def window_attention(q, k, v, kc, vc, sink):
    Bn, S = q.shape[0], q.shape[1]
    nb = S // BLOCK
    G = ATT_HEADS // ATT_KV_HEADS
    scale = HEAD_DIM ** -0.5
    qb = q.reshape(Bn, nb, BLOCK, ATT_KV_HEADS, G, HEAD_DIM)

    def band(t):
        tp = jnp.pad(t, ((0, 0), (BLOCK, BLOCK), (0, 0), (0, 0)))
        tp = tp.reshape(Bn, nb + 2, BLOCK, ATT_KV_HEADS, HEAD_DIM)
        return jnp.concatenate([tp[:, :-2], tp[:, 1:-1], tp[:, 2:]], axis=2)

    kb, vb = band(k), band(v)
    qi = jnp.arange(BLOCK)
    kj = jnp.arange(3 * BLOCK) - BLOCK
    rel = qi[:, None] - kj[None, :]
    kpos = jnp.arange(nb)[:, None] * BLOCK + kj[None, :]
    mask = (jnp.abs(rel) <= WINDOW)[None] & ((kpos >= 0) & (kpos < S))[:, None, :]
    s_loc = jnp.einsum('bnihgd,bnjhd->bnhgij', qb, kb).astype(F32) * scale
    s_loc = jnp.where(mask[None, :, None, None], s_loc, -jnp.inf)
    s_ctx = jnp.einsum('bnihgd,bchd->bnhgic', qb, kc).astype(F32) * scale
    sink_b = sink.reshape(ATT_KV_HEADS, G).astype(F32)[None, None, :, :, None, None]
    p_loc, p_ctx = softmax_with_sink([s_loc, s_ctx], sink_b)
    o = (jnp.einsum('bnhgij,bnjhd->bnihgd', p_loc, vb.astype(F32))
         + jnp.einsum('bnhgic,bchd->bnihgd', p_ctx, vc.astype(F32)))
    return o.reshape(Bn, S, ATT_WIDTH).astype(q.dtype)


def context_attention(qc, kc, vc, sink):
    Bn, Cn = qc.shape[0], qc.shape[1]
    G = ATT_HEADS // ATT_KV_HEADS
    qg = qc.reshape(Bn, Cn, ATT_KV_HEADS, G, HEAD_DIM)
    s = jnp.einsum('bihgd,bjhd->bhgij', qg, kc).astype(F32) * (HEAD_DIM ** -0.5)
    sink_b = sink.reshape(ATT_KV_HEADS, G).astype(F32)[None, :, :, None, None]
    (p,) = softmax_with_sink([s], sink_b)
    o = jnp.einsum('bhgij,bjhd->bihgd', p, vc.astype(F32))
    return o.reshape(Bn, Cn, ATT_WIDTH).astype(qc.dtype)


def hyena_filters(L, w1, b1, w2, b2, w3, freq):
    t = jnp.linspace(0.0, 1.0, L, dtype=F32)[:, None]
    bands = (HY_EMB - 1) // 2
    w = 2.0 * math.pi * jnp.arange(L, dtype=F32)[:, None] / L
    f = jnp.linspace(1e-4, bands - 1, bands, dtype=F32)[None]
    z = jnp.concatenate([t, jnp.cos(f * w), -jnp.sin(f * w)], axis=-1)
    hid = jnp.sin(freq * (z @ w1 + b1))
    hid = jnp.sin(freq * (hid @ w2 + b2))
    filt = (hid @ w3).astype(F32)
    max_decay = math.log(HY_DECAY_TARGET) / HY_FAST_PCT
    min_decay = math.log(HY_DECAY_TARGET) / HY_SLOW_PCT
    deltas = jnp.linspace(min_decay, max_decay, HY_WIDTH, dtype=F32)
    decay = jnp.exp(-t * jnp.abs(deltas))
    return filt.reshape(L, HY_ORDER, 2, HY_WIDTH) * decay[:, None, None, :]


def fft_conv_bidir(u, h_fwd, h_bwd, bias):
    L = u.shape[1]
    kern = jnp.concatenate([h_fwd, jnp.zeros((1, HY_WIDTH), F32), jnp.flip(h_bwd[1:], axis=0)], axis=0)
    kf = jnp.fft.rfft(kern, n=2 * L, axis=0)
    uf = jnp.fft.rfft(u.astype(F32), n=2 * L, axis=1)
    y = jnp.fft.irfft(uf * kf[None], n=2 * L, axis=1)[:, :L]
    return y + u.astype(F32) * bias.astype(F32)


def hyena_mixer(z, conv_w, conv_b, w1, b1, w2, b2, w3, freq, bias):
    L = z.shape[1]
    z = depthwise_conv(z, conv_w, conv_b, HY_CONV // 2)
    parts = jnp.split(z, HY_ORDER + 1, axis=-1)
    filt = hyena_filters(L, w1, b1, w2, b2, w3, freq)
    y = parts[0].astype(F32)
    for o in range(HY_ORDER):
        y = parts[o + 1].astype(F32) * fft_conv_bidir(y, filt[:, o, 0], filt[:, o, 1], bias[o])
    return y.astype(z.dtype)


def retention_dir(q, k, v, log_gamma, state0, include_diag):
    Bn, L, H, dk = q.shape
    dv = v.shape[-1]
    n = L // RET_CHUNK
    idx = jnp.arange(RET_CHUNK, dtype=F32)
    diff = idx[:, None] - idx[None, :]
    mask = (diff >= 0) if include_diag else (diff > 0)
    inner = jnp.where(mask, jnp.exp(jnp.maximum(diff, 0.0)[None] * log_gamma[:, None, None]), 0.0)
    q_dec = jnp.exp((idx[:, None] + 1.0) * log_gamma[None])
    k_dec = jnp.exp((RET_CHUNK - 1.0 - idx[:, None]) * log_gamma[None])
    c_dec = jnp.exp(RET_CHUNK * log_gamma)

    def chunks(t):
        return jnp.moveaxis(t.reshape(Bn, n, RET_CHUNK, H, t.shape[-1]), 1, 0)

    def step(state, qkv):
        qj, kj, vj = qkv
        att = jnp.einsum('bihd,bjhd->bhij', qj, kj) * inner
        o = (jnp.einsum('bhij,bjhe->bihe', att, vj)
             + jnp.einsum('bihd,bhde->bihe', qj, state) * q_dec[None, :, :, None])
        state = state * c_dec[None, :, None, None] + jnp.einsum('bjhd,bjhe->bhde', kj * k_dec[None, :, :, None], vj)
        return state, o

    state, o = lax.scan(step, state0, (chunks(q), chunks(k), chunks(v)))
    return jnp.moveaxis(o, 0, 1).reshape(Bn, L, H, dv), state


def _head_rms(o):
    return o * lax.rsqrt(jnp.mean(o * o, axis=-1, keepdims=True) + EPS)


def retention_mixer(q, k, v, qc, kc, vc, log_gamma):
    Bn, S = q.shape[0], q.shape[1]
    Cn = qc.shape[1]
    kscale = RET_DK ** -0.5
    q, k, v = q.astype(F32), k.astype(F32) * kscale, v.astype(F32)
    qc, kc, vc = qc.astype(F32), kc.astype(F32) * kscale, vc.astype(F32)
    zero = jnp.zeros((Bn, RET_HEADS, RET_DK, RET_DV), F32)
    o = 0.0
    oc = 0.0
    for d in range(2):
        fl = _flip if d == 1 else _ident
        o_c, s_c = retention_dir(fl(qc), fl(kc), fl(vc), log_gamma[d], zero, d == 0)
        o_l, _ = retention_dir(fl(q), fl(k), fl(v), log_gamma[d], s_c, d == 0)
        o = o + fl(o_l)
        oc = oc + fl(o_c)
    return _head_rms(o).reshape(Bn, S, RET_WIDTH), _head_rms(oc).reshape(Bn, Cn, RET_WIDTH)


def even_mixer(h, hc, w_in, w_out, conv_w, conv_b, wa, ba, wx, bx, lam, sink, ang_r, ang_c, need_ctx):
    Bn, S = h.shape[0], h.shape[1]
    Cn = hc.shape[1]
    sizes = (LRU_WIDTH, LRU_WIDTH, ATT_WIDTH, KV_WIDTH, KV_WIDTH, ATT_WIDTH)
    xa, ga, q, k, v, gb = _split(h @ w_in, sizes)
    xac, gac, qc, kc, vc, gbc = _split(hc @ w_in, sizes)
    ya, yac = rglru_mixer(xa, xac, conv_w, conv_b, wa, ba, wx, bx, lam)
    q = rope_2d(q.reshape(Bn, S, ATT_HEADS, HEAD_DIM), ang_r, ang_c)
    k = rope_2d(k.reshape(Bn, S, ATT_KV_HEADS, HEAD_DIM), ang_r, ang_c)
    v = v.reshape(Bn, S, ATT_KV_HEADS, HEAD_DIM)
    kc = kc.reshape(Bn, Cn, ATT_KV_HEADS, HEAD_DIM)
    vc = vc.reshape(Bn, Cn, ATT_KV_HEADS, HEAD_DIM)
    yb = window_attention(q, k, v, kc, vc, sink)
    y = jnp.concatenate([ya * jax.nn.silu(ga), yb * jax.nn.silu(gb)], axis=-1) @ w_out
    if not need_ctx:
        return y, None
    ybc = context_attention(qc.reshape(Bn, Cn, ATT_HEADS, HEAD_DIM), kc, vc, sink)
    yc = jnp.concatenate([yac * jax.nn.silu(gac), ybc * jax.nn.silu(gbc)], axis=-1) @ w_out
    return y, yc


def odd_mixer(h, hc, w_in, w_out, hy_conv_w, hy_conv_b, hy_w1, hy_b1, hy_w2, hy_b2, hy_w3, hy_freq,
              hy_bias, decay_logit, need_ctx):
    Bn, S = h.shape[0], h.shape[1]
    Cn = hc.shape[1]
    sizes = ((HY_ORDER + 1) * HY_WIDTH, HY_WIDTH, RET_HEADS * RET_DK, RET_HEADS * RET_DK, RET_WIDTH, RET_WIDTH)
    z, gh, q, k, v, gd = _split(h @ w_in, sizes)
    zc, ghc, qc, kc, vc, gdc = _split(hc @ w_in, sizes)
    hyp = (hy_conv_w, hy_conv_b, hy_w1, hy_b1, hy_w2, hy_b2, hy_w3, hy_freq, hy_bias)
    yh = hyena_mixer(z, *hyp)
    log_gamma = -jax.nn.softplus(-decay_logit.astype(F32))
    yd, ydc = retention_mixer(
        q.reshape(Bn, S, RET_HEADS, RET_DK), k.reshape(Bn, S, RET_HEADS, RET_DK),
        v.reshape(Bn, S, RET_HEADS, RET_DV), qc.reshape(Bn, Cn, RET_HEADS, RET_DK),
        kc.reshape(Bn, Cn, RET_HEADS, RET_DK), vc.reshape(Bn, Cn, RET_HEADS, RET_DV), log_gamma)
    yd = yd.astype(h.dtype)
    y = jnp.concatenate([yh * jax.nn.silu(gh), yd * jax.nn.silu(gd)], axis=-1) @ w_out
    if not need_ctx:
        return y, None
    yhc = hyena_mixer(zc, *hyp)
    yc = jnp.concatenate([yhc * jax.nn.silu(ghc), ydc.astype(h.dtype) * jax.nn.silu(gdc)], axis=-1) @ w_out
    return y, yc


def setup_inputs(seed: int = 0) -> dict:
    key = jax.random.key(seed)
    ks = iter(jax.random.split(key, 40))

    def nrm(shape, s):
        return jax.random.normal(next(ks), shape, F32) * s

    NE, NO = N_EVEN, N_ODD
    x = nrm((BATCH, SEQ, D_MODEL), 1.0)
    c = nrm((BATCH, D_MODEL), 1.0)
    ctx = nrm((BATCH, CTX_LEN, D_MODEL), 1.0)
    c_ctx = nrm((D_MODEL,), 1.0)
    mod_w = nrm((DEPTH, D_MODEL, 3 * D_MODEL), 0.5 * D_MODEL ** -0.5)
    mod_b = nrm((DEPTH, 3 * D_MODEL), 0.01)
    norm_pre = 1.0 + nrm((DEPTH, D_MODEL), 0.05)
    norm_post = 1.0 + nrm((DEPTH, D_MODEL), 0.05)
    ev_w_in = nrm((NE, D_MODEL, EVEN_IN), D_MODEL ** -0.5)
    ev_w_out = nrm((NE, EVEN_MIX, D_MODEL), EVEN_MIX ** -0.5)
    lru_conv_w = nrm((NE, LRU_CONV, LRU_WIDTH), LRU_CONV ** -0.5)
    lru_conv_b = nrm((NE, LRU_WIDTH), 0.01)
    lru_wa = nrm((NE, 2, LRU_BLOCKS, LRU_BLOCK_W, LRU_BLOCK_W), LRU_BLOCK_W ** -0.5)
    lru_ba = nrm((NE, 2, LRU_WIDTH), 0.01)
    lru_wx = nrm((NE, 2, LRU_BLOCKS, LRU_BLOCK_W, LRU_BLOCK_W), LRU_BLOCK_W ** -0.5)
    lru_bx = nrm((NE, 2, LRU_WIDTH), 0.01)
    a0 = jax.random.uniform(next(ks), (NE, 2, LRU_WIDTH), F32, 0.9, 0.999)
    lru_lambda = jnp.log(a0) - jnp.log1p(-a0)
    attn_sink = nrm((NE, ATT_HEADS), 1.0)
    od_w_in = nrm((NO, D_MODEL, ODD_IN), D_MODEL ** -0.5)
    od_w_out = nrm((NO, ODD_MIX, D_MODEL), ODD_MIX ** -0.5)
    hy_conv_w = nrm((NO, HY_CONV, (HY_ORDER + 1) * HY_WIDTH), HY_CONV ** -0.5)
    hy_conv_b = nrm((NO, (HY_ORDER + 1) * HY_WIDTH), 0.01)
    hy_w1 = nrm((NO, HY_EMB, HY_FILTER_DIM), HY_EMB ** -0.5)
    hy_b1 = nrm((NO, HY_FILTER_DIM), 0.1)
    hy_w2 = nrm((NO, HY_FILTER_DIM, HY_FILTER_DIM), HY_FILTER_DIM ** -0.5)
    hy_b2 = nrm((NO, HY_FILTER_DIM), 0.1)
    hy_w3 = nrm((NO, HY_FILTER_DIM, HY_ORDER * 2 * HY_WIDTH), 0.1 * HY_FILTER_DIM ** -0.5)
    hy_freq = 1.0 + nrm((NO, HY_FILTER_DIM), 0.1)
    hy_bias = nrm((NO, HY_ORDER, HY_WIDTH), 1.0)
    base_logit = jnp.log(2.0 ** (5.0 + jnp.arange(RET_HEADS, dtype=F32)) - 1.0)
    ret_decay_logit = base_logit[None, None] + nrm((NO, 2, RET_HEADS), 0.1)
    return {'x': x, 'c': c, 'ctx': ctx, 'c_ctx': c_ctx, 'mod_w': mod_w, 'mod_b': mod_b,
            'norm_pre': norm_pre, 'norm_post': norm_post, 'ev_w_in': ev_w_in, 'ev_w_out': ev_w_out,
            'lru_conv_w': lru_conv_w, 'lru_conv_b': lru_conv_b, 'lru_wa': lru_wa, 'lru_ba': lru_ba,
            'lru_wx': lru_wx, 'lru_bx': lru_bx, 'lru_lambda': lru_lambda, 'attn_sink': attn_sink,
            'od_w_in': od_w_in, 'od_w_out': od_w_out, 'hy_conv_w': hy_conv_w, 'hy_conv_b': hy_conv_b,
            'hy_w1': hy_w1, 'hy_b1': hy_b1, 'hy_w2': hy_w2, 'hy_b2': hy_b2, 'hy_w3': hy_w3,
            'hy_freq': hy_freq, 'hy_bias': hy_bias, 'ret_decay_logit': ret_decay_logit}


def reference(x, c, ctx, c_ctx, mod_w, mod_b, norm_pre, norm_post, ev_w_in, ev_w_out, lru_conv_w,
              lru_conv_b, lru_wa, lru_ba, lru_wx, lru_bx, lru_lambda, attn_sink, od_w_in, od_w_out,
              hy_conv_w, hy_conv_b, hy_w1, hy_b1, hy_w2, hy_b2, hy_w3, hy_freq, hy_bias, ret_decay_logit):
    S = x.shape[1]
    ang_r, ang_c = axial_rope_angles(S)
    sc = jax.nn.silu(c)
    scc = jax.nn.silu(c_ctx)
    for l in range(DEPTH):
        need_ctx = l < DEPTH - 1
        shift, scale, gate = jnp.split(sc @ mod_w[l] + mod_b[l], 3, axis=-1)
        shift_c, scale_c, gate_c = jnp.split(scc @ mod_w[l] + mod_b[l], 3, axis=-1)
        h = rmsnorm(x, norm_pre[l]) * (1.0 + scale[:, None]) + shift[:, None]
        hc = rmsnorm(ctx, norm_pre[l]) * (1.0 + scale_c) + shift_c
        if l % 2 == 0:
            e = l // 2
            y, yc = even_mixer(h, hc, ev_w_in[e], ev_w_out[e], lru_conv_w[e], lru_conv_b[e], lru_wa[e],
                               lru_ba[e], lru_wx[e], lru_bx[e], lru_lambda[e], attn_sink[e], ang_r, ang_c,
                               need_ctx)
        else:
            o = l // 2
            y, yc = odd_mixer(h, hc, od_w_in[o], od_w_out[o], hy_conv_w[o], hy_conv_b[o], hy_w1[o],
                              hy_b1[o], hy_w2[o], hy_b2[o], hy_w3[o], hy_freq[o], hy_bias[o],
                              ret_decay_logit[o], need_ctx)
        x = x + gate[:, None] * rmsnorm(y, norm_post[l])
        if need_ctx:
            ctx = ctx + gate_c * rmsnorm(yc, norm_post[l])
    return x
```

```python
import numpy as np
import concourse.bass as bass
import concourse.mybir as mybir

F32 = mybir.dt.float32
BF16 = mybir.dt.bfloat16
AF = mybir.ActivationFunctionType
ALU = mybir.AluOpType
AX = mybir.AxisListType

SEM_CHUNK = 20000
NDMA_SEMS = 12


class V:
    __slots__ = ("ap", "units")

    def __init__(self, ap, units):
        self.ap = ap
        self.units = tuple(units)

    def __getitem__(self, idx):
        return V(self.ap[idx], self.units)

    def re(self, pat, **kw):
        return V(self.ap.rearrange(pat, **kw), self.units)

    def bc(self, dt):
        return V(self.ap.bitcast(dt), self.units)


class Op:
    __slots__ = ("eng", "fn", "deps", "dma", "sig", "waits", "idx", "has_dep")

    def __init__(self, eng, fn, deps, dma):
        self.eng = eng
        self.fn = fn
        self.deps = deps
        self.dma = dma
        self.sig = None
        self.waits = None
        self.has_dep = False


class Builder:
    def __init__(self, nc):
        self.nc = nc
        self.ops = []
        self.units = {}
        self.last_dma = {"sp": [], "act": [], "pool": []}
        self.pending = {}
        self.psum_units = set()
        self.arena = None
        self.arena_off = 0
        self.arena_base = 0
        self.uid = 0

    def init_arena(self, nbytes):
        self.arena_bytes = nbytes
        self.arena = self.nc.alloc_sbuf_tensor("arena", [128, nbytes // 4], F32)

    def sbt(self, name, shape, dtype=F32, glob=False):
        esz = 2 if dtype == BF16 else 4
        n = 1
        for d in shape[1:]:
            n *= d
        nb = (n * esz + 31) // 32 * 32
        off = self.arena_off
        assert off + nb <= self.arena_bytes, (name, off, nb)
        self.arena_off += nb
        ap = self.arena[:, off // 4:(off + nb) // 4]
        if esz == 2:
            ap = ap.bitcast(BF16)
        elif dtype != F32:
            ap = ap.bitcast(dtype)
        ap = ap[:, 0:n]
        if len(shape) == 3:
            ap = ap.rearrange("p (a b) -> p a b", a=shape[1])
        elif len(shape) == 4:
            ap = ap.rearrange("p (a b c) -> p a b c", a=shape[1], b=shape[2])
        if shape[0] != 128:
            ap = ap[0:shape[0]]
        self.uid += 1
        return V(ap, (f"{name}#{self.uid}",))

    def phase(self):
        self.barrier()
        self.arena_off = self.arena_base

    def freeze_globals(self):
        self.arena_base = self.arena_off

    def barrier(self):
        deps = set()
        last = {}
        for i, op in enumerate(self.ops):
            if not op.dma:
                last[op.eng] = i
        deps.update(last.values())
        for q, lst in self.last_dma.items():
            deps.update(lst[-NDMA_SEMS:])
        for e in ("pe", "act", "dve", "pool", "sp"):
            self.pending[e] = set(deps)

    def sb(self, name, shape, dtype=F32, units=None):
        h = self.nc.alloc_sbuf_tensor(name, list(shape), dtype)
        return V(h[:], units if units is not None else (name,))

    def ps(self, name, shape, dtype=F32):
        h = self.nc.alloc_psum_tensor(name, list(shape), dtype)
        self.psum_units.add(name)
        return V(h[:], (name,))

    def dram(self, name, shape, dtype=F32, kind="Internal"):
        h = self.nc.dram_tensor(name, list(shape), dtype, kind=kind)
        return V(h.ap(), (name,))

    def add(self, eng, fn, r=(), w=(), dma=False):
        deps = set()
        ru = []
        wu = []
        for v in r:
            ru.extend(v.units if isinstance(v, V) else (v,))
        for v in w:
            wu.extend(v.units if isinstance(v, V) else (v,))
        pr = [u for u in ru if u in self.psum_units]
        if pr:
            ru = [u for u in ru if u not in self.psum_units]
            wu = wu + pr
        for u in ru:
            st = self.units.get(u)
            if st is not None and st[0] is not None:
                deps.add(st[0])
        for u in wu:
            st = self.units.get(u)
            if st is not None:
                if st[0] is not None:
                    deps.add(st[0])
                deps.update(st[1])
        opid = len(self.ops)
        pend = self.pending.pop(eng, None)
        if pend:
            deps.update(pend)
        if dma:
            q = self.last_dma[eng]
            if len(q) >= NDMA_SEMS:
                deps.add(q[-NDMA_SEMS])
            q.append(opid)
        if eng == "pe" and not dma:
            deps = {d for d in deps if not (self.ops[d].eng == "pe" and not self.ops[d].dma)}
        op = Op(eng, fn, deps, dma)
        self.ops.append(op)
        for d in deps:
            self.ops[d].has_dep = True
        for u in ru:
            st = self.units.get(u)
            if st is None:
                st = self.units[u] = [None, []]
            st[1].append(opid)
        for u in wu:
            self.units[u] = [opid, []]
        return opid

    def mm(self, out, lhsT, rhs, start=True, stop=True):
        self.add("pe", lambda e: e.matmul(out.ap, lhsT.ap, rhs.ap, start=start, stop=stop),
                 r=(lhsT, rhs), w=(out,))

    def tr(self, out, in_, ident):
        self.add("pe", lambda e: e.transpose(out.ap, in_.ap, ident.ap), r=(in_, ident), w=(out,))

    def act(self, out, in_, func, bias=None, scale=None, accum=None, eng="act"):
        kw = {}
        r = [in_]
        w = [out]
        if bias is not None:
            if isinstance(bias, V):
                kw["bias"] = bias.ap
                r.append(bias)
            else:
                kw["bias"] = bias
        if scale is not None:
            if isinstance(scale, V):
                kw["scale"] = scale.ap
                r.append(scale)
            else:
                kw["scale"] = scale
        if accum is not None:
            kw["accum_out"] = accum.ap
            w.append(accum)
        self.add("act", lambda e: e.activation(out.ap, in_.ap, func, **kw), r=r, w=w)

    def tt(self, out, a, b, op, eng="dve"):
        self.add(eng, lambda e: e.tensor_tensor(out.ap, a.ap, b.ap, op), r=(a, b), w=(out,))

    def ts(self, out, a, s1, s2, op0, op1=None, eng="dve", accum=None):
        r = [a]
        w = [out]
        a1 = s1
        a2 = s2
        if isinstance(s1, V):
            r.append(s1)
            a1 = s1.ap
        if isinstance(s2, V):
            r.append(s2)
            a2 = s2.ap
        kw = {}
        if a2 is None:
            a2 = 0.0
            op1 = ALU.add
        if op1 is not None:
            kw["op1"] = op1
        if accum is not None:
            kw["accum_out"] = accum.ap
            w.append(accum)
        self.add(eng, lambda e: e.tensor_scalar(out.ap, a.ap, a1, a2, op0, **kw), r=r, w=w)

    def stt(self, out, a, s, b, op0, op1, eng="dve"):
        r = [a, b]
        sa = s
        if isinstance(s, V):
            r.append(s)
            sa = s.ap
        self.add(eng, lambda e: e.scalar_tensor_tensor(out.ap, a.ap, sa, b.ap, op0, op1), r=r, w=(out,))

    def scan(self, out, d0, d1, init, op0=ALU.mult, op1=ALU.add):
        r = [d0, d1]
        ia = init
        if isinstance(init, V):
            r.append(init)
            ia = init.ap
        self.add("dve", lambda e: e.tensor_tensor_scan(out.ap, d0.ap, d1.ap, ia, op0, op1), r=r, w=(out,))

    def copy(self, out, in_, eng="dve"):
        if eng == "act":
            self.add("act", lambda e: e.copy(out.ap, in_.ap), r=(in_,), w=(out,))
        else:
            self.add(eng, lambda e: e.tensor_copy(out.ap, in_.ap), r=(in_,), w=(out,))

    def memset(self, out, val, eng="dve"):
        self.add(eng, lambda e: e.memset(out.ap, val), w=(out,))

    def recip(self, out, in_):
        self.add("dve", lambda e: e.reciprocal(out.ap, in_.ap), r=(in_,), w=(out,))

    def dma(self, out, in_, q="sp"):
        self.add(q, lambda e: e.dma_start(out=out.ap, in_=in_.ap), r=(in_,), w=(out,), dma=True)

    def emit(self, final_wait_units=()):
        nc = self.nc
        ops = self.ops
        engs = ("pe", "act", "dve", "pool", "sp")
        final_deps = set()
        for u in final_wait_units:
            st = self.units.get(u)
            if st is not None and st[0] is not None:
                final_deps.add(st[0])
        for d in final_deps:
            ops[d].has_dep = True
        n_sig = {e: 0 for e in engs}
        n_dma = {"sp": 0, "act": 0, "pool": 0}
        for op in ops:
            if op.dma:
                k = n_dma[op.eng]
                n_dma[op.eng] += 1
                op.sig = ("dma", op.eng, k % NDMA_SEMS, 16 * (k // NDMA_SEMS + 1))
            elif op.has_dep:
                k = n_sig[op.eng]
                n_sig[op.eng] += 1
                op.sig = ("cmp", op.eng, k // SEM_CHUNK, k % SEM_CHUNK + 1)
        sems = {}
        for e in engs:
            for c in range((n_sig[e] + SEM_CHUNK - 1) // SEM_CHUNK):
                sems[("cmp", e, c)] = nc.alloc_semaphore(name=f"s_{e}_{c}")
        for q, n in n_dma.items():
            for c in range(min(n, NDMA_SEMS)):
                sems[("dma", q, c)] = nc.alloc_semaphore(name=f"d_{q}_{c}")
        self.n_sems = len(sems)
        by_eng = {e: [] for e in engs}
        waited = {e: {} for e in engs}
        for op in ops:
            ws = {}
            for d in op.deps:
                s = ops[d].sig
                key = s[:3]
                if ws.get(key, 0) < s[3]:
                    ws[key] = s[3]
            wl = []
            wd = waited[op.eng]
            for key, val in ws.items():
                if wd.get(key, 0) < val:
                    wd[key] = val
                    wl.append((key, val))
            op.waits = wl
            by_eng[op.eng].append(op)
        fin = []
        fw = {}
        for d in final_deps:
            s = ops[d].sig
            if fw.get(s[:3], 0) < s[3]:
                fw[s[:3]] = s[3]
        fin = list(fw.items())

        def run(engine, name):
            for op in by_eng[name]:
                for key, val in op.waits:
                    engine.wait_ge(sems[key], val)
                ins = op.fn(engine)
                if op.sig is not None:
                    ins.then_inc(sems[op.sig[:3]], 16 if op.dma else 1)
            if name == "sp":
                for key, val in fin:
                    engine.wait_ge(sems[key], val)

        with nc.Block() as block:
            @block.tensor
            def _(e):
                run(e, "pe")

            @block.scalar
            def _(e):
                run(e, "act")

            @block.vector
            def _(e):
                run(e, "dve")

            @block.gpsimd
            def _(e):
                run(e, "pool")

            @block.sync
            def _(e):
                run(e, "sp")
        return {e: len(by_eng[e]) for e in engs}


import math
import ml_dtypes
from concourse.ap import AP
from concourse.bass_utils import run_bass_kernel_spmd

NB = 2
S = 4096
CTX = 256
T = S + CTX
D = 1024
TT = 256
NT = T // TT
DEPTH = 4
EPS = 1e-6
I32 = mybir.dt.int32
BF = ml_dtypes.bfloat16


def sp_layout():
    lay = {}
    off = 0

    def reg(name, n):
        nonlocal off
        lay[name] = (off, n)
        off += n
    reg("mod_b", 96)
    reg("norm_pre", 32)
    reg("norm_post", 32)
    for e in range(2):
        reg(f"lru_cw{e}", 32)
        reg(f"lru_cb{e}", 8)
        reg(f"lru_ba{e}", 16)
        reg(f"lru_bx{e}", 16)
        reg(f"lru_lam{e}", 16)
        reg(f"sink{e}", 8)
    for o in range(2):
        reg(f"hy_cw{o}", 72)
        reg(f"hy_cb{o}", 24)
        reg(f"hy_b1{o}", 1)
        reg(f"hy_freq{o}", 1)
        reg(f"hy_b2{o}", 1)
        reg(f"ret_logit{o}", 16)
    return lay, off


def chunked(v):
    v = np.asarray(v, np.float32)
    lead = v.shape[:-1]
    n = v.shape[-1] // 128
    a = v.reshape(lead + (n, 128))
    a = np.moveaxis(a, -1, 0)
    return a.reshape(128, -1)


def host_sp(inp):
    lay, n = sp_layout()
    sp = np.zeros((128, n), np.float32)

    def put(name, arr):
        o, c = lay[name]
        assert arr.shape == (128, c), (name, arr.shape, c)
        sp[:, o:o + c] = arr
    put("mod_b", chunked(inp["mod_b"].reshape(4, 3, 1024)))
    put("norm_pre", chunked(inp["norm_pre"]))
    put("norm_post", chunked(inp["norm_post"]))
    for e in range(2):
        put(f"lru_cw{e}", chunked(inp["lru_conv_w"][e]))
        put(f"lru_cb{e}", chunked(inp["lru_conv_b"][e]))
        put(f"lru_ba{e}", chunked(inp["lru_ba"][e]))
        put(f"lru_bx{e}", chunked(inp["lru_bx"][e]))
        put(f"lru_lam{e}", chunked(inp["lru_lambda"][e]))
        put(f"sink{e}", np.broadcast_to(inp["attn_sink"][e][None, :], (128, 8)))
    for o in range(2):
        put(f"hy_cw{o}", chunked(inp["hy_conv_w"][o]))
        put(f"hy_cb{o}", chunked(inp["hy_conv_b"][o]))
        for nm, key in (("hy_b1", "hy_b1"), ("hy_freq", "hy_freq"), ("hy_b2", "hy_b2")):
            col = np.zeros((128, 1), np.float32)
            col[:64, 0] = inp[key][o]
            put(f"{nm}{o}", col)
        put(f"ret_logit{o}", np.broadcast_to(inp["ret_decay_logit"][o].reshape(1, 16), (128, 16)))
    return sp


def host_consts():
    c = {}
    c["ident_f"] = np.eye(128, dtype=np.float32)
    half = 64
    inv = (10000.0 ** (-np.arange(0, half, 2, dtype=np.float32) / half)).astype(np.float32)
    tok = np.arange(S)
    row = (tok // 64).astype(np.float32)
    col = (tok % 64).astype(np.float32)
    ang_r = row[:, None] * inv[None]
    ang_c = col[:, None] * inv[None]
    cosT = np.ones((128, T), np.float32)
    sinT = np.zeros((128, T), np.float32)
    for d in range(128):
        ang = ang_r if d < 64 else ang_c
        j = d % 32
        first = (d % 64) < 32
        cosT[d, CTX:] = np.cos(ang[:, j])
        sinT[d, CTX:] = (-1.0 if first else 1.0) * np.sin(ang[:, j])
    c["ropec"] = cosT
    c["ropes"] = sinT
    pm = np.zeros((128, 128), np.float32)
    for d in range(128):
        partner = d + 32 if (d % 64) < 32 else d - 32
        pm[partner, d] = 1.0
    c["perm"] = pm
    jj = np.arange(128)[:, None]
    ii = np.arange(128)[None, :]
    mprev = (jj >= ii).astype(np.float32)
    mnext = (jj <= ii).astype(np.float32)
    c["mprev"] = np.tile(mprev[:, None, :], (1, 4, 1)).reshape(128, 512)
    c["mnext"] = np.tile(mnext[:, None, :], (1, 4, 1)).reshape(128, 512)
    c.update(host_consts_odd())
    c.update(host_consts_hy())
    c.update(host_consts_fft())
    return c


class Ctx:
    pass


def build_program(n_layers=DEPTH, debug=False, stop=99):
    nc = bass.Bass("TRN2", target_bir_lowering=False)
    k = Builder(nc)
    g = Ctx()
    g.k = k
    lay, nsp = sp_layout()
    g.lay = lay
    EI = "ExternalInput"
    g.xin = k.dram("xin", [NB, D, T], F32, kind=EI)
    g.cT = k.dram("cT", [128, 8, 3], F32, kind=EI)
    g.spd = k.dram("sp", [128, nsp], F32, kind=EI)
    g.mod_w = k.dram("mod_w", [4, D, 3 * D], F32, kind=EI)
    g.ev_w_in = k.dram("ev_w_in", [2, D, 4608], F32, kind=EI)
    g.ev_w_out = k.dram("ev_w_out", [2, 2048, D], F32, kind=EI)
    g.lru_wa = k.dram("lru_wa", [2, 2, 8, 128, 128], F32, kind=EI)
    g.lru_wx = k.dram("lru_wx", [2, 2, 8, 128, 128], F32, kind=EI)
    g.cst = {}
    for nm, shp in (("ident_f", [128, 128]), ("ropec", [128, T]), ("ropes", [128, T]), ("perm", [128, 128]),
                    ("mprev", [128, 512]), ("mnext", [128, 512])):
        g.cst[nm] = k.dram(nm, shp, F32, kind=EI)
    g.yout = k.dram("yout", [NB, D, T], F32, kind="ExternalOutput")
    g.XR = k.dram("XR", [NB, D, T], F32)
    g.XA = k.dram("XA", [NB, D, T], F32)
    g.GA = k.dram("GA", [NB, D, T], BF16)
    g.Q = k.dram("Q", [NB, D, T], BF16)
    g.K = k.dram("K", [NB, 256, T], BF16)
    g.VT = k.dram("VT", [NB, T, 256], BF16)
    g.GB = k.dram("GB", [NB, D, T], BF16)
    g.MIXT = k.dram("MIXT", [NB, 2048, T], BF16)
    g.od_w_in = k.dram("od_w_in", [2, D, 8192], F32, kind=EI)
    g.od_w_out = k.dram("od_w_out", [2, 2048, D], F32, kind=EI)
    g.hy_w1 = k.dram("hy_w1", [2, 33, 64], F32, kind=EI)
    g.hy_w2 = k.dram("hy_w2", [2, 64, 64], F32, kind=EI)
    g.hy_w3 = k.dram("hy_w3", [2, 64, 4096], F32, kind=EI)
    g.hy_bias = k.dram("hy_bias", [2, 2, 1024], F32, kind=EI)
    for nm, shp, dt_ in (("retc", [128, 770], F32), ("zemb4096", [33, 4096], F32), ("zemb256", [33, 256], F32),
                         ("tneg4096", [128, 32], F32), ("tneg256", [128, 2], F32), ("dabs", [128, 1024], F32),
                         ("altc", [128, 1], F32), ("altrow", [1, 256], F32),
                         ("cm4096", [4096, 4096], BF16), ("sm4096", [4096, 4096], BF16),
                         ("cm256", [256, 256], BF16), ("sm256", [256, 256], BF16)):
        g.cst[nm] = k.dram(nm, shp, dt_, kind=EI)
    for nm, shp, dt_ in (("f1tab", [32, 128], BF16), ("t2tab", [128, 64, 256], BF16), ("t3tab", [128, 64, 3, 128], BF16),
                         ("gtab", [128, 32], BF16), ("pm1", [1, 64], F32)):
        g.cst[nm] = k.dram(nm, shp, dt_, kind=EI)
    g.Bd = [k.dram(f"Bd{i}", [128, 128, 512], BF16) for i in range(4)]
    g.Dd = [k.dram(f"Dd{i}", [128, 128, 512], BF16) for i in range(2)]
    g.KR2 = k.dram("KR2", [2, 64, 64, D], F32)
    g.KS2 = k.dram("KS2", [2, 64, 64, D], F32)
    g.KN2 = k.dram("KN2", [2, 1, D], F32)
    g.Z = k.dram("Z", [NB, 3072, T], F32)
    g.ZC = k.dram("ZC", [NB, 2048, T], F32)
    g.GH = k.dram("GH", [NB, D, T], BF16)
    g.RQ = k.dram("RQ", [NB, D, T], BF16)
    g.RK = k.dram("RK", [NB, D, T], BF16)
    g.RKT = k.dram("RKT", [NB, T, D], BF16)
    g.RVT = k.dram("RVT", [NB, T, D], BF16)
    g.GD = k.dram("GD", [NB, D, T], BF16)
    g.U1T = k.dram("U1T", [NB, T, D], BF16)
    g.U2T = k.dram("U2T", [NB, T, D], BF16)
    g.HS = {L: k.dram(f"HS{L}", [2, L, D], BF16) for L in (S, CTX)}
    g.HD = {L: k.dram(f"HD{L}", [2, L, D], BF16) for L in (S, CTX)}
    g.KR = {L: k.dram(f"KR{L}", [2, L, D], F32) for L in (S, CTX)}
    g.KI = {L: k.dram(f"KI{L}", [2, L, D], F32) for L in (S, CTX)}
    g.KN = {L: k.dram(f"KN{L}", [2, 1, D], F32) for L in (S, CTX)}

    k.init_arena(200 * 1024)
    g.P = [k.ps(f"P{i}", [128, 512], F32) for i in range(8)]
    g.sp = k.sbt("sp", [128, nsp])
    k.dma(g.sp, g.spd)
    g.ident_f = k.sbt("ident_f", [128, 128])
    k.dma(g.ident_f, g.cst["ident_f"], q="act")
    g.ident_b = k.sbt("ident_b", [128, 128], BF16)
    k.copy(g.ident_b, g.ident_f)
    g.ones_b = k.sbt("ones_b", [128, 128], BF16)
    k.memset(g.ones_b, 1.0)
    g.modA = k.sbt("modA", [128, 4, 8, 3])
    g.modB = k.sbt("modB", [128, 4, 8, 3])
    g.modG = k.sbt("modG", [128, 4, 8, 3])
    g.ynq = k.sbt("ynq_g", [1, 2, 512], BF16)
    k.freeze_globals()

    phase_mod(g)
    for l in range(n_layers):
        last = (l == n_layers - 1)
        if l % 2 == 0:
            e = l // 2
            if stop >= 1:
                phase_proj(g, l, g.ev_w_in[e], 4608, even_specs(g))
            if stop >= 2:
                phase_lru(g, l, e)
            if stop >= 3:
                phase_attn(g, l, e)
            if stop >= 4:
                phase_out(g, l, g.ev_w_out[e], last)
        else:
            o = l // 2
            if stop >= 1:
                phase_proj(g, l, g.od_w_in[o][:, 0:4096], 4096, odd_specs1(g))
                phase_proj(g, l, g.od_w_in[o][:, 4096:8192], 4096, odd_specs2(g))
            if stop >= 2:
                phase_ret(g, l, o)
            if stop >= 3:
                for L in (S, CTX):
                    phase_hyfilt(g, o, L)
                phase_hyspec(g, o, CTX)
                phase_hyspec2(g, o)
                phase_hyprep(g, o)
            if stop >= 4:
                for b in range(NB):
                    for o2 in range(2):
                        phase_hyconv(g, l, o, b, 0, o2)
                        phase_hyconv2(g, l, o, b, o2)
            if stop >= 5:
                phase_out(g, l, g.od_w_out[o], last)
    stats = k.emit(final_wait_units=("yout",))
    print("ops per engine", stats, "sems", k.n_sems, flush=True)
    return nc


def spcol(g, name, i=0, n=1):
    o, c = g.lay[name]
    return g.sp[:, o + i:o + i + n]


def phase_mod(g):
    k = g.k
    k.phase()
    c_sb = k.sbt("c_sb", [128, 8, 3])
    k.dma(c_sb, g.cT)
    sc = k.sbt("sc", [128, 8, 3])
    k.act(sc, c_sb, AF.Silu)
    raw = k.sbt("modraw", [128, 96, 3])
    wm = [k.sbt(f"wm{i}", [128, 8, 1024]) for i in range(2)]
    it = 0
    for l in range(DEPTH):
        for part in range(3):
            w = wm[it % 2]
            it += 1
            src = g.mod_w[l][:, part * 1024:(part + 1) * 1024].re("(dc p) f -> p dc f", p=128)
            k.dma(w, src, q="sp" if it % 2 else "act")
            ps = g.P[it % 2]
            for fc in range(8):
                for dc in range(8):
                    k.mm(ps[:, fc * 4:fc * 4 + 3], w[:, dc, fc * 128:(fc + 1) * 128], sc[:, dc, :],
                         start=(dc == 0), stop=(dc == 7))
            for fc in range(8):
                k.ts(raw[:, (l * 3 + part) * 8 + fc, :], ps[:, fc * 4:fc * 4 + 3],
                     spcol(g, "mod_b", (l * 3 + part) * 8 + fc), None, ALU.add)
    for l in range(DEPTH):
        for dc in range(8):
            k.ts(g.modA[:, l, dc, :], raw[:, (l * 3 + 1) * 8 + dc, :], 1.0, spcol(g, "norm_pre", l * 8 + dc), ALU.add, ALU.mult)
            k.copy(g.modB[:, l, dc, :], raw[:, (l * 3 + 0) * 8 + dc, :])
            k.ts(g.modG[:, l, dc, :], raw[:, (l * 3 + 2) * 8 + dc, :], spcol(g, "norm_post", l * 8 + dc), None, ALU.mult)


def even_specs(g):
    return [
        (0, 1024, "raw32", g.XA),
        (1024, 1024, "silu16", g.GA),
        (2048, 1024, "rope16", g.Q),
        (3072, 256, "rope16", g.K),
        (3328, 256, "tok16", g.VT),
        (3584, 1024, "silu16", g.GB),
    ]


def load_weight_bf16(g, name, w_dram, ncols, nchunks):
    k = g.k
    wb = []
    for dc in range(nchunks):
        t = k.sbt(f"{name}{dc}", [128, ncols], BF16)
        step = 2304 if ncols % 2304 == 0 else ncols
        for c0 in range(0, ncols, step):
            k.dma(t[:, c0:c0 + step], w_dram[dc * 128:(dc + 1) * 128, c0:c0 + step], q="pool")
        wb.append(t)
    return wb


def norm_load(g, b, tt, xsrc, xs):
    k = g.k
    t0 = tt * TT
    k.dma(xs, xsrc[b][:, t0:t0 + TT].re("(dc p) t -> p dc t", p=128), q="pool")


def norm_compute(g, l, b, tt, xs, hT, scr, msbank):
    k = g.k
    j = 2 if tt == 0 else b
    sq = scr["sq"]
    k.act(sq, xs, AF.Square)
    ms = msbank[:, 0:TT]
    for dc in range(8):
        k.mm(ms, g.ones_b, sq[:, dc, :], start=(dc == 0), stop=(dc == 7))
    rstd = scr["rstd"]
    k.act(rstd, ms, AF.Sqrt, bias=g.epsc, scale=1.0 / D)
    k.recip(rstd, rstd)
    tmp = scr["tmp"]
    for dc in range(8):
        k.tt(tmp[:, dc, :], xs[:, dc, :], rstd, ALU.mult)
        k.ts(hT[:, dc, :], tmp[:, dc, :], g.modA[:, l, dc, j:j + 1], g.modB[:, l, dc, j:j + 1],
             ALU.mult, ALU.add)


def phase_proj(g, l, w_dram, F, specs):
    k = g.k
    k.phase()
    wb = load_weight_bf16(g, "wb", w_dram, F, 8)
    xsrc = g.xin if l == 0 else g.XR
    g.epsc = k.sbt("epsc", [128, 1])
    k.memset(g.epsc, EPS)
    need_rope = any(s[2] == "rope16" for s in specs)
    if need_rope:
        ropec = k.sbt("ropec", [128, T])
        ropes = k.sbt("ropes", [128, T])
        k.dma(ropec, g.cst["ropec"], q="act")
        k.dma(ropes, g.cst["ropes"], q="act")
        permf = k.sbt("permf", [128, 128])
        k.dma(permf, g.cst["perm"], q="act")
        perm = k.sbt("perm", [128, 128], BF16)
        k.copy(perm, permf)
    xs = [k.sbt(f"xs{i}", [128, 8, TT]) for i in range(3)]
    hT = [k.sbt(f"hT{i}", [128, 8, TT], BF16) for i in range(2)]
    scr = {"sq": k.sbt("sq", [128, 8, TT], BF16), "rstd": k.sbt("rstd", [128, TT]),
           "tmp": k.sbt("tmp", [128, 8, TT])}
    st32 = [k.sbt(f"st32_{i}", [128, 8, TT]) for i in range(2)]
    st16 = [k.sbt(f"st16_{i}", [128, 8, TT], BF16) for i in range(2)]
    stt16 = [k.sbt(f"stt16_{i}", [128, 512], BF16) for i in range(2)]
    qb = [k.sbt(f"qb{i}", [128, TT], BF16) for i in range(2)]
    t1 = [k.sbt(f"rt1_{i}", [128, TT]) for i in range(2)]
    t2 = [k.sbt(f"rt2_{i}", [128, TT]) for i in range(2)]
    cnt = {"ps": 0, "s32": 0, "s16": 0, "t16": 0, "q": 0, "dq": 0}
    tiles = [(b_, t_) for b_ in range(NB) for t_ in range(NT)]
    norm_load(g, tiles[0][0], tiles[0][1], xsrc, xs[0])
    norm_load(g, tiles[1][0], tiles[1][1], xsrc, xs[1])
    norm_compute(g, l, tiles[0][0], tiles[0][1], xs[0], hT[0], scr, g.P[7])
    for it, (b, tt) in enumerate(tiles):
        if True:
            t0 = tt * TT
            h_ = hT[it % 2]
            if it + 2 < len(tiles):
                norm_load(g, tiles[it + 2][0], tiles[it + 2][1], xsrc, xs[(it + 2) % 3])
            if it + 1 < len(tiles):
                norm_compute(g, l, tiles[it + 1][0], tiles[it + 1][1], xs[(it + 1) % 3], hT[(it + 1) % 2], scr, g.P[7])
            for (c0, ncols, kind, dest) in specs:
                if kind == "tok16":
                    for sb in range(TT // 128):
                        for cc in range(0, ncols, 512):
                            cw = min(512, ncols - cc)
                            ps = g.P[6][:, 0:cw]
                            for dc in range(8):
                                k.mm(ps, h_[:, dc, sb * 128:(sb + 1) * 128], wb[dc][:, c0 + cc:c0 + cc + cw],
                                     start=(dc == 0), stop=(dc == 7))
                            st = stt16[cnt["t16"] % 2][:, 0:cw]
                            cnt["t16"] += 1
                            scale = dest_scale(kind, dest, g)
                            k.act(st, ps, AF.Copy, scale=scale)
                            dstt = dest[0] if isinstance(dest, tuple) else dest
                            r0t = c0 - spec_base(specs, dstt)
                            k.dma(dstt[b][t0 + sb * 128:t0 + (sb + 1) * 128, r0t + cc:r0t + cc + cw], st,
                                  q="sp")
                    continue
                dst = dest[0] if isinstance(dest, tuple) else dest
                for gc in range(0, ncols // 128, 8):
                    ng = min(8, ncols // 128 - gc)
                    if kind == "raw32":
                        stg = st32[cnt["s32"] % 2]
                        cnt["s32"] += 1
                    else:
                        stg = st16[cnt["s16"] % 2]
                        cnt["s16"] += 1
                    for ci in range(ng):
                        f0 = c0 + (gc + ci) * 128
                        ps = g.P[cnt["ps"] % 4][:, 0:TT]
                        cnt["ps"] += 1
                        for dc in range(8):
                            k.mm(ps, wb[dc][:, f0:f0 + 128], h_[:, dc, :], start=(dc == 0), stop=(dc == 7))
                        if kind == "raw32":
                            k.copy(stg[:, ci, :], ps, eng="dve")
                        elif kind == "silu16":
                            k.act(stg[:, ci, :], ps, AF.Silu)
                        elif kind == "copy16":
                            k.act(stg[:, ci, :], ps, AF.Copy, scale=dest_scale(kind, dest, g))
                        elif kind == "rope16":
                            q_ = qb[cnt["q"] % 2]
                            a_ = t1[cnt["q"] % 2]
                            b_ = t2[cnt["q"] % 2]
                            pp = g.P[4 + cnt["q"] % 2][:, 0:TT]
                            cnt["q"] += 1
                            k.act(q_, ps, AF.Copy)
                            k.mm(pp, perm, q_)
                            k.tt(a_, ps, ropec[:, t0:t0 + TT], ALU.mult)
                            k.tt(b_, pp, ropes[:, t0:t0 + TT], ALU.mult)
                            k.tt(stg[:, ci, :], a_, b_, ALU.add)
                    r0 = (c0 - spec_base(specs, dst)) + gc * 128
                    k.dma(dst[b][r0:r0 + ng * 128, t0:t0 + TT].re("(c p) t -> p c t", p=128),
                          stg[:, 0:ng, :], q="act" if cnt["dq"] % 2 else "sp")
                    cnt["dq"] += 1


def spec_base(specs, dst):
    for (c0, ncols, kind, dest) in specs:
        d = dest[0] if isinstance(dest, tuple) else dest
        if d is dst:
            return c0
    raise KeyError


def dest_scale(kind, dest, g):
    if isinstance(dest, tuple):
        return dest[1]
    return 1.0


def rev(v, lo, hi):
    a = v.ap[:, lo:hi]
    pat = [list(p) for p in a.ap]
    off = a.offset + (hi - lo - 1) * pat[-1][0]
    pat[-1][0] = -pat[-1][0]
    return V(AP(a.tensor, off, pat), v.units)


def dwconv(k, out, x, wcols, bcol, taps, left, segs, eng="dve"):
    k.ts(out, x, wcols[left], bcol, ALU.mult, ALU.add, eng=eng)
    for kk in range(taps):
        o = kk - left
        if o == 0:
            continue
        for (lo, hi) in segs:
            a = lo + max(0, -o)
            bnd = hi - max(0, o)
            k.stt(out[:, a:bnd], x[:, a + o:bnd + o], wcols[kk], out[:, a:bnd], ALU.mult, ALU.add, eng=eng)


def phase_lru(g, l, e):
    k = g.k
    k.phase()
    segs = [(0, CTX), (CTX, T)]
    lam = spcol(g, f"lru_lam{e}", 0, 16)
    cA = k.sbt("cA", [128, 16])
    cA2 = k.sbt("cA2", [128, 16])
    k.act(cA, lam, AF.Exp, scale=-1.0)
    k.act(cA, cA, AF.Ln, bias=1.0)
    k.ts(cA2, cA, -16.0, None, ALU.mult)
    k.ts(cA, cA, -8.0, None, ALU.mult)
    one_c = k.sbt("one_c", [128, 1])
    k.memset(one_c, 1.0)
    wa = k.sbt("wa", [128, 2, 8, 128], BF16)
    wx = k.sbt("wx", [128, 2, 8, 128], BF16)
    for d in range(2):
        k.dma(wa[:, d], g.lru_wa[e][d].re("n c d -> c n d"), q="pool")
        k.dma(wx[:, d], g.lru_wx[e][d].re("n c d -> c n d"), q="pool")
    xa = [k.sbt(f"xa{i}", [128, T]) for i in range(2)]
    ga = [k.sbt(f"ga{i}", [128, T], BF16) for i in range(2)]
    u = k.sbt("u", [128, T])
    ub = k.sbt("ub", [128, T], BF16)
    r_ = k.sbt("r", [128, T])
    i_ = k.sbt("i", [128, T])
    s_ = k.sbt("s", [128, T])
    hh = [k.sbt(f"h{d}", [128, T]) for d in range(2)]
    yo = [k.sbt(f"yo{i}", [128, T], BF16) for i in range(2)]
    pc = 0
    items = [(b_, n_) for b_ in range(NB) for n_ in range(8)]

    def lru_load(i):
        b_, n_ = items[i]
        k.dma(xa[i % 2], g.XA[b_][n_ * 128:(n_ + 1) * 128, :], q="sp")
        k.dma(ga[i % 2], g.GA[b_][n_ * 128:(n_ + 1) * 128, :], q="act")
    lru_load(0)
    for it, (b, n) in enumerate(items):
        if True:
            xa_ = xa[it % 2]
            ga_ = ga[it % 2]
            yo_ = yo[it % 2]
            if it + 1 < len(items):
                lru_load(it + 1)
            wc = [spcol(g, f"lru_cw{e}", kk * 8 + n) for kk in range(4)]
            dwconv(k, u, xa_, wc, spcol(g, f"lru_cb{e}", n), 4, 2, segs)
            k.copy(ub, u, eng="pool")
            for d in range(2):
                for c0 in range(0, T, 512):
                    cw = min(512, T - c0)
                    pr = g.P[pc % 8][:, 0:cw]
                    pi = g.P[(pc + 1) % 8][:, 0:cw]
                    pc += 2
                    k.mm(pr, wa[:, d, n, :], ub[:, c0:c0 + cw])
                    k.mm(pi, wx[:, d, n, :], ub[:, c0:c0 + cw])
                    k.act(r_[:, c0:c0 + cw], pr, AF.Sigmoid, bias=spcol(g, f"lru_ba{e}", d * 8 + n))
                    k.act(i_[:, c0:c0 + cw], pi, AF.Sigmoid, bias=spcol(g, f"lru_bx{e}", d * 8 + n))
                k.act(s_, r_, AF.Exp, scale=cA2[:, d * 8 + n:d * 8 + n + 1])
                k.act(r_, r_, AF.Exp, scale=cA[:, d * 8 + n:d * 8 + n + 1])
                k.ts(s_, s_, 1.0, None, ALU.min)
                k.act(s_, s_, AF.Sqrt, bias=one_c, scale=-1.0)
                k.tt(i_, i_, u, ALU.mult)
                k.tt(i_, i_, s_, ALU.mult)
                h_ = hh[d]
                if d == 0:
                    k.scan(h_, r_, i_, 0.0)
                else:
                    k.scan(rev(h_, 0, CTX), rev(r_, 0, CTX), rev(i_, 0, CTX), 0.0)
                    k.scan(rev(h_, CTX, T), rev(r_, CTX, T), rev(i_, CTX, T), h_[:, 0:1])
            k.tt(hh[0], hh[0], hh[1], ALU.add, eng="pool")
            k.tt(yo_, hh[0], ga_, ALU.mult, eng="pool")
            k.dma(g.MIXT[b][n * 128:(n + 1) * 128, :], yo_, q="pool")


def phase_attn(g, l, e):
    k = g.k
    k.phase()
    scale = 128 ** -0.5
    mprev_f = k.sbt("mprev_f", [128, 512])
    mnext_f = k.sbt("mnext_f", [128, 512])
    k.dma(mprev_f, g.cst["mprev"], q="act")
    k.dma(mnext_f, g.cst["mnext"], q="act")
    mprev = k.sbt("mprev", [128, 512], BF16)
    mnext = k.sbt("mnext", [128, 512], BF16)
    k.copy(mprev, mprev_f)
    k.copy(mnext, mnext_f)
    sinkexp = k.sbt("sinkexp", [128, 8])
    k.act(sinkexp, spcol(g, f"sink{e}", 0, 8), AF.Exp)
    NBLK = T // 128
    Kt = [k.sbt(f"Kt{i}", [128, 2, T], BF16) for i in range(2)]
    Vt = [k.sbt(f"Vt{i}", [128, NBLK, 256], BF16) for i in range(2)]
    Qt = [k.sbt(f"Qt{i}", [128, 1024], BF16) for i in range(2)]
    Gt = [k.sbt(f"Gt{i}", [128, 1024], BF16) for i in range(2)]
    pT = [k.sbt(f"pT{i}", [128, 512], BF16) for i in range(8)]
    den = [k.sbt(f"den{i}", [128, 512]) for i in range(2)]
    ost = [k.sbt(f"ost{i}", [128, 1024], BF16) for i in range(2)]
    otmp = [k.sbt(f"otmp{i}", [128, 512]) for i in range(2)]
    pcnt = 0
    ppc = 0
    items = [(b_, k_) for b_ in range(NB) for k_ in range(NBLK)]

    def attn_load(i):
        b_, k_ = items[i]
        if k_ == 0:
            k.dma(Kt[b_], g.K[b_].re("(h p) t -> p h t", p=128), q="sp")
            k.dma(Vt[b_], g.VT[b_].re("(n p) c -> p n c", p=128), q="act")
        cc = k_ * 128
        k.dma(Qt[i % 2].re("p (h t) -> p h t", h=8), g.Q[b_][:, cc:cc + 128].re("(h p) t -> p h t", p=128), q="sp")
        k.dma(Gt[i % 2].re("p (h t) -> p h t", h=8), g.GB[b_][:, cc:cc + 128].re("(h p) t -> p h t", p=128), q="act")
    attn_load(0)
    for it, (b, blk) in enumerate(items):
        if True:
            Kb = Kt[b]
            Vb = Vt[b]
            q_ = Qt[it % 2]
            g_ = Gt[it % 2]
            o_ = ost[it % 2]
            c0 = blk * 128
            if it + 1 < len(items):
                attn_load(it + 1)
            if blk < 2:
                kbs = [(0, None), (1, None)]
            else:
                kbs = []
                if blk > 2:
                    kbs.append((blk - 1, mprev))
                kbs.append((blk, None))
                if blk < NBLK - 1:
                    kbs.append((blk + 1, mnext))
                kbs += [(0, None), (1, None)]
            for hh in range(2):
                po = g.P[4 + ppc % 2]
                pd = g.P[6 + ppc % 2]
                ppc += 1
                rhs = q_[:, hh * 512:(hh + 1) * 512]
                pts = []
                for (kb, msk) in kbs:
                    pst = g.P[pcnt % 4]
                    p_ = pT[pcnt % 8]
                    pcnt += 1
                    k.mm(pst, Kb[:, hh, kb * 128:(kb + 1) * 128], rhs)
                    k.act(p_, pst, AF.Exp, scale=scale)
                    if msk is not None:
                        k.tt(p_, p_, msk, ALU.mult, eng="pool")
                    pts.append((kb, p_))
                for ii, (kb, p_) in enumerate(pts):
                    first = (ii == 0)
                    lastk = (ii == len(pts) - 1)
                    k.mm(po, Vb[:, kb, hh * 128:(hh + 1) * 128], p_, start=first, stop=lastk)
                for ii, (kb, p_) in enumerate(pts):
                    first = (ii == 0)
                    lastk = (ii == len(pts) - 1)
                    k.mm(pd, g.ones_b, p_, start=first, stop=lastk)
                dn = den[ppc % 2]
                ot = otmp[ppc % 2]
                for gg in range(4):
                    k.ts(dn[:, gg * 128:(gg + 1) * 128], pd[:, gg * 128:(gg + 1) * 128],
                         sinkexp[:, hh * 4 + gg:hh * 4 + gg + 1], None, ALU.add)
                k.recip(dn, dn)
                k.tt(ot, po, dn, ALU.mult)
                k.tt(o_[:, hh * 512:(hh + 1) * 512], ot, g_[:, hh * 512:(hh + 1) * 512], ALU.mult)
            k.dma(g.MIXT[b][1024:2048, c0:c0 + 128].re("(h p) t -> p h t", p=128), o_.re("p (h t) -> p h t", h=8), q="pool")


def phase_out(g, l, w_dram, last):
    k = g.k
    k.phase()
    wo = load_weight_bf16(g, "wo", w_dram, D, 16)
    xsrc = g.xin if l == 0 else g.XR
    epsc = k.sbt("epsc", [128, 1])
    k.memset(epsc, EPS)
    mix = [k.sbt(f"mix{i}", [128, 16, TT], BF16) for i in range(2)]
    xs = [k.sbt(f"xs{i}", [128, 8, TT]) for i in range(2)]
    xn = [k.sbt(f"xn{i}", [128, 8, TT]) for i in range(2)]
    ysb = k.sbt("ysb", [128, 8, TT])
    sq = k.sbt("sq", [128, 8, TT], BF16)
    rstd = k.sbt("rstd", [128, TT])
    tmp = k.sbt("tmp", [128, TT])
    pc = 0
    tiles = [(b_, t_) for b_ in range(NB) for t_ in range(NT)]

    def out_load(i):
        b_, t_ = tiles[i]
        k.dma(mix[i % 2], g.MIXT[b_][:, t_ * TT:(t_ + 1) * TT].re("(c p) t -> p c t", p=128), q="pool")
        k.dma(xs[i % 2], xsrc[b_][:, t_ * TT:(t_ + 1) * TT].re("(c p) t -> p c t", p=128), q="pool")
    out_load(0)
    for it, (b, tt) in enumerate(tiles):
        if True:
            t0 = tt * TT
            j = 2 if tt == 0 else b
            m_ = mix[it % 2]
            x_ = xs[it % 2]
            n_ = xn[it % 2]
            if it + 1 < len(tiles):
                out_load(it + 1)
            ms = g.P[7][:, 0:TT]
            for dmc in range(8):
                ps = g.P[pc % 6][:, 0:TT]
                pc += 1
                for fc in range(16):
                    k.mm(ps, wo[fc][:, dmc * 128:(dmc + 1) * 128], m_[:, fc, :], start=(fc == 0), stop=(fc == 15))
                k.act(sq[:, dmc, :], ps, AF.Square)
                k.copy(ysb[:, dmc, :], ps, eng="dve")
            for dmc in range(8):
                k.mm(ms, g.ones_b, sq[:, dmc, :], start=(dmc == 0), stop=(dmc == 7))
            k.act(rstd, ms, AF.Sqrt, bias=epsc, scale=1.0 / D)
            k.recip(rstd, rstd)
            for dmc in range(8):
                k.tt(tmp, ysb[:, dmc, :], rstd, ALU.mult)
                k.stt(n_[:, dmc, :], tmp, g.modG[:, l, dmc, j:j + 1], x_[:, dmc, :], ALU.mult, ALU.add)
            dst = g.yout if last else g.XR
            k.dma(dst[b][:, t0:t0 + TT].re("(c p) t -> p c t", p=128), n_, q="sp")


def host_inputs(inp, core, consts, sp):
    b0 = core * NB
    xin = np.empty((NB, D, T), np.float32)
    for j in range(NB):
        xin[j, :, :CTX] = inp["ctx"][b0 + j].T
        xin[j, :, CTX:] = inp["x"][b0 + j].T
    cols = np.stack([inp["c"][b0], inp["c"][b0 + 1], inp["c_ctx"]], axis=-1)
    cT = np.ascontiguousarray(cols.reshape(8, 128, 3).transpose(1, 0, 2))
    m = {"xin": xin, "cT": cT, "sp": sp,
         "mod_w": inp["mod_w"], "ev_w_in": inp["ev_w_in"], "ev_w_out": inp["ev_w_out"],
         "lru_wa": inp["lru_wa"], "lru_wx": inp["lru_wx"],
         "od_w_in": inp["od_w_in"], "od_w_out": inp["od_w_out"], "hy_w1": inp["hy_w1"], "hy_w2": inp["hy_w2"],
         "hy_w3": inp["hy_w3"], "hy_bias": inp["hy_bias"]}
    m.update(consts)
    return m


_NC_CACHE = {}


def kernel(**inputs):
    inp = {k_: np.asarray(v, np.float32) for k_, v in inputs.items()}
    n_cores = 8
    if "nc" not in _NC_CACHE:
        _NC_CACHE["nc"] = build_program()
    nc = _NC_CACHE["nc"]
    consts = host_consts()
    sp = host_sp(inp)
    in_maps = [host_inputs(inp, c, consts, sp) for c in range(n_cores)]
    res = run_bass_kernel_spmd(nc, in_maps, core_ids=list(range(n_cores)))
    out = np.empty((16, S, D), np.float32)
    for c in range(n_cores):
        y = res.results[c]["yout"]
        for j in range(NB):
            out[c * NB + j] = y[j][:, CTX:].T
    return out


def odd_specs1(g):
    return [(0, 3072, "raw32", g.Z), (3072, 1024, "silu16", g.GH)]


def odd_specs2(g):
    ks = 128 ** -0.5
    return [(0, 1024, "copy16", g.RQ), (1024, 1024, "copy16", (g.RK, ks)), (1024, 1024, "tok16", (g.RKT, ks)),
            (2048, 1024, "tok16", g.RVT), (3072, 1024, "silu16", g.GD)]


def host_consts_odd():
    c = {}
    jj = np.arange(128, dtype=np.float32)[:, None]
    ii = np.arange(128, dtype=np.float32)[None, :]
    retc = np.zeros((128, 6 * 128 + 2), np.float32)
    retc[:, 0:128] = np.maximum(ii - jj, 0)
    retc[:, 128:256] = (ii >= jj)
    retc[:, 256:384] = np.maximum(jj - ii, 0)
    retc[:, 384:512] = (jj > ii)
    retc[:, 512:640] = ii + 1.0
    retc[:, 640:768] = 128.0 - ii
    retc[:, 768] = 127.0 - jj[:, 0]
    retc[:, 769] = jj[:, 0]
    c["retc"] = retc
    return c


def phase_ret(g, l, o):
    k = g.k
    k.phase()
    NBLK = T // 128
    retc = k.sbt("retc", [128, 770])
    k.dma(retc, g.cst["retc"], q="act")
    lg = k.sbt("lg", [128, 16])
    k.act(lg, spcol(g, f"ret_logit{o}", 0, 16), AF.Sigmoid)
    k.act(lg, lg, AF.Ln)
    inner = k.sbt("inner", [128, 16, 128])
    qdec = k.sbt("qdec", [128, 16, 128])
    kdec = k.sbt("kdec", [128, 16])
    cdec = k.sbt("cdec", [128, 16])
    for d in range(2):
        for h in range(8):
            c = d * 8 + h
            lgc = lg[:, c:c + 1]
            k.act(inner[:, c, :], retc[:, d * 256:d * 256 + 128], AF.Exp, scale=lgc)
            k.tt(inner[:, c, :], inner[:, c, :], retc[:, d * 256 + 128:d * 256 + 256], ALU.mult)
            k.act(qdec[:, c, :], retc[:, 512 + d * 128:640 + d * 128], AF.Exp, scale=lgc)
            k.act(kdec[:, c:c + 1], retc[:, 768 + d:769 + d], AF.Exp, scale=lgc)
    k.act(cdec, lg, AF.Exp, scale=128.0)
    epsc = k.sbt("epsc", [128, 1])
    k.memset(epsc, EPS)
    HP = 2
    qT = [k.sbt(f"qT{i}", [128, T], BF16) for i in range(HP)]
    kT = [k.sbt(f"kT{i}", [128, T], BF16) for i in range(HP)]
    ktok = [k.sbt(f"ktok{i}", [128, NBLK, 128], BF16) for i in range(HP)]
    vtok = [k.sbt(f"vtok{i}", [128, NBLK, 128], BF16) for i in range(HP)]
    gd = [k.sbt(f"gd{i}", [128, T], BF16) for i in range(HP)]
    oacc = [[k.sbt(f"oacc{i}_{d}", [128, T]) for d in range(2)] for i in range(HP)]
    Sf = [k.sbt(f"S{c}", [128, 128]) for c in range(4)]
    Sb = [k.sbt(f"Sb{c}", [128, 128], BF16) for c in range(4)]
    attS = [[k.sbt(f"attS{c}_{i}", [128, 128], BF16) for i in range(2)] for c in range(4)]
    qd = [[k.sbt(f"qd{c}_{i}", [128, 128], BF16) for i in range(2)] for c in range(4)]
    vdec = [[k.sbt(f"vdec{c}_{i}", [128, 128], BF16) for i in range(2)] for c in range(4)]
    sq = k.sbt("rsq", [128, 512], BF16)
    rstd = k.sbt("rrstd", [128, 512])
    yo = qT
    order = [list(range(NBLK)), [1, 0] + list(range(NBLK - 1, 1, -1))]
    for b in range(NB):
        for hp in range(8 // HP):
            for i in range(HP):
                h = hp * HP + i
                k.dma(qT[i], g.RQ[b][h * 128:(h + 1) * 128, :], q="sp")
                k.dma(kT[i], g.RK[b][h * 128:(h + 1) * 128, :], q="act")
                k.dma(ktok[i], g.RKT[b][:, h * 128:(h + 1) * 128].re("(n p) c -> p n c", p=128), q="sp")
                k.dma(vtok[i], g.RVT[b][:, h * 128:(h + 1) * 128].re("(n p) c -> p n c", p=128), q="act")
                k.dma(gd[i], g.GD[b][h * 128:(h + 1) * 128, :], q="sp")
            for s in range(NBLK):
                chains = [(i, d) for i in range(HP) for d in range(2)]

                def cvars(i, d):
                    h = hp * HP + i
                    ch = i * 2 + d
                    c = d * 8 + h
                    blk = order[d][s]
                    cs = slice(blk * 128, (blk + 1) * 128)
                    return h, ch, c, blk, cs
                for (i, d) in chains:
                    h, ch, c, blk, cs = cvars(i, d)
                    att = g.P[ch][:, 0:128]
                    a_ = attS[ch][s % 2]
                    k.mm(att, kT[i][:, cs], qT[i][:, cs])
                    k.tt(a_, att, inner[:, c, :], ALU.mult)
                    if s > 0:
                        k.tt(qd[ch][s % 2], qT[i][:, cs], qdec[:, c, :], ALU.mult, eng="pool")
                    if s < NBLK - 1:
                        k.ts(vdec[ch][s % 2], vtok[i][:, blk, :], kdec[:, c:c + 1], None, ALU.mult, eng="pool")
                if s < NBLK - 1:
                    for (i, d) in chains:
                        h, ch, c, blk, cs = cvars(i, d)
                        k.mm(g.P[4 + ch][:, 0:128], ktok[i][:, blk, :], vdec[ch][s % 2])
                for (i, d) in chains:
                    h, ch, c, blk, cs = cvars(i, d)
                    ops = g.P[ch][:, 128:256]
                    kv = g.P[4 + ch][:, 0:128]
                    k.mm(ops, vtok[i][:, blk, :], attS[ch][s % 2], start=True, stop=(s == 0))
                    if s > 0:
                        k.mm(ops, Sb[ch], qd[ch][s % 2], start=False, stop=True)
                    k.copy(oacc[i][d][:, cs], ops, eng="act")
                    if s < NBLK - 1:
                        if s == 0:
                            k.copy(Sf[ch], kv)
                        else:
                            k.stt(Sf[ch], Sf[ch], cdec[:, c:c + 1], kv, ALU.mult, ALU.add)
                        k.copy(Sb[ch], Sf[ch], eng="act")
            for i in range(HP):
                h = hp * HP + i
                oa = oacc[i][0]
                k.tt(oa, oa, oacc[i][1], ALU.add)
                for c0 in range(0, T, 512):
                    cw = min(512, T - c0)
                    k.act(sq[:, 0:cw], oa[:, c0:c0 + cw], AF.Square)
                    ms = g.P[i][:, 0:cw]
                    k.mm(ms, g.ones_b, sq[:, 0:cw])
                    k.act(rstd[:, 0:cw], ms, AF.Sqrt, bias=epsc, scale=1.0 / 128)
                    k.recip(rstd[:, 0:cw], rstd[:, 0:cw])
                    k.tt(oa[:, c0:c0 + cw], oa[:, c0:c0 + cw], rstd[:, 0:cw], ALU.mult)
                k.tt(yo[i], oa, gd[i], ALU.mult, eng="pool")
                k.dma(g.MIXT[b][1024 + h * 128:1024 + (h + 1) * 128, :], yo[i], q="sp")


def host_consts_hy():
    c = {}
    for L in (4096, 256):
        t = np.linspace(0.0, 1.0, L, dtype=np.float32)[:, None]
        bands = 16
        w = (2.0 * np.float32(math.pi) * np.arange(L, dtype=np.float32)[:, None] / np.float32(L)).astype(np.float32)
        f = np.linspace(1e-4, bands - 1, bands, dtype=np.float32)[None]
        z = np.concatenate([t, np.cos(f * w), -np.sin(f * w)], axis=-1).astype(np.float32)
        c[f"zemb{L}"] = np.ascontiguousarray(z.T)
        c[f"tneg{L}"] = np.ascontiguousarray((-t[:, 0]).reshape(L // 128, 128).T)
        N = 2 * L
        kk = np.arange(L, dtype=np.int64)
        prod = (kk[:, None] * kk[None, :]) % N
        ang = prod.astype(np.float64) * (2.0 * math.pi / N)
        c[f"cm{L}"] = np.cos(ang).astype(np.float32).astype(BF)
        c[f"sm{L}"] = np.sin(ang).astype(np.float32).astype(BF)
    max_decay = math.log(1e-2) / 0.3
    min_decay = math.log(1e-2) / 1.5
    deltas = np.linspace(min_decay, max_decay, 1024, dtype=np.float32)
    c["dabs"] = np.broadcast_to(np.abs(deltas)[None, :], (128, 1024)).astype(np.float32).copy()
    alt = np.where(np.arange(128) % 2 == 0, 1.0, -1.0).astype(np.float32)
    c["altc"] = alt[:, None].copy()
    c["altrow"] = np.where(np.arange(256) % 2 == 0, 1.0, -1.0).astype(np.float32)[None, :].copy()
    return c


def sin_reduce(k, out, arg, tmp_i, tmp_f):
    k.ts(arg, arg, 1.0 / (2 * math.pi), 64.5, ALU.mult, ALU.add)
    k.copy(tmp_i, arg)
    k.copy(tmp_f, tmp_i)
    k.tt(arg, arg, tmp_f, ALU.subtract)
    k.stt(arg, arg, 0.0, arg, ALU.is_lt, ALU.add)
    k.ts(arg, arg, 2 * math.pi, -math.pi, ALU.mult, ALU.add)
    k.ts(arg, arg, -math.pi, math.pi, ALU.max, ALU.min)
    k.act(out, arg, AF.Sin)


def phase_hyfilt(g, o, L):
    k = g.k
    k.phase()
    zemb = k.sbt("zemb", [33, L])
    k.dma(zemb, g.cst[f"zemb{L}"], q="sp")
    w1 = k.sbt("w1", [33, 64])
    k.dma(w1, g.hy_w1[o], q="act")
    w2 = k.sbt("w2", [64, 64])
    k.dma(w2, g.hy_w2[o], q="act")
    w3 = k.sbt("w3", [64, 4096])
    k.dma(w3, g.hy_w3[o], q="sp")
    tneg = k.sbt("tneg", [128, L // 128])
    k.dma(tneg, g.cst[f"tneg{L}"], q="act")
    dabs = k.sbt("dabs", [128, 1024])
    k.dma(dabs, g.cst["dabs"], q="act")
    bias = k.sbt("hbias", [1, 2048])
    k.dma(bias, g.hy_bias[o:o + 1].re("a b c -> a (b c)"), q="act")
    hid1 = k.sbt("hid1", [64, L])
    hid2 = k.sbt("hid2", [64, L])
    arg = k.sbt("arg", [64, 512])
    ti = k.sbt("ti", [64, 512], I32)
    tf = k.sbt("tf", [64, 512])
    b1 = spcol(g, f"hy_b1{o}")[0:64]
    b2 = spcol(g, f"hy_b2{o}")[0:64]
    fr = spcol(g, f"hy_freq{o}")[0:64]
    for (src, wgt, bcol, dst) in ((zemb, w1, b1, hid1), (hid1, w2, b2, hid2)):
        for c0 in range(0, L, 512):
            cw = min(512, L - c0)
            ps = g.P[0][0:64, 0:cw]
            k.mm(ps, wgt, src[:, c0:c0 + cw])
            k.ts(arg[:, 0:cw], ps, bcol, fr, ALU.add, ALU.mult)
            sin_reduce(k, dst[:, c0:c0 + cw], arg[:, 0:cw], ti[:, 0:cw], tf[:, 0:cw])
    dec = k.sbt("dec", [128, 1024])
    hf = k.sbt("hf", [128, 512])
    hb = k.sbt("hb", [128, 512])
    hs = [k.sbt(f"hs{i}", [128, 512], BF16) for i in range(2)]
    hd = [k.sbt(f"hd{i}", [128, 512], BF16) for i in range(2)]
    it = 0
    HS, HD = g.HS[L], g.HD[L]
    for tb in range(L // 128):
        k.act(dec, dabs, AF.Exp, scale=tneg[:, tb:tb + 1])
        for o2 in range(2):
            for ch in range(2):
                pf = g.P[1 + it % 2]
                pb = g.P[3 + it % 2]
                cf = (o2 * 2 + 0) * 1024 + ch * 512
                cb = (o2 * 2 + 1) * 1024 + ch * 512
                k.mm(pf, hid2[:, tb * 128:(tb + 1) * 128], w3[:, cf:cf + 512])
                k.mm(pb, hid2[:, tb * 128:(tb + 1) * 128], w3[:, cb:cb + 512])
                k.tt(hf, pf, dec[:, ch * 512:(ch + 1) * 512], ALU.mult)
                k.tt(hb, pb, dec[:, ch * 512:(ch + 1) * 512], ALU.mult)
                if tb == 0:
                    k.memset(hb[0:1, :], 0.0)
                    bo = o2 * 1024 + ch * 512
                    k.tt(hf[0:1, :], hf[0:1, :], bias[:, bo:bo + 512], ALU.add)
                s_ = hs[it % 2]
                d_ = hd[it % 2]
                it += 1
                k.tt(s_, hf, hb, ALU.add)
                k.tt(d_, hf, hb, ALU.subtract, eng="pool")
                k.dma(HS[o2][tb * 128:(tb + 1) * 128, ch * 512:(ch + 1) * 512], s_, q="sp")
                k.dma(HD[o2][tb * 128:(tb + 1) * 128, ch * 512:(ch + 1) * 512], d_, q="act")


def hy_tables(g, L):
    k = g.k
    nch = L // 128
    altf = k.sbt("altf", [128, 1])
    k.dma(altf, g.cst["altc"], q="act")
    altc = k.sbt("altc", [128, 1], BF16)
    k.copy(altc, altf)
    arf = k.sbt("arf", [1, 256])
    k.dma(arf, g.cst["altrow"], q="act")
    altrow = k.sbt("altrow", [1, 256], BF16)
    k.copy(altrow, arf)
    Ct = [k.sbt(f"Ct{i}", [128, nch, 128], BF16) for i in range(2)]
    St = [k.sbt(f"St{i}", [128, nch, 128], BF16) for i in range(2)]
    return altc, altrow, Ct, St


def load_ft(g, L, kc, Ct, St, it):
    k = g.k
    c_ = Ct[it % 2]
    s_ = St[it % 2]
    k.dma(c_, g.cst[f"cm{L}"][:, kc * 128:(kc + 1) * 128].re("(tc p) k -> p tc k", p=128), q="sp")
    k.dma(s_, g.cst[f"sm{L}"][:, kc * 128:(kc + 1) * 128].re("(tc p) k -> p tc k", p=128), q="act")
    return c_, s_


def phase_hyspec(g, o, L):
    k = g.k
    k.phase()
    nch = L // 128
    N = 2 * L
    altc, altrow, Ct, St = hy_tables(g, L)
    hs = k.sbt("hs_sb", [128, nch, 512], BF16)
    hd = k.sbt("hd_sb", [128, nch, 512], BF16)
    kr = [k.sbt(f"kr{i}", [128, 512]) for i in range(2)]
    ki = [k.sbt(f"ki{i}", [128, 512]) for i in range(2)]
    kn = k.sbt("kn", [1, 512])
    it = 0
    for o2 in range(2):
        for ch in range(2):
            k.dma(hs, g.HS[L][o2][:, ch * 512:(ch + 1) * 512].re("(tc p) c -> p tc c", p=128), q="sp")
            k.dma(hd, g.HD[L][o2][:, ch * 512:(ch + 1) * 512].re("(tc p) c -> p tc c", p=128), q="act")
            pn = g.P[4][0:1, :]
            for tc in range(nch):
                k.mm(pn, altc, hs[:, tc, :], start=(tc == 0), stop=(tc == nch - 1))
            k.act(kn, pn, AF.Copy, scale=1.0 / N)
            k.dma(g.KN[L][o2][:, ch * 512:(ch + 1) * 512], kn, q="sp")
            for kc in range(nch):
                c_, s_ = load_ft(g, L, kc, Ct, St, it)
                pa = g.P[it % 2]
                pb = g.P[2 + it % 2]
                r_ = kr[it % 2]
                i_ = ki[it % 2]
                it += 1
                for tc in range(nch):
                    k.mm(pa, c_[:, tc, :], hs[:, tc, :], start=(tc == 0), stop=(tc == nch - 1))
                for tc in range(nch):
                    k.mm(pb, s_[:, tc, :], hd[:, tc, :], start=(tc == 0), stop=(tc == nch - 1))
                k.act(r_, pa, AF.Copy, scale=2.0 / N)
                k.act(i_, pb, AF.Copy, scale=2.0 / N)
                if kc == 0:
                    k.ts(r_[0:1, :], r_[0:1, :], 0.5, None, ALU.mult)
                k.dma(g.KR[L][o2][kc * 128:(kc + 1) * 128, ch * 512:(ch + 1) * 512], r_, q="sp")
                k.dma(g.KI[L][o2][kc * 128:(kc + 1) * 128, ch * 512:(ch + 1) * 512], i_, q="act")


def phase_hyprep(g, o):
    k = g.k
    k.phase()
    NBLK = T // 128
    segs = [(0, CTX), (CTX, T)]
    z = [k.sbt(f"z{i}", [128, T]) for i in range(2)]
    zc = [k.sbt(f"zc{i}", [128, T]) for i in range(2)]
    zb = k.sbt("zb", [128, T], BF16)
    tok = [k.sbt(f"tok{i}", [128, NBLK, 128], BF16) for i in range(2)]
    it = 0
    pc = 0
    for b in range(NB):
        for chk in range(24):
            z_ = z[it % 2]
            zc_ = zc[it % 2]
            t_ = tok[it % 2]
            it += 1
            k.dma(z_, g.Z[b][chk * 128:(chk + 1) * 128, :], q="sp")
            wc = [spcol(g, f"hy_cw{o}", kk * 24 + chk) for kk in range(3)]
            dwconv(k, zc_, z_, wc, spcol(g, f"hy_cb{o}", chk), 3, 1, segs)
            if chk < 8:
                k.copy(zb, zc_, eng="act")
                for blk in range(NBLK):
                    pt = g.P[pc % 4].bc(BF16)[:, 0:128]
                    pc += 1
                    k.tr(pt, zb[:, blk * 128:(blk + 1) * 128], g.ident_b)
                    k.copy(t_[:, blk, :], pt, eng="act" if blk % 2 else "dve")
                k.dma(g.U1T[b][:, chk * 128:(chk + 1) * 128].re("(n p) c -> p n c", p=128), t_, q="act")
            else:
                k.dma(g.ZC[b][(chk - 8) * 128:(chk - 7) * 128, :], zc_, q="act")


def phase_hyconv(g, l, o, b, seg, o2):
    k = g.k
    k.phase()
    L = CTX if seg == 0 else S
    t_off = 0 if seg == 0 else CTX
    nch = L // 128
    altc, altrow, Ct, St = hy_tables(g, L)
    usrc = g.U1T if o2 == 0 else g.U2T
    u = k.sbt("u_sb", [128, nch, 512], BF16)
    Yr = k.sbt("Yr", [128, nch, 512], BF16)
    Yi = k.sbt("Yi", [128, nch, 512], BF16)
    ynq = k.sbt("ynq", [1, 512], BF16)
    knq = k.sbt("knq", [1, 512])
    kr = [k.sbt(f"kr{i}", [128, 512]) for i in range(2)]
    ki = [k.sbt(f"ki{i}", [128, 512]) for i in range(2)]
    t1 = k.sbt("t1", [128, 512])
    t2 = k.sbt("t2", [128, 512])
    Cn = [k.sbt(f"Cn{i}", [128, nch, 256], BF16) for i in range(1)]
    Sn = [k.sbt(f"Sn{i}", [128, nch, 256], BF16) for i in range(1)]
    xm = [k.sbt(f"xm{i}", [128, 256]) for i in range(2)]
    gh = [k.sbt(f"gh{i}", [128, 256], BF16) for i in range(2)]
    yb = [k.sbt(f"yb{i}", [128, 256], BF16) for i in range(2)]
    ytok = [k.sbt(f"ytok{i}", [128, 2, 128], BF16) for i in range(2)]
    it = 0
    it2 = 0
    it3 = 0
    for ch in range(2):
        k.dma(u, usrc[b][t_off:t_off + L, ch * 512:(ch + 1) * 512].re("(tc p) c -> p tc c", p=128), q="sp")
        k.dma(knq, g.KN[L][o2][:, ch * 512:(ch + 1) * 512], q="act")
        pn = g.P[6][0:1, :]
        for tc in range(nch):
            k.mm(pn, altc, u[:, tc, :], start=(tc == 0), stop=(tc == nch - 1))
        k.tt(ynq, pn, knq, ALU.mult)
        for kc in range(nch):
            c_, s_ = load_ft(g, L, kc, Ct, St, it)
            pa = g.P[it % 2]
            pb = g.P[2 + it % 2]
            r_ = kr[it % 2]
            i_ = ki[it % 2]
            it += 1
            k.dma(r_, g.KR[L][o2][kc * 128:(kc + 1) * 128, ch * 512:(ch + 1) * 512], q="sp")
            k.dma(i_, g.KI[L][o2][kc * 128:(kc + 1) * 128, ch * 512:(ch + 1) * 512], q="act")
            for tc in range(nch):
                k.mm(pa, c_[:, tc, :], u[:, tc, :], start=(tc == 0), stop=(tc == nch - 1))
            for tc in range(nch):
                k.mm(pb, s_[:, tc, :], u[:, tc, :], start=(tc == 0), stop=(tc == nch - 1))
            k.tt(t1, pa, r_, ALU.mult)
            k.tt(t2, pb, i_, ALU.mult)
            k.tt(Yr[:, kc, :], t1, t2, ALU.subtract)
            k.tt(t1, pa, i_, ALU.mult)
            k.tt(t2, pb, r_, ALU.mult)
            k.tt(Yi[:, kc, :], t1, t2, ALU.add)
        for nb in range(L // 256):
            cn = Cn[0]
            sn = Sn[0]
            it2 += 1
            k.dma(cn, g.cst[f"cm{L}"][:, nb * 256:(nb + 1) * 256].re("(kc p) n -> p kc n", p=128), q="sp")
            k.dma(sn, g.cst[f"sm{L}"][:, nb * 256:(nb + 1) * 256].re("(kc p) n -> p kc n", p=128), q="act")
            n0 = t_off + nb * 256
            for cs in range(4):
                crow = ch * 512 + cs * 128
                x_ = xm[it3 % 2]
                g_ = gh[it3 % 2]
                y_ = yb[it3 % 2]
                yt = ytok[it3 % 2]
                ps = g.P[4 + it3 % 2][:, 0:256]
                it3 += 1
                k.dma(x_, g.ZC[b][o2 * 1024 + crow:o2 * 1024 + crow + 128, n0:n0 + 256], q="sp")
                for kc in range(nch):
                    k.mm(ps, Yr[:, kc, cs * 128:(cs + 1) * 128], cn[:, kc, :], start=(kc == 0), stop=False)
                    k.mm(ps, Yi[:, kc, cs * 128:(cs + 1) * 128], sn[:, kc, :], start=False, stop=False)
                k.mm(ps, ynq[:, cs * 128:(cs + 1) * 128], altrow, start=False, stop=True)
                if o2 == 0:
                    k.tt(y_, ps, x_, ALU.mult)
                    for sb in range(2):
                        pt = g.P[7].bc(BF16)[:, sb * 128:(sb + 1) * 128]
                        k.tr(pt, y_[:, sb * 128:(sb + 1) * 128], g.ident_b)
                        k.copy(yt[:, sb, :], pt, eng="act")
                    k.dma(g.U2T[b][n0:n0 + 256, crow:crow + 128].re("(s p) c -> p s c", p=128), yt, q="act")
                else:
                    k.dma(g_, g.GH[b][crow:crow + 128, n0:n0 + 256], q="act")
                    k.tt(x_, ps, x_, ALU.mult)
                    k.tt(y_, x_, g_, ALU.mult, eng="pool")
                    k.dma(g.MIXT[b][crow:crow + 128, n0:n0 + 256], y_, q="act")


def host_consts_fft():
    c = {}
    N = 8192
    tc = np.arange(32)[:, None]
    k1 = np.arange(64)[None, :]
    a = 2 * np.pi * (tc * k1 % 64) / 64.0
    c["f1tab"] = np.concatenate([np.cos(a), np.sin(a)], axis=1).astype(np.float32).astype(BF)
    p = np.arange(128)[:, None, None]
    kk = (np.arange(64)[None, :, None] + 64 * np.arange(64)[None, None, :])
    ang = 2 * np.pi * ((kk * p) % N) / N
    c["t2tab"] = np.concatenate([np.cos(ang), np.sin(ang), -np.sin(ang), np.cos(ang)], axis=2).astype(np.float32).astype(BF)
    k2 = np.arange(64)[:, None, None]
    k1b = np.arange(64)[None, :, None]
    n2 = np.arange(128)[None, None, :]
    ang3 = 2 * np.pi * (((k1b + 64 * k2) * n2) % N) / N
    top = np.stack([np.cos(ang3), np.sin(ang3), -np.cos(ang3)], axis=2)
    bot = np.stack([np.sin(ang3), -np.cos(ang3), -np.sin(ang3)], axis=2)
    c["t3tab"] = np.concatenate([top, bot], axis=0).astype(np.float32).astype(BF)
    j = np.arange(64)[:, None]
    n1 = np.arange(32)[None, :]
    ag = 2 * np.pi * ((j * n1) % 64) / 64.0
    c["gtab"] = np.concatenate([np.cos(ag), -np.sin(ag)], axis=0).astype(np.float32).astype(BF)
    c["pm1"] = np.concatenate([np.ones(32), -np.ones(32)])[None, :].astype(np.float32)
    return c


def fft_f1(g, src, ch, Bd, f1tab, ubuf, bst, cnt):
    k = g.k
    for pg in range(4):
        u = ubuf[cnt["u"] % 2]
        cnt["u"] += 1
        k.dma(u, src[:, ch * 512:(ch + 1) * 512].re("(tc p) c -> tc p c", p=128)[:, pg * 32:(pg + 1) * 32, :], q="sp")
        for pq in range(4):
            st = bst[cnt["b"] % 2]
            cnt["b"] += 1
            for i in range(8):
                ps_ = pq * 8 + i
                pp = g.P[cnt["p"] % 8]
                cnt["p"] += 1
                k.mm(pp, f1tab, u[:, ps_, :])
                if i % 2:
                    k.act(st[:, i, :], pp, AF.Copy)
                else:
                    k.copy(st[:, i, :], pp, eng="dve")
            p0 = pg * 32 + pq * 8
            k.dma(Bd[:, p0:p0 + 8, :], st, q="act")


def phase_fft_f1(g, srcs):
    k = g.k
    k.phase()
    f1tab = k.sbt("f1tab", [32, 128], BF16)
    k.dma(f1tab, g.cst["f1tab"], q="act")
    ubuf = [k.sbt(f"fu{i}", [32, 32, 512], BF16) for i in range(2)]
    bst = [k.sbt(f"fb{i}", [128, 8, 512], BF16) for i in range(2)]
    cnt = {"u": 0, "b": 0, "p": 0}
    for (src, ch, Bd) in srcs:
        fft_f1(g, src, ch, Bd, f1tab, ubuf, bst, cnt)


def phase_hyspec2(g, o):
    k = g.k
    N = 8192
    for o2 in range(2):
        phase_fft_f1(g, [(g.HS[S][o2], 0, g.Bd[0]), (g.HS[S][o2], 1, g.Bd[1]),
                         (g.HD[S][o2], 0, g.Bd[2]), (g.HD[S][o2], 1, g.Bd[3])])
        k.phase()
        t2 = k.sbt("t2", [128, 64, 256], BF16)
        k.dma(t2, g.cst["t2tab"], q="sp")
        altf = k.sbt("altf", [128, 1])
        k.dma(altf, g.cst["altc"], q="act")
        altc = k.sbt("altc", [128, 1], BF16)
        k.copy(altc, altf)
        br = [[k.sbt(f"br{s}_{i}", [128, 8, 512], BF16) for i in range(2)] for s in range(2)]
        bs = [[k.sbt(f"bs{s}_{i}", [128, 8, 512], BF16) for i in range(2)] for s in range(2)]
        kr = [k.sbt(f"kr{i}", [64, 512]) for i in range(2)]
        ks = [k.sbt(f"ks{i}", [64, 512]) for i in range(2)]
        kn = k.sbt("kn", [1, 512])
        it = 0
        groups = [(c_, kg_) for c_ in range(2) for kg_ in range(8)]

        def sp_load(i):
            c_, kg_ = groups[i]
            for s_, Bd in ((0, g.Bd[c_]), (1, g.Bd[2 + c_])):
                k.dma(br[s_][i % 2], Bd[kg_ * 8:(kg_ + 1) * 8].re("k p c -> p k c"), q="sp")
                k.dma(bs[s_][i % 2], Bd[64 + kg_ * 8:64 + (kg_ + 1) * 8].re("k p c -> p k c"), q="sp")
        sp_load(0)
        for gi, (ch, kg) in enumerate(groups):
            bb = gi % 2
            if gi + 1 < len(groups):
                sp_load(gi + 1)
            for kk in range(8):
                k1 = kg * 8 + kk
                pr = g.P[it % 2][0:64, :]
                pi = g.P[2 + it % 2][0:64, :]
                r_ = kr[it % 2]
                s2 = ks[it % 2]
                it += 1
                k.mm(pr, t2[:, k1, 0:64], br[0][bb][:, kk, :], start=True, stop=False)
                k.mm(pr, t2[:, k1, 128:192], bs[0][bb][:, kk, :], start=False, stop=True)
                k.mm(pi, t2[:, k1, 64:128], br[1][bb][:, kk, :], start=True, stop=False)
                k.mm(pi, t2[:, k1, 0:64], bs[1][bb][:, kk, :], start=False, stop=True)
                k.act(r_, pr, AF.Copy, scale=2.0 / N)
                k.ts(s2, pi, 2.0 / N, None, ALU.mult)
                if k1 == 0:
                    k.ts(r_[0:1, :], r_[0:1, :], 0.5, None, ALU.mult)
                    pn = g.P[4][0:1, :]
                    k.mm(pn, altc, br[0][bb][:, 0, :])
                    k.act(kn, pn, AF.Copy, scale=1.0 / N)
                    k.dma(g.KN2[o2][:, ch * 512:(ch + 1) * 512], kn, q="act")
                k.dma(g.KR2[o2][k1][:, ch * 512:(ch + 1) * 512], r_, q="act")
                k.dma(g.KS2[o2][k1][:, ch * 512:(ch + 1) * 512], s2, q="act")


def phase_hyconv2(g, l, o, b, o2):
    k = g.k
    usrc = g.U1T if o2 == 0 else g.U2T
    uv = usrc[b][CTX:CTX + S, :]
    phase_fft_f1(g, [(uv, 0, g.Bd[0]), (uv, 1, g.Bd[1])])
    k.phase()
    t2 = k.sbt("t2", [128, 64, 256], BF16)
    k.dma(t2, g.cst["t2tab"], q="sp")
    t3 = k.sbt("t3", [128, 64, 3, 128], BF16)
    k.dma(t3, g.cst["t3tab"], q="act")
    altf = k.sbt("altf", [128, 1])
    k.dma(altf, g.cst["altc"], q="act")
    altc = k.sbt("altc", [128, 1], BF16)
    k.copy(altc, altf)
    br = [k.sbt(f"br{i}", [128, 4, 512], BF16) for i in range(2)]
    bs = [k.sbt(f"bs{i}", [128, 4, 512], BF16) for i in range(2)]
    kr = [k.sbt(f"kr{i}", [128, 4, 512]) for i in range(2)]
    ks = [k.sbt(f"ks{i}", [128, 4, 512]) for i in range(2)]
    knq = [k.sbt(f"knq{i}", [1, 512]) for i in range(2)]
    p1 = [k.sbt(f"p1_{i}", [128, 512], BF16) for i in range(2)]
    p2 = [k.sbt(f"p2_{i}", [128, 512], BF16) for i in range(2)]
    dst_ = [k.sbt(f"dst{i}", [128, 2, 512], BF16) for i in range(2)]
    it = 0
    groups = [(c_, kg_) for c_ in range(2) for kg_ in range(16)]

    def f2_load(i):
        c_, kg_ = groups[i]
        bb_ = i % 2
        if kg_ == 0:
            k.dma(knq[c_], g.KN2[o2][:, c_ * 512:(c_ + 1) * 512], q="sp")
        k.dma(br[bb_], g.Bd[c_][kg_ * 4:(kg_ + 1) * 4].re("k p c -> p k c"), q="sp")
        k.dma(bs[bb_], g.Bd[c_][64 + kg_ * 4:64 + (kg_ + 1) * 4].re("k p c -> p k c"), q="sp")
        krv = g.KR2[o2][kg_ * 4:(kg_ + 1) * 4][:, :, c_ * 512:(c_ + 1) * 512].re("k q c -> q k c")
        ksv = g.KS2[o2][kg_ * 4:(kg_ + 1) * 4][:, :, c_ * 512:(c_ + 1) * 512].re("k q c -> q k c")
        k.dma(kr[bb_][0:64], krv, q="sp")
        k.dma(kr[bb_][64:128], krv, q="sp")
        k.dma(ks[bb_][0:64], ksv, q="sp")
        k.dma(ks[bb_][64:128], ksv, q="sp")
    f2_load(0)
    for gi, (ch, kg) in enumerate(groups):
        Dd = g.Dd[ch]
        bb = gi % 2
        if gi + 1 < len(groups):
            f2_load(gi + 1)
        for kk in range(4):
            k1 = kg * 4 + kk
            px = g.P[it % 2]
            pdr = g.P[2 + it % 2]
            pdi = g.P[4 + it % 2]
            a_ = p1[it % 2]
            b_ = p2[it % 2]
            d_ = dst_[it % 2]
            it += 1
            k.mm(px, t2[:, k1, 0:128], br[bb][:, kk, :], start=True, stop=False)
            k.mm(px, t2[:, k1, 128:256], bs[bb][:, kk, :], start=False, stop=True)
            if k1 == 0:
                pn = g.P[6][0:1, :]
                k.mm(pn, altc, br[bb][:, 0, :])
                k.tt(g.ynq[:, ch, :], pn, knq[ch], ALU.mult)
            k.tt(a_, px, kr[bb][:, kk, :], ALU.mult)
            k.tt(b_, px, ks[bb][:, kk, :], ALU.mult)
            k.mm(pdr, t3[:, k1, 0, :], a_, start=True, stop=False)
            k.mm(pdr, t3[:, k1, 1, :], b_, start=False, stop=True)
            k.mm(pdi, t3[:, k1, 1, :], a_, start=True, stop=False)
            k.mm(pdi, t3[:, k1, 2, :], b_, start=False, stop=True)
            k.act(d_[:, 0, :], pdr, AF.Copy)
            k.copy(d_[:, 1, :], pdi, eng="dve")
            k.dma(Dd[k1], d_[:, 0, :], q="act")
            k.dma(Dd[64 + k1], d_[:, 1, :], q="act")
    k.phase()
    gtab = k.sbt("gtab", [128, 32], BF16)
    k.dma(gtab, g.cst["gtab"], q="act")
    pmf = k.sbt("pmf", [1, 64])
    k.dma(pmf, g.cst["pm1"], q="act")
    pm = k.sbt("pm", [1, 64], BF16)
    k.copy(pm, pmf)
    NBL = S // 128
    altpat = k.sbt("altpat", [128, S])
    k.memset(altpat, 1.0)
    k.memset(altpat.re("p (a two) -> p a two", two=2)[:, :, 1], -1.0)
    ycol = [k.sbt(f"ycol{i}", [128, 1]) for i in range(2)]
    dt_ = [k.sbt(f"dt{i}", [128, 16, 512], BF16) for i in range(2)]
    yT = [k.sbt(f"yT{i}", [128, 32, 128]) for i in range(2)]
    xT = [k.sbt(f"xT{i}", [128, S]) for i in range(2)]
    ob = [k.sbt(f"ob{i}", [128, S], BF16) for i in range(2)]
    ghT = [k.sbt(f"ghT{i}", [128, S], BF16) for i in range(2)]
    tok = [k.sbt(f"tok{i}", [128, NBL, 128], BF16) for i in range(2)] if o2 == 0 else None
    it = 0
    pc = 0
    for ch in range(2):
        Dd = g.Dd[ch]
        for cpair in range(2):
            for ci in range(2):
                crow = ch * 512 + (cpair * 2 + ci) * 128
                k.dma(xT[ci], g.ZC[b][o2 * 1024 + crow:o2 * 1024 + crow + 128, CTX:CTX + S], q="act")
                if o2 == 1:
                    k.dma(ghT[ci], g.GH[b][crow:crow + 128, CTX:CTX + S], q="act")
            for n2g in range(8):
                d_ = dt_[it % 2]
                it += 1
                k.dma(d_, Dd[:, n2g * 16:(n2g + 1) * 16, :], q="sp")
                for ci in range(2):
                    cs = cpair * 2 + ci
                    ps = g.P[pc % 6]
                    pc += 1
                    for j in range(16):
                        k.mm(ps[:, j * 32:(j + 1) * 32], d_[:, j, cs * 128:(cs + 1) * 128], gtab, start=True, stop=True)
                    k.copy(yT[ci][:, :, n2g * 16:(n2g + 1) * 16], ps.re("p (a b) -> p b a", a=16),
                           eng="act" if ci else "dve")
            for ci in range(2):
                crow = ch * 512 + (cpair * 2 + ci) * 128
                cs = cpair * 2 + ci
                yflat = yT[ci].re("p a b -> p (a b)")
                pcol = g.P[7][:, 0:1]
                k.mm(pcol, g.ynq[:, ch, cs * 128:(cs + 1) * 128], pm[:, 0:1])
                k.copy(ycol[ci], pcol, eng="dve")
                k.stt(yflat, altpat, ycol[ci], yflat, ALU.mult, ALU.add)
                if o2 == 0:
                    k.tt(ob[ci], yflat, xT[ci], ALU.mult)
                    t_ = tok[ci]
                    for blk in range(NBL):
                        pt = g.P[6 + blk % 2].bc(BF16)[:, 0:128]
                        k.tr(pt, ob[ci][:, blk * 128:(blk + 1) * 128], g.ident_b)
                        k.copy(t_[:, blk, :], pt, eng="act" if blk % 2 else "dve")
                    k.dma(g.U2T[b][CTX:CTX + S, crow:crow + 128].re("(n p) c -> p n c", p=128), t_, q="act")
                else:
                    k.tt(yflat, yflat, xT[ci], ALU.mult)
                    k.tt(ob[ci], yflat, ghT[ci], ALU.mult, eng="pool")
                    k.dma(g.MIXT[b][crow:crow + 128, CTX:CTX + S], ob[ci], q="act")
```

```python
import numpy as np
import concourse.bass as bass
import concourse.mybir as mybir

F32 = mybir.dt.float32
BF16 = mybir.dt.bfloat16
AF = mybir.ActivationFunctionType
ALU = mybir.AluOpType
AX = mybir.AxisListType

SEM_CHUNK = 20000
NDMA_SEMS = 12


class V:
    __slots__ = ("ap", "units")

    def __init__(self, ap, units):
        self.ap = ap
        self.units = tuple(units)

    def __getitem__(self, idx):
        return V(self.ap[idx], self.units)

    def re(self, pat, **kw):
        return V(self.ap.rearrange(pat, **kw), self.units)

    def bc(self, dt):
        return V(self.ap.bitcast(dt), self.units)


class Op:
    __slots__ = ("eng", "fn", "deps", "dma", "sig", "waits", "idx", "has_dep")

    def __init__(self, eng, fn, deps, dma):
        self.eng = eng
        self.fn = fn
        self.deps = deps
        self.dma = dma
        self.sig = None
        self.waits = None
        self.has_dep = False


class Builder:
    def __init__(self, nc):
        self.nc = nc
        self.ops = []
        self.units = {}
        self.last_dma = {"sp": [], "act": [], "pool": []}
        self.pending = {}
        self.psum_units = set()
        self.arena = None
        self.arena_off = 0
        self.arena_base = 0
        self.uid = 0

    def init_arena(self, nbytes):
        self.arena_bytes = nbytes
        self.arena = self.nc.alloc_sbuf_tensor("arena", [128, nbytes // 4], F32)

    def sbt(self, name, shape, dtype=F32, glob=False):
        esz = 2 if dtype == BF16 else 4
        n = 1
        for d in shape[1:]:
            n *= d
        nb = (n * esz + 31) // 32 * 32
        off = self.arena_off
        assert off + nb <= self.arena_bytes, (name, off, nb)
        self.arena_off += nb
        ap = self.arena[:, off // 4:(off + nb) // 4]
        if esz == 2:
            ap = ap.bitcast(BF16)
        elif dtype != F32:
            ap = ap.bitcast(dtype)
        ap = ap[:, 0:n]
        if len(shape) == 3:
            ap = ap.rearrange("p (a b) -> p a b", a=shape[1])
        elif len(shape) == 4:
            ap = ap.rearrange("p (a b c) -> p a b c", a=shape[1], b=shape[2])
        if shape[0] != 128:
            ap = ap[0:shape[0]]
        self.uid += 1
        return V(ap, (f"{name}#{self.uid}",))

    def phase(self):
        self.barrier()
        self.arena_off = self.arena_base

    def freeze_globals(self):
        self.arena_base = self.arena_off

    def barrier(self):
        deps = set()
        last = {}
        for i, op in enumerate(self.ops):
            if not op.dma:
                last[op.eng] = i
        deps.update(last.values())
        for q, lst in self.last_dma.items():
            deps.update(lst[-NDMA_SEMS:])
        for e in ("pe", "act", "dve", "pool", "sp"):
            self.pending[e] = set(deps)

    def sb(self, name, shape, dtype=F32, units=None):
        h = self.nc.alloc_sbuf_tensor(name, list(shape), dtype)
        return V(h[:], units if units is not None else (name,))

    def ps(self, name, shape, dtype=F32):
        h = self.nc.alloc_psum_tensor(name, list(shape), dtype)
        self.psum_units.add(name)
        return V(h[:], (name,))

    def dram(self, name, shape, dtype=F32, kind="Internal"):
        h = self.nc.dram_tensor(name, list(shape), dtype, kind=kind)
        return V(h.ap(), (name,))

    def add(self, eng, fn, r=(), w=(), dma=False):
        deps = set()
        ru = []
        wu = []
        for v in r:
            ru.extend(v.units if isinstance(v, V) else (v,))
        for v in w:
            wu.extend(v.units if isinstance(v, V) else (v,))
        pr = [u for u in ru if u in self.psum_units]
        if pr:
            ru = [u for u in ru if u not in self.psum_units]
            wu = wu + pr
        for u in ru:
            st = self.units.get(u)
            if st is not None and st[0] is not None:
                deps.add(st[0])
        for u in wu:
            st = self.units.get(u)
            if st is not None:
                if st[0] is not None:
                    deps.add(st[0])
                deps.update(st[1])
        opid = len(self.ops)
        pend = self.pending.pop(eng, None)
        if pend:
            deps.update(pend)
        if dma:
            q = self.last_dma[eng]
            if len(q) >= NDMA_SEMS:
                deps.add(q[-NDMA_SEMS])
            q.append(opid)
        if eng == "pe" and not dma:
            deps = {d for d in deps if not (self.ops[d].eng == "pe" and not self.ops[d].dma)}
        op = Op(eng, fn, deps, dma)
        self.ops.append(op)
        for d in deps:
            self.ops[d].has_dep = True
        for u in ru:
            st = self.units.get(u)
            if st is None:
                st = self.units[u] = [None, []]
            st[1].append(opid)
        for u in wu:
            self.units[u] = [opid, []]
        return opid

    def mm(self, out, lhsT, rhs, start=True, stop=True):
        self.add("pe", lambda e: e.matmul(out.ap, lhsT.ap, rhs.ap, start=start, stop=stop),
                 r=(lhsT, rhs), w=(out,))

    def tr(self, out, in_, ident):
        self.add("pe", lambda e: e.transpose(out.ap, in_.ap, ident.ap), r=(in_, ident), w=(out,))

    def act(self, out, in_, func, bias=None, scale=None, accum=None, eng="act"):
        kw = {}
        r = [in_]
        w = [out]
        if bias is not None:
            if isinstance(bias, V):
                kw["bias"] = bias.ap
                r.append(bias)
            else:
                kw["bias"] = bias
        if scale is not None:
            if isinstance(scale, V):
                kw["scale"] = scale.ap
                r.append(scale)
            else:
                kw["scale"] = scale
        if accum is not None:
            kw["accum_out"] = accum.ap
            w.append(accum)
        self.add("act", lambda e: e.activation(out.ap, in_.ap, func, **kw), r=r, w=w)

    def tt(self, out, a, b, op, eng="dve"):
        self.add(eng, lambda e: e.tensor_tensor(out.ap, a.ap, b.ap, op), r=(a, b), w=(out,))

    def ts(self, out, a, s1, s2, op0, op1=None, eng="dve", accum=None):
        r = [a]
        w = [out]
        a1 = s1
        a2 = s2
        if isinstance(s1, V):
            r.append(s1)
            a1 = s1.ap
        if isinstance(s2, V):
            r.append(s2)
            a2 = s2.ap
        kw = {}
        if a2 is None:
            a2 = 0.0
            op1 = ALU.add
        if op1 is not None:
            kw["op1"] = op1
        if accum is not None:
            kw["accum_out"] = accum.ap
            w.append(accum)
        self.add(eng, lambda e: e.tensor_scalar(out.ap, a.ap, a1, a2, op0, **kw), r=r, w=w)

    def stt(self, out, a, s, b, op0, op1, eng="dve"):
        r = [a, b]
        sa = s
        if isinstance(s, V):
            r.append(s)
            sa = s.ap
        self.add(eng, lambda e: e.scalar_tensor_tensor(out.ap, a.ap, sa, b.ap, op0, op1), r=r, w=(out,))

    def scan(self, out, d0, d1, init, op0=ALU.mult, op1=ALU.add):
        r = [d0, d1]
        ia = init
        if isinstance(init, V):
            r.append(init)
            ia = init.ap
        self.add("dve", lambda e: e.tensor_tensor_scan(out.ap, d0.ap, d1.ap, ia, op0, op1), r=r, w=(out,))

    def copy(self, out, in_, eng="dve"):
        if eng == "act":
            self.add("act", lambda e: e.copy(out.ap, in_.ap), r=(in_,), w=(out,))
        else:
            self.add(eng, lambda e: e.tensor_copy(out.ap, in_.ap), r=(in_,), w=(out,))

    def memset(self, out, val, eng="dve"):
        self.add(eng, lambda e: e.memset(out.ap, val), w=(out,))

    def recip(self, out, in_):
        self.add("dve", lambda e: e.reciprocal(out.ap, in_.ap), r=(in_,), w=(out,))

    def dma(self, out, in_, q="sp"):
        self.add(q, lambda e: e.dma_start(out=out.ap, in_=in_.ap), r=(in_,), w=(out,), dma=True)

    def emit(self, final_wait_units=()):
        nc = self.nc
        ops = self.ops
        engs = ("pe", "act", "dve", "pool", "sp")
        final_deps = set()
        for u in final_wait_units:
            st = self.units.get(u)
            if st is not None and st[0] is not None:
                final_deps.add(st[0])
        for d in final_deps:
            ops[d].has_dep = True
        n_sig = {e: 0 for e in engs}
        n_dma = {"sp": 0, "act": 0, "pool": 0}
        for op in ops:
            if op.dma:
                k = n_dma[op.eng]
                n_dma[op.eng] += 1
                op.sig = ("dma", op.eng, k % NDMA_SEMS, 16 * (k // NDMA_SEMS + 1))
            elif op.has_dep:
                k = n_sig[op.eng]
                n_sig[op.eng] += 1
                op.sig = ("cmp", op.eng, k // SEM_CHUNK, k % SEM_CHUNK + 1)
        sems = {}
        for e in engs:
            for c in range((n_sig[e] + SEM_CHUNK - 1) // SEM_CHUNK):
                sems[("cmp", e, c)] = nc.alloc_semaphore(name=f"s_{e}_{c}")
        for q, n in n_dma.items():
            for c in range(min(n, NDMA_SEMS)):
                sems[("dma", q, c)] = nc.alloc_semaphore(name=f"d_{q}_{c}")
        self.n_sems = len(sems)
        by_eng = {e: [] for e in engs}
        waited = {e: {} for e in engs}
        for op in ops:
            ws = {}
            for d in op.deps:
                s = ops[d].sig
                key = s[:3]
                if ws.get(key, 0) < s[3]:
                    ws[key] = s[3]
            wl = []
            wd = waited[op.eng]
            for key, val in ws.items():
                if wd.get(key, 0) < val:
                    wd[key] = val
                    wl.append((key, val))
            op.waits = wl
            by_eng[op.eng].append(op)
        fin = []
        fw = {}
        for d in final_deps:
            s = ops[d].sig
            if fw.get(s[:3], 0) < s[3]:
                fw[s[:3]] = s[3]
        fin = list(fw.items())

        def run(engine, name):
            for op in by_eng[name]:
                for key, val in op.waits:
                    engine.wait_ge(sems[key], val)
                ins = op.fn(engine)
                if op.sig is not None:
                    ins.then_inc(sems[op.sig[:3]], 16 if op.dma else 1)
            if name == "sp":
                for key, val in fin:
                    engine.wait_ge(sems[key], val)

        with nc.Block() as block:
            @block.tensor
            def _(e):
                run(e, "pe")

            @block.scalar
            def _(e):
                run(e, "act")

            @block.vector
            def _(e):
                run(e, "dve")

            @block.gpsimd
            def _(e):
                run(e, "pool")

            @block.sync
            def _(e):
                run(e, "sp")
        return {e: len(by_eng[e]) for e in engs}


import math
import ml_dtypes
from concourse.ap import AP
from concourse.bass_utils import run_bass_kernel_spmd

NB = 2
S = 4096
CTX = 256
T = S + CTX
D = 1024
TT = 256
NT = T // TT
DEPTH = 4
EPS = 1e-6
I32 = mybir.dt.int32
BF = ml_dtypes.bfloat16


def sp_layout():
    lay = {}
    off = 0

    def reg(name, n):
        nonlocal off
        lay[name] = (off, n)
        off += n
    reg("mod_b", 96)
    reg("norm_pre", 32)
    reg("norm_post", 32)
    for e in range(2):
        reg(f"lru_cw{e}", 32)
        reg(f"lru_cb{e}", 8)
        reg(f"lru_ba{e}", 16)
        reg(f"lru_bx{e}", 16)
        reg(f"lru_lam{e}", 16)
        reg(f"sink{e}", 8)
    for o in range(2):
        reg(f"hy_cw{o}", 72)
        reg(f"hy_cb{o}", 24)
        reg(f"hy_b1{o}", 1)
        reg(f"hy_freq{o}", 1)
        reg(f"hy_b2{o}", 1)
        reg(f"ret_logit{o}", 16)
    return lay, off


def chunked(v):
    v = np.asarray(v, np.float32)
    lead = v.shape[:-1]
    n = v.shape[-1] // 128
    a = v.reshape(lead + (n, 128))
    a = np.moveaxis(a, -1, 0)
    return a.reshape(128, -1)


def host_sp(inp):
    lay, n = sp_layout()
    sp = np.zeros((128, n), np.float32)

    def put(name, arr):
        o, c = lay[name]
        assert arr.shape == (128, c), (name, arr.shape, c)
        sp[:, o:o + c] = arr
    put("mod_b", chunked(inp["mod_b"].reshape(4, 3, 1024)))
    put("norm_pre", chunked(inp["norm_pre"]))
    put("norm_post", chunked(inp["norm_post"]))
    for e in range(2):
        put(f"lru_cw{e}", chunked(inp["lru_conv_w"][e]))
        put(f"lru_cb{e}", chunked(inp["lru_conv_b"][e]))
        put(f"lru_ba{e}", chunked(inp["lru_ba"][e]))
        put(f"lru_bx{e}", chunked(inp["lru_bx"][e]))
        put(f"lru_lam{e}", chunked(inp["lru_lambda"][e]))
        put(f"sink{e}", np.broadcast_to(inp["attn_sink"][e][None, :], (128, 8)))
    for o in range(2):
        put(f"hy_cw{o}", chunked(inp["hy_conv_w"][o]))
        put(f"hy_cb{o}", chunked(inp["hy_conv_b"][o]))
        for nm, key in (("hy_b1", "hy_b1"), ("hy_freq", "hy_freq"), ("hy_b2", "hy_b2")):
            col = np.zeros((128, 1), np.float32)
            col[:64, 0] = inp[key][o]
            put(f"{nm}{o}", col)
        put(f"ret_logit{o}", np.broadcast_to(inp["ret_decay_logit"][o].reshape(1, 16), (128, 16)))
    return sp


def host_consts():
    c = {}
    c["ident_f"] = np.eye(128, dtype=np.float32)
    half = 64
    inv = (10000.0 ** (-np.arange(0, half, 2, dtype=np.float32) / half)).astype(np.float32)
    tok = np.arange(S)
    row = (tok // 64).astype(np.float32)
    col = (tok % 64).astype(np.float32)
    ang_r = row[:, None] * inv[None]
    ang_c = col[:, None] * inv[None]
    cosT = np.ones((128, T), np.float32)
    sinT = np.zeros((128, T), np.float32)
    for d in range(128):
        ang = ang_r if d < 64 else ang_c
        j = d % 32
        first = (d % 64) < 32
        cosT[d, CTX:] = np.cos(ang[:, j])
        sinT[d, CTX:] = (-1.0 if first else 1.0) * np.sin(ang[:, j])
    c["ropec"] = cosT
    c["ropes"] = sinT
    pm = np.zeros((128, 128), np.float32)
    for d in range(128):
        partner = d + 32 if (d % 64) < 32 else d - 32
        pm[partner, d] = 1.0
    c["perm"] = pm
    jj = np.arange(128)[:, None]
    ii = np.arange(128)[None, :]
    mprev = (jj >= ii).astype(np.float32)
    mnext = (jj <= ii).astype(np.float32)
    c["mprev"] = np.tile(mprev[:, None, :], (1, 4, 1)).reshape(128, 512)
    c["mnext"] = np.tile(mnext[:, None, :], (1, 4, 1)).reshape(128, 512)
    c.update(host_consts_odd())
    c.update(host_consts_hy())
    c.update(host_consts_fft())
    return c


class Ctx:
    pass


def build_program(n_layers=DEPTH, debug=False, stop=99):
    nc = bass.Bass("TRN2", target_bir_lowering=False)
    k = Builder(nc)
    g = Ctx()
    g.k = k
    lay, nsp = sp_layout()
    g.lay = lay
    EI = "ExternalInput"
    g.xin = k.dram("xin", [NB, D, T], F32, kind=EI)
    g.cT = k.dram("cT", [128, 8, 3], F32, kind=EI)
    g.spd = k.dram("sp", [128, nsp], F32, kind=EI)
    g.mod_w = k.dram("mod_w", [4, D, 3 * D], F32, kind=EI)
    g.ev_w_in = k.dram("ev_w_in", [2, D, 4608], F32, kind=EI)
    g.ev_w_out = k.dram("ev_w_out", [2, 2048, D], F32, kind=EI)
    g.lru_wa = k.dram("lru_wa", [2, 2, 8, 128, 128], F32, kind=EI)
    g.lru_wx = k.dram("lru_wx", [2, 2, 8, 128, 128], F32, kind=EI)
    g.cst = {}
    for nm, shp in (("ident_f", [128, 128]), ("ropec", [128, T]), ("ropes", [128, T]), ("perm", [128, 128]),
                    ("mprev", [128, 512]), ("mnext", [128, 512])):
        g.cst[nm] = k.dram(nm, shp, F32, kind=EI)
    g.yout = k.dram("yout", [NB, D, T], F32, kind="ExternalOutput")
    g.XR = k.dram("XR", [NB, D, T], F32)
    g.XA = k.dram("XA", [NB, D, T], F32)
    g.GA = k.dram("GA", [NB, D, T], BF16)
    g.Q = k.dram("Q", [NB, D, T], BF16)
    g.K = k.dram("K", [NB, 256, T], BF16)
    g.VT = k.dram("VT", [NB, T, 256], BF16)
    g.GB = k.dram("GB", [NB, D, T], BF16)
    g.MIXT = k.dram("MIXT", [NB, 2048, T], BF16)
    g.od_w_in = k.dram("od_w_in", [2, D, 8192], F32, kind=EI)
    g.od_w_out = k.dram("od_w_out", [2, 2048, D], F32, kind=EI)
    g.hy_w1 = k.dram("hy_w1", [2, 33, 64], F32, kind=EI)
    g.hy_w2 = k.dram("hy_w2", [2, 64, 64], F32, kind=EI)
    g.hy_w3 = k.dram("hy_w3", [2, 64, 4096], F32, kind=EI)
    g.hy_bias = k.dram("hy_bias", [2, 2, 1024], F32, kind=EI)
    for nm, shp, dt_ in (("retc", [128, 770], F32), ("zemb4096", [33, 4096], F32), ("zemb256", [33, 256], F32),
                         ("tneg4096", [128, 32], F32), ("tneg256", [128, 2], F32), ("dabs", [128, 1024], F32),
                         ("altc", [128, 1], F32), ("altrow", [1, 256], F32),
                         ("cm4096", [4096, 4096], BF16), ("sm4096", [4096, 4096], BF16),
                         ("cm256", [256, 256], BF16), ("sm256", [256, 256], BF16)):
        g.cst[nm] = k.dram(nm, shp, dt_, kind=EI)
    for nm, shp, dt_ in (("f1tab", [32, 128], BF16), ("t2tab", [128, 64, 256], BF16), ("t3tab", [128, 64, 3, 128], BF16),
                         ("gtab", [128, 32], BF16), ("pm1", [1, 64], F32)):
        g.cst[nm] = k.dram(nm, shp, dt_, kind=EI)
    g.Bd = [k.dram(f"Bd{i}", [128, 128, 512], BF16) for i in range(4)]
    g.Dd = [k.dram(f"Dd{i}", [128, 128, 512], BF16) for i in range(2)]
    g.KR2 = k.dram("KR2", [2, 64, 64, D], BF16)
    g.KS2 = k.dram("KS2", [2, 64, 64, D], BF16)
    g.KN2 = k.dram("KN2", [2, 1, D], F32)
    g.Z = k.dram("Z", [NB, 3072, T], F32)
    g.ZC = k.dram("ZC", [NB, 2048, T], F32)
    g.GH = k.dram("GH", [NB, D, T], BF16)
    g.RQ = k.dram("RQ", [NB, D, T], BF16)
    g.RK = k.dram("RK", [NB, D, T], BF16)
    g.RKT = k.dram("RKT", [NB, T, D], BF16)
    g.RVT = k.dram("RVT", [NB, T, D], BF16)
    g.GD = k.dram("GD", [NB, D, T], BF16)
    g.U1T = k.dram("U1T", [NB, T, D], BF16)
    g.U2T = k.dram("U2T", [NB, T, D], BF16)
    g.HS = {L: k.dram(f"HS{L}", [2, L, D], BF16) for L in (S, CTX)}
    g.HD = {L: k.dram(f"HD{L}", [2, L, D], BF16) for L in (S, CTX)}
    g.KR = {L: k.dram(f"KR{L}", [2, L, D], F32) for L in (S, CTX)}
    g.KI = {L: k.dram(f"KI{L}", [2, L, D], F32) for L in (S, CTX)}
    g.KN = {L: k.dram(f"KN{L}", [2, 1, D], F32) for L in (S, CTX)}

    k.init_arena(200 * 1024)
    g.P = [k.ps(f"P{i}", [128, 512], F32) for i in range(8)]
    g.sp = k.sbt("sp", [128, nsp])
    k.dma(g.sp, g.spd)
    g.ident_f = k.sbt("ident_f", [128, 128])
    k.dma(g.ident_f, g.cst["ident_f"], q="act")
    g.ident_b = k.sbt("ident_b", [128, 128], BF16)
    k.copy(g.ident_b, g.ident_f)
    g.ones_b = k.sbt("ones_b", [128, 128], BF16)
    k.memset(g.ones_b, 1.0)
    g.modA = k.sbt("modA", [128, 4, 8, 3])
    g.modB = k.sbt("modB", [128, 4, 8, 3])
    g.modG = k.sbt("modG", [128, 4, 8, 3])
    g.ynq = k.sbt("ynq_g", [1, 2, 512], BF16)
    k.freeze_globals()

    phase_mod(g)
    for l in range(n_layers):
        last = (l == n_layers - 1)
        if l % 2 == 0:
            e = l // 2
            if stop >= 1:
                phase_proj(g, l, g.ev_w_in[e], 4608, even_specs(g))
            if stop >= 2:
                phase_lru(g, l, e)
            if stop >= 3:
                phase_attn(g, l, e)
            if stop >= 4:
                phase_out(g, l, g.ev_w_out[e], last)
        else:
            o = l // 2
            if stop >= 1:
                phase_proj(g, l, g.od_w_in[o][:, 0:4096], 4096, odd_specs1(g))
                phase_proj(g, l, g.od_w_in[o][:, 4096:8192], 4096, odd_specs2(g))
            if stop >= 2:
                phase_ret(g, l, o)
            if stop >= 3:
                for L in (S, CTX):
                    phase_hyfilt(g, o, L)
                phase_hyspec(g, o, CTX)
                phase_hyspec2(g, o)
                phase_hyprep(g, o)
            if stop >= 4:
                for b in range(NB):
                    for o2 in range(2):
                        phase_hyconv(g, l, o, b, 0, o2)
                        phase_hyconv2(g, l, o, b, o2)
            if stop >= 5:
                phase_out(g, l, g.od_w_out[o], last)
    stats = k.emit(final_wait_units=("yout",))
    print("ops per engine", stats, "sems", k.n_sems, flush=True)
    return nc


def spcol(g, name, i=0, n=1):
    o, c = g.lay[name]
    return g.sp[:, o + i:o + i + n]


def phase_mod(g):
    k = g.k
    k.phase()
    c_sb = k.sbt("c_sb", [128, 8, 3])
    k.dma(c_sb, g.cT)
    sc = k.sbt("sc", [128, 8, 3])
    k.act(sc, c_sb, AF.Silu)
    raw = k.sbt("modraw", [128, 96, 3])
    wm = [k.sbt(f"wm{i}", [128, 8, 1024]) for i in range(2)]
    it = 0
    for l in range(DEPTH):
        for part in range(3):
            w = wm[it % 2]
            it += 1
            src = g.mod_w[l][:, part * 1024:(part + 1) * 1024].re("(dc p) f -> p dc f", p=128)
            k.dma(w, src, q="sp" if it % 2 else "act")
            ps = g.P[it % 2]
            for fc in range(8):
                for dc in range(8):
                    k.mm(ps[:, fc * 4:fc * 4 + 3], w[:, dc, fc * 128:(fc + 1) * 128], sc[:, dc, :],
                         start=(dc == 0), stop=(dc == 7))
            for fc in range(8):
                k.ts(raw[:, (l * 3 + part) * 8 + fc, :], ps[:, fc * 4:fc * 4 + 3],
                     spcol(g, "mod_b", (l * 3 + part) * 8 + fc), None, ALU.add)
    for l in range(DEPTH):
        for dc in range(8):
            k.ts(g.modA[:, l, dc, :], raw[:, (l * 3 + 1) * 8 + dc, :], 1.0, spcol(g, "norm_pre", l * 8 + dc), ALU.add, ALU.mult)
            k.copy(g.modB[:, l, dc, :], raw[:, (l * 3 + 0) * 8 + dc, :])
            k.ts(g.modG[:, l, dc, :], raw[:, (l * 3 + 2) * 8 + dc, :], spcol(g, "norm_post", l * 8 + dc), None, ALU.mult)


def even_specs(g):
    return [
        (0, 1024, "raw32", g.XA),
        (1024, 1024, "silu16", g.GA),
        (2048, 1024, "rope16", g.Q),
        (3072, 256, "rope16", g.K),
        (3328, 256, "tok16", g.VT),
        (3584, 1024, "silu16", g.GB),
    ]


def load_weight_bf16(g, name, w_dram, ncols, nchunks):
    k = g.k
    wb = []
    for dc in range(nchunks):
        t = k.sbt(f"{name}{dc}", [128, ncols], BF16)
        step = 2304 if ncols % 2304 == 0 else ncols
        for c0 in range(0, ncols, step):
            k.dma(t[:, c0:c0 + step], w_dram[dc * 128:(dc + 1) * 128, c0:c0 + step], q="pool")
        wb.append(t)
    return wb


def norm_load(g, b, tt, xsrc, xs):
    k = g.k
    t0 = tt * TT
    k.dma(xs, xsrc[b][:, t0:t0 + TT].re("(dc p) t -> p dc t", p=128), q="pool")


def norm_compute(g, l, b, tt, xs, hT, scr, msbank):
    k = g.k
    j = 2 if tt == 0 else b
    sq = scr["sq"]
    k.act(sq, xs, AF.Square)
    ms = msbank[:, 0:TT]
    for dc in range(8):
        k.mm(ms, g.ones_b, sq[:, dc, :], start=(dc == 0), stop=(dc == 7))
    rstd = scr["rstd"]
    k.act(rstd, ms, AF.Sqrt, bias=g.epsc, scale=1.0 / D)
    k.recip(rstd, rstd)
    tmp = scr["tmp"]
    for dc in range(8):
        k.tt(tmp[:, dc, :], xs[:, dc, :], rstd, ALU.mult)
        k.ts(hT[:, dc, :], tmp[:, dc, :], g.modA[:, l, dc, j:j + 1], g.modB[:, l, dc, j:j + 1],
             ALU.mult, ALU.add)


def phase_proj(g, l, w_dram, F, specs):
    k = g.k
    k.phase()
    wb = load_weight_bf16(g, "wb", w_dram, F, 8)
    xsrc = g.xin if l == 0 else g.XR
    g.epsc = k.sbt("epsc", [128, 1])
    k.memset(g.epsc, EPS)
    need_rope = any(s[2] == "rope16" for s in specs)
    if need_rope:
        ropec = k.sbt("ropec", [128, T])
        ropes = k.sbt("ropes", [128, T])
        k.dma(ropec, g.cst["ropec"], q="act")
        k.dma(ropes, g.cst["ropes"], q="act")
        permf = k.sbt("permf", [128, 128])
        k.dma(permf, g.cst["perm"], q="act")
        perm = k.sbt("perm", [128, 128], BF16)
        k.copy(perm, permf)
    xs = [k.sbt(f"xs{i}", [128, 8, TT]) for i in range(3)]
    hT = [k.sbt(f"hT{i}", [128, 8, TT], BF16) for i in range(2)]
    scr = {"sq": k.sbt("sq", [128, 8, TT], BF16), "rstd": k.sbt("rstd", [128, TT]),
           "tmp": k.sbt("tmp", [128, 8, TT])}
    st32 = [k.sbt(f"st32_{i}", [128, 8, TT]) for i in range(2)]
    st16 = [k.sbt(f"st16_{i}", [128, 8, TT], BF16) for i in range(2)]
    stt16 = [k.sbt(f"stt16_{i}", [128, 512], BF16) for i in range(2)]
    qb = [k.sbt(f"qb{i}", [128, TT], BF16) for i in range(2)]
    t1 = [k.sbt(f"rt1_{i}", [128, TT]) for i in range(2)]
    t2 = [k.sbt(f"rt2_{i}", [128, TT]) for i in range(2)]
    cnt = {"ps": 0, "s32": 0, "s16": 0, "t16": 0, "q": 0, "dq": 0}
    tiles = [(b_, t_) for b_ in range(NB) for t_ in range(NT)]
    norm_load(g, tiles[0][0], tiles[0][1], xsrc, xs[0])
    norm_load(g, tiles[1][0], tiles[1][1], xsrc, xs[1])
    norm_compute(g, l, tiles[0][0], tiles[0][1], xs[0], hT[0], scr, g.P[7])
    for it, (b, tt) in enumerate(tiles):
        if True:
            t0 = tt * TT
            h_ = hT[it % 2]
            if it + 2 < len(tiles):
                norm_load(g, tiles[it + 2][0], tiles[it + 2][1], xsrc, xs[(it + 2) % 3])
            if it + 1 < len(tiles):
                norm_compute(g, l, tiles[it + 1][0], tiles[it + 1][1], xs[(it + 1) % 3], hT[(it + 1) % 2], scr, g.P[7])
            for (c0, ncols, kind, dest) in specs:
                if kind == "tok16":
                    for sb in range(TT // 128):
                        for cc in range(0, ncols, 512):
                            cw = min(512, ncols - cc)
                            ps = g.P[6][:, 0:cw]
                            for dc in range(8):
                                k.mm(ps, h_[:, dc, sb * 128:(sb + 1) * 128], wb[dc][:, c0 + cc:c0 + cc + cw],
                                     start=(dc == 0), stop=(dc == 7))
                            st = stt16[cnt["t16"] % 2][:, 0:cw]
                            cnt["t16"] += 1
                            scale = dest_scale(kind, dest, g)
                            k.act(st, ps, AF.Copy, scale=scale)
                            dstt = dest[0] if isinstance(dest, tuple) else dest
                            r0t = c0 - spec_base(specs, dstt)
                            k.dma(dstt[b][t0 + sb * 128:t0 + (sb + 1) * 128, r0t + cc:r0t + cc + cw], st,
                                  q="sp")
                    continue
                dst = dest[0] if isinstance(dest, tuple) else dest
                for gc in range(0, ncols // 128, 8):
                    ng = min(8, ncols // 128 - gc)
                    if kind == "raw32":
                        stg = st32[cnt["s32"] % 2]
                        cnt["s32"] += 1
                    else:
                        stg = st16[cnt["s16"] % 2]
                        cnt["s16"] += 1
                    for ci in range(ng):
                        f0 = c0 + (gc + ci) * 128
                        ps = g.P[cnt["ps"] % 4][:, 0:TT]
                        cnt["ps"] += 1
                        for dc in range(8):
                            k.mm(ps, wb[dc][:, f0:f0 + 128], h_[:, dc, :], start=(dc == 0), stop=(dc == 7))
                        if kind == "raw32":
                            k.copy(stg[:, ci, :], ps, eng="dve")
                        elif kind == "silu16":
                            k.act(stg[:, ci, :], ps, AF.Silu)
                        elif kind == "copy16":
                            k.act(stg[:, ci, :], ps, AF.Copy, scale=dest_scale(kind, dest, g))
                        elif kind == "rope16":
                            q_ = qb[cnt["q"] % 2]
                            a_ = t1[cnt["q"] % 2]
                            b_ = t2[cnt["q"] % 2]
                            pp = g.P[4 + cnt["q"] % 2][:, 0:TT]
                            cnt["q"] += 1
                            k.act(q_, ps, AF.Copy)
                            k.mm(pp, perm, q_)
                            k.tt(a_, ps, ropec[:, t0:t0 + TT], ALU.mult)
                            k.tt(b_, pp, ropes[:, t0:t0 + TT], ALU.mult)
                            k.tt(stg[:, ci, :], a_, b_, ALU.add)
                    r0 = (c0 - spec_base(specs, dst)) + gc * 128
                    k.dma(dst[b][r0:r0 + ng * 128, t0:t0 + TT].re("(c p) t -> p c t", p=128),
                          stg[:, 0:ng, :], q="act" if cnt["dq"] % 2 else "sp")
                    cnt["dq"] += 1


def spec_base(specs, dst):
    for (c0, ncols, kind, dest) in specs:
        d = dest[0] if isinstance(dest, tuple) else dest
        if d is dst:
            return c0
    raise KeyError


def dest_scale(kind, dest, g):
    if isinstance(dest, tuple):
        return dest[1]
    return 1.0


def rev(v, lo, hi):
    a = v.ap[:, lo:hi]
    pat = [list(p) for p in a.ap]
    off = a.offset + (hi - lo - 1) * pat[-1][0]
    pat[-1][0] = -pat[-1][0]
    return V(AP(a.tensor, off, pat), v.units)


def dwconv(k, out, x, wcols, bcol, taps, left, segs, eng="dve"):
    k.ts(out, x, wcols[left], bcol, ALU.mult, ALU.add, eng=eng)
    for kk in range(taps):
        o = kk - left
        if o == 0:
            continue
        for (lo, hi) in segs:
            a = lo + max(0, -o)
            bnd = hi - max(0, o)
            k.stt(out[:, a:bnd], x[:, a + o:bnd + o], wcols[kk], out[:, a:bnd], ALU.mult, ALU.add, eng=eng)


def phase_lru(g, l, e):
    k = g.k
    k.phase()
    segs = [(0, CTX), (CTX, T)]
    lam = spcol(g, f"lru_lam{e}", 0, 16)
    cA = k.sbt("cA", [128, 16])
    cA2 = k.sbt("cA2", [128, 16])
    k.act(cA, lam, AF.Exp, scale=-1.0)
    k.act(cA, cA, AF.Ln, bias=1.0)
    k.ts(cA2, cA, -16.0, None, ALU.mult)
    k.ts(cA, cA, -8.0, None, ALU.mult)
    one_c = k.sbt("one_c", [128, 1])
    k.memset(one_c, 1.0)
    wa = k.sbt("wa", [128, 2, 8, 128], BF16)
    wx = k.sbt("wx", [128, 2, 8, 128], BF16)
    for d in range(2):
        k.dma(wa[:, d], g.lru_wa[e][d].re("n c d -> c n d"), q="pool")
        k.dma(wx[:, d], g.lru_wx[e][d].re("n c d -> c n d"), q="pool")
    xa = [k.sbt(f"xa{i}", [128, T]) for i in range(2)]
    ga = [k.sbt(f"ga{i}", [128, T], BF16) for i in range(2)]
    u = k.sbt("u", [128, T])
    ub = k.sbt("ub", [128, T], BF16)
    r_ = k.sbt("r", [128, T])
    i_ = k.sbt("i", [128, T])
    s_ = k.sbt("s", [128, T])
    hh = [k.sbt(f"h{d}", [128, T]) for d in range(2)]
    yo = [k.sbt(f"yo{i}", [128, T], BF16) for i in range(2)]
    pc = 0
    items = [(b_, n_) for b_ in range(NB) for n_ in range(8)]

    def lru_load(i):
        b_, n_ = items[i]
        k.dma(xa[i % 2], g.XA[b_][n_ * 128:(n_ + 1) * 128, :], q="sp")
        k.dma(ga[i % 2], g.GA[b_][n_ * 128:(n_ + 1) * 128, :], q="act")
    lru_load(0)
    for it, (b, n) in enumerate(items):
        if True:
            xa_ = xa[it % 2]
            ga_ = ga[it % 2]
            yo_ = yo[it % 2]
            if it + 1 < len(items):
                lru_load(it + 1)
            wc = [spcol(g, f"lru_cw{e}", kk * 8 + n) for kk in range(4)]
            dwconv(k, u, xa_, wc, spcol(g, f"lru_cb{e}", n), 4, 2, segs)
            k.copy(ub, u, eng="pool")
            for d in range(2):
                for c0 in range(0, T, 512):
                    cw = min(512, T - c0)
                    pr = g.P[pc % 8][:, 0:cw]
                    pi = g.P[(pc + 1) % 8][:, 0:cw]
                    pc += 2
                    k.mm(pr, wa[:, d, n, :], ub[:, c0:c0 + cw])
                    k.mm(pi, wx[:, d, n, :], ub[:, c0:c0 + cw])
                    k.act(r_[:, c0:c0 + cw], pr, AF.Sigmoid, bias=spcol(g, f"lru_ba{e}", d * 8 + n))
                    k.act(i_[:, c0:c0 + cw], pi, AF.Sigmoid, bias=spcol(g, f"lru_bx{e}", d * 8 + n))
                k.act(s_, r_, AF.Exp, scale=cA2[:, d * 8 + n:d * 8 + n + 1])
                k.act(r_, r_, AF.Exp, scale=cA[:, d * 8 + n:d * 8 + n + 1])
                k.ts(s_, s_, 1.0, None, ALU.min)
                k.act(s_, s_, AF.Sqrt, bias=one_c, scale=-1.0)
                k.tt(i_, i_, u, ALU.mult, eng="pool")
                k.tt(i_, i_, s_, ALU.mult, eng="pool")
                h_ = hh[d]
                if d == 0:
                    k.scan(h_, r_, i_, 0.0)
                else:
                    k.scan(rev(h_, 0, CTX), rev(r_, 0, CTX), rev(i_, 0, CTX), 0.0)
                    k.scan(rev(h_, CTX, T), rev(r_, CTX, T), rev(i_, CTX, T), h_[:, 0:1])
            k.tt(hh[0], hh[0], hh[1], ALU.add, eng="pool")
            k.tt(yo_, hh[0], ga_, ALU.mult, eng="pool")
            k.dma(g.MIXT[b][n * 128:(n + 1) * 128, :], yo_, q="pool")


def phase_attn(g, l, e):
    k = g.k
    k.phase()
    scale = 128 ** -0.5
    mprev_f = k.sbt("mprev_f", [128, 512])
    mnext_f = k.sbt("mnext_f", [128, 512])
    k.dma(mprev_f, g.cst["mprev"], q="act")
    k.dma(mnext_f, g.cst["mnext"], q="act")
    mprev = k.sbt("mprev", [128, 512], BF16)
    mnext = k.sbt("mnext", [128, 512], BF16)
    k.copy(mprev, mprev_f)
    k.copy(mnext, mnext_f)
    sinkexp = k.sbt("sinkexp", [128, 8])
    k.act(sinkexp, spcol(g, f"sink{e}", 0, 8), AF.Exp)
    onesf = k.sbt("onesf", [128, 128])
    k.memset(onesf, 1.0)
    sinkrow = k.sbt("sinkrow", [128, 1024])
    for h_ in range(8):
        k.ts(sinkrow[:, h_ * 128:(h_ + 1) * 128], onesf, sinkexp[:, h_:h_ + 1], None, ALU.mult)
    NBLK = T // 128
    Kt = [k.sbt(f"Kt{i}", [128, 2, T], BF16) for i in range(2)]
    Vt = [k.sbt(f"Vt{i}", [128, NBLK, 256], BF16) for i in range(2)]
    Qt = [k.sbt(f"Qt{i}", [128, 1024], BF16) for i in range(2)]
    Gt = [k.sbt(f"Gt{i}", [128, 1024], BF16) for i in range(2)]
    pT = [k.sbt(f"pT{i}", [128, 512], BF16) for i in range(8)]
    den = [k.sbt(f"den{i}", [128, 512]) for i in range(2)]
    ost = [k.sbt(f"ost{i}", [128, 1024], BF16) for i in range(2)]
    otmp = [k.sbt(f"otmp{i}", [128, 512]) for i in range(2)]
    pcnt = 0
    ppc = 0
    items = [(b_, k_) for b_ in range(NB) for k_ in range(NBLK)]

    def attn_load(i):
        b_, k_ = items[i]
        if k_ == 0:
            k.dma(Kt[b_], g.K[b_].re("(h p) t -> p h t", p=128), q="sp")
            k.dma(Vt[b_], g.VT[b_].re("(n p) c -> p n c", p=128), q="act")
        cc = k_ * 128
        k.dma(Qt[i % 2].re("p (h t) -> p h t", h=8), g.Q[b_][:, cc:cc + 128].re("(h p) t -> p h t", p=128), q="sp")
        k.dma(Gt[i % 2].re("p (h t) -> p h t", h=8), g.GB[b_][:, cc:cc + 128].re("(h p) t -> p h t", p=128), q="act")
    attn_load(0)
    for it, (b, blk) in enumerate(items):
        if True:
            Kb = Kt[b]
            Vb = Vt[b]
            q_ = Qt[it % 2]
            g_ = Gt[it % 2]
            o_ = ost[it % 2]
            c0 = blk * 128
            if it + 1 < len(items):
                attn_load(it + 1)
            if blk < 2:
                kbs = [(0, None), (1, None)]
            else:
                kbs = []
                if blk > 2:
                    kbs.append((blk - 1, mprev))
                kbs.append((blk, None))
                if blk < NBLK - 1:
                    kbs.append((blk + 1, mnext))
                kbs += [(0, None), (1, None)]
            for hh in range(2):
                po = g.P[4 + ppc % 2]
                pd = g.P[6 + ppc % 2]
                ppc += 1
                rhs = q_[:, hh * 512:(hh + 1) * 512]
                pts = []
                for (kb, msk) in kbs:
                    pst = g.P[pcnt % 4]
                    p_ = pT[pcnt % 8]
                    pcnt += 1
                    k.mm(pst, Kb[:, hh, kb * 128:(kb + 1) * 128], rhs)
                    k.act(p_, pst, AF.Exp, scale=scale)
                    if msk is not None:
                        k.tt(p_, p_, msk, ALU.mult, eng="pool")
                    pts.append((kb, p_))
                for ii, (kb, p_) in enumerate(pts):
                    first = (ii == 0)
                    lastk = (ii == len(pts) - 1)
                    k.mm(po, Vb[:, kb, hh * 128:(hh + 1) * 128], p_, start=first, stop=lastk)
                for ii, (kb, p_) in enumerate(pts):
                    first = (ii == 0)
                    lastk = (ii == len(pts) - 1)
                    k.mm(pd, g.ones_b, p_, start=first, stop=lastk)
                dn = den[ppc % 2]
                ot = otmp[ppc % 2]
                k.tt(dn, pd, sinkrow[:, hh * 512:(hh + 1) * 512], ALU.add)
                k.recip(dn, dn)
                k.tt(ot, po, dn, ALU.mult)
                k.tt(o_[:, hh * 512:(hh + 1) * 512], ot, g_[:, hh * 512:(hh + 1) * 512], ALU.mult, eng="pool")
            k.dma(g.MIXT[b][1024:2048, c0:c0 + 128].re("(h p) t -> p h t", p=128), o_.re("p (h t) -> p h t", h=8), q="pool")


def phase_out(g, l, w_dram, last):
    k = g.k
    k.phase()
    wo = load_weight_bf16(g, "wo", w_dram, D, 16)
    xsrc = g.xin if l == 0 else g.XR
    epsc = k.sbt("epsc", [128, 1])
    k.memset(epsc, EPS)
    mix = [k.sbt(f"mix{i}", [128, 16, TT], BF16) for i in range(2)]
    xs = [k.sbt(f"xs{i}", [128, 8, TT]) for i in range(2)]
    xn = [k.sbt(f"xn{i}", [128, 8, TT]) for i in range(2)]
    ysb = k.sbt("ysb", [128, 8, TT])
    sq = k.sbt("sq", [128, 8, TT], BF16)
    rstd = k.sbt("rstd", [128, TT])
    tmp = k.sbt("tmp", [128, TT])
    pc = 0
    tiles = [(b_, t_) for b_ in range(NB) for t_ in range(NT)]

    def out_load(i):
        b_, t_ = tiles[i]
        k.dma(mix[i % 2], g.MIXT[b_][:, t_ * TT:(t_ + 1) * TT].re("(c p) t -> p c t", p=128), q="pool")
        k.dma(xs[i % 2], xsrc[b_][:, t_ * TT:(t_ + 1) * TT].re("(c p) t -> p c t", p=128), q="pool")
    out_load(0)
    for it, (b, tt) in enumerate(tiles):
        if True:
            t0 = tt * TT
            j = 2 if tt == 0 else b
            m_ = mix[it % 2]
            x_ = xs[it % 2]
            n_ = xn[it % 2]
            if it + 1 < len(tiles):
                out_load(it + 1)
            ms = g.P[7][:, 0:TT]
            for dmc in range(8):
                ps = g.P[pc % 6][:, 0:TT]
                pc += 1
                for fc in range(16):
                    k.mm(ps, wo[fc][:, dmc * 128:(dmc + 1) * 128], m_[:, fc, :], start=(fc == 0), stop=(fc == 15))
                k.act(sq[:, dmc, :], ps, AF.Square)
                k.copy(ysb[:, dmc, :], ps, eng="dve")
            for dmc in range(8):
                k.mm(ms, g.ones_b, sq[:, dmc, :], start=(dmc == 0), stop=(dmc == 7))
            k.act(rstd, ms, AF.Sqrt, bias=epsc, scale=1.0 / D)
            k.recip(rstd, rstd)
            for dmc in range(8):
                k.tt(tmp, ysb[:, dmc, :], rstd, ALU.mult)
                k.stt(n_[:, dmc, :], tmp, g.modG[:, l, dmc, j:j + 1], x_[:, dmc, :], ALU.mult, ALU.add)
            dst = g.yout if last else g.XR
            k.dma(dst[b][:, t0:t0 + TT].re("(c p) t -> p c t", p=128), n_, q="sp")


def host_inputs(inp, core, consts, sp):
    b0 = core * NB
    xin = np.empty((NB, D, T), np.float32)
    for j in range(NB):
        xin[j, :, :CTX] = inp["ctx"][b0 + j].T
        xin[j, :, CTX:] = inp["x"][b0 + j].T
    cols = np.stack([inp["c"][b0], inp["c"][b0 + 1], inp["c_ctx"]], axis=-1)
    cT = np.ascontiguousarray(cols.reshape(8, 128, 3).transpose(1, 0, 2))
    m = {"xin": xin, "cT": cT, "sp": sp,
         "mod_w": inp["mod_w"], "ev_w_in": inp["ev_w_in"], "ev_w_out": inp["ev_w_out"],
         "lru_wa": inp["lru_wa"], "lru_wx": inp["lru_wx"],
         "od_w_in": inp["od_w_in"], "od_w_out": inp["od_w_out"], "hy_w1": inp["hy_w1"], "hy_w2": inp["hy_w2"],
         "hy_w3": inp["hy_w3"], "hy_bias": inp["hy_bias"]}
    m.update(consts)
    return m


_NC_CACHE = {}


def kernel(**inputs):
    inp = {k_: np.asarray(v, np.float32) for k_, v in inputs.items()}
    n_cores = 8
    if "nc" not in _NC_CACHE:
        _NC_CACHE["nc"] = build_program()
    nc = _NC_CACHE["nc"]
    consts = host_consts()
    sp = host_sp(inp)
    in_maps = [host_inputs(inp, c, consts, sp) for c in range(n_cores)]
    res = run_bass_kernel_spmd(nc, in_maps, core_ids=list(range(n_cores)))
    out = np.empty((16, S, D), np.float32)
    for c in range(n_cores):
        y = res.results[c]["yout"]
        for j in range(NB):
            out[c * NB + j] = y[j][:, CTX:].T
    return out


def odd_specs1(g):
    return [(0, 3072, "raw32", g.Z), (3072, 1024, "silu16", g.GH)]


def odd_specs2(g):
    ks = 128 ** -0.5
    return [(0, 1024, "copy16", g.RQ), (1024, 1024, "copy16", (g.RK, ks)), (1024, 1024, "tok16", (g.RKT, ks)),
            (2048, 1024, "tok16", g.RVT), (3072, 1024, "silu16", g.GD)]


def host_consts_odd():
    c = {}
    jj = np.arange(128, dtype=np.float32)[:, None]
    ii = np.arange(128, dtype=np.float32)[None, :]
    retc = np.zeros((128, 6 * 128 + 2), np.float32)
    retc[:, 0:128] = np.maximum(ii - jj, 0)
    retc[:, 128:256] = (ii >= jj)
    retc[:, 256:384] = np.maximum(jj - ii, 0)
    retc[:, 384:512] = (jj > ii)
    retc[:, 512:640] = ii + 1.0
    retc[:, 640:768] = 128.0 - ii
    retc[:, 768] = 127.0 - jj[:, 0]
    retc[:, 769] = jj[:, 0]
    c["retc"] = retc
    return c


def phase_ret(g, l, o):
    k = g.k
    k.phase()
    NBLK = T // 128
    retc = k.sbt("retc", [128, 770])
    k.dma(retc, g.cst["retc"], q="act")
    lg = k.sbt("lg", [128, 16])
    k.act(lg, spcol(g, f"ret_logit{o}", 0, 16), AF.Sigmoid)
    k.act(lg, lg, AF.Ln)
    inner = k.sbt("inner", [128, 16, 128])
    qdec = k.sbt("qdec", [128, 16, 128])
    kdec = k.sbt("kdec", [128, 16])
    cdec = k.sbt("cdec", [128, 16])
    for d in range(2):
        for h in range(8):
            c = d * 8 + h
            lgc = lg[:, c:c + 1]
            k.act(inner[:, c, :], retc[:, d * 256:d * 256 + 128], AF.Exp, scale=lgc)
            k.tt(inner[:, c, :], inner[:, c, :], retc[:, d * 256 + 128:d * 256 + 256], ALU.mult)
            k.act(qdec[:, c, :], retc[:, 512 + d * 128:640 + d * 128], AF.Exp, scale=lgc)
            k.act(kdec[:, c:c + 1], retc[:, 768 + d:769 + d], AF.Exp, scale=lgc)
    k.act(cdec, lg, AF.Exp, scale=128.0)
    epsc = k.sbt("epsc", [128, 1])
    k.memset(epsc, EPS)
    HP = 2
    qT = [k.sbt(f"qT{i}", [128, T], BF16) for i in range(HP)]
    kT = [k.sbt(f"kT{i}", [128, T], BF16) for i in range(HP)]
    ktok = [k.sbt(f"ktok{i}", [128, NBLK, 128], BF16) for i in range(HP)]
    vtok = [k.sbt(f"vtok{i}", [128, NBLK, 128], BF16) for i in range(HP)]
    gd = [k.sbt(f"gd{i}", [128, T], BF16) for i in range(HP)]
    oacc = [[k.sbt(f"oacc{i}_{d}", [128, T]) for d in range(2)] for i in range(HP)]
    Sf = [k.sbt(f"S{c}", [128, 128]) for c in range(4)]
    Sb = [k.sbt(f"Sb{c}", [128, 128], BF16) for c in range(4)]
    attS = [[k.sbt(f"attS{c}_{i}", [128, 128], BF16) for i in range(2)] for c in range(4)]
    qd = [[k.sbt(f"qd{c}_{i}", [128, 128], BF16) for i in range(2)] for c in range(4)]
    vdec = [[k.sbt(f"vdec{c}_{i}", [128, 128], BF16) for i in range(2)] for c in range(4)]
    sq = k.sbt("rsq", [128, 512], BF16)
    rstd = k.sbt("rrstd", [128, 512])
    yo = qT
    order = [list(range(NBLK)), [1, 0] + list(range(NBLK - 1, 1, -1))]
    for b in range(NB):
        for hp in range(8 // HP):
            for i in range(HP):
                h = hp * HP + i
                k.dma(qT[i], g.RQ[b][h * 128:(h + 1) * 128, :], q="sp")
                k.dma(kT[i], g.RK[b][h * 128:(h + 1) * 128, :], q="act")
                k.dma(ktok[i], g.RKT[b][:, h * 128:(h + 1) * 128].re("(n p) c -> p n c", p=128), q="sp")
                k.dma(vtok[i], g.RVT[b][:, h * 128:(h + 1) * 128].re("(n p) c -> p n c", p=128), q="act")
                k.dma(gd[i], g.GD[b][h * 128:(h + 1) * 128, :], q="sp")
            for s in range(NBLK):
                chains = [(i, d) for i in range(HP) for d in range(2)]

                def cvars(i, d):
                    h = hp * HP + i
                    ch = i * 2 + d
                    c = d * 8 + h
                    blk = order[d][s]
                    cs = slice(blk * 128, (blk + 1) * 128)
                    return h, ch, c, blk, cs
                for (i, d) in chains:
                    h, ch, c, blk, cs = cvars(i, d)
                    att = g.P[ch][:, 0:128]
                    a_ = attS[ch][s % 2]
                    k.mm(att, kT[i][:, cs], qT[i][:, cs])
                    k.tt(a_, att, inner[:, c, :], ALU.mult)
                    if s > 0:
                        k.tt(qd[ch][s % 2], qT[i][:, cs], qdec[:, c, :], ALU.mult, eng="pool")
                    if s < NBLK - 1:
                        k.ts(vdec[ch][s % 2], vtok[i][:, blk, :], kdec[:, c:c + 1], None, ALU.mult, eng="pool")
                if s < NBLK - 1:
                    for (i, d) in chains:
                        h, ch, c, blk, cs = cvars(i, d)
                        k.mm(g.P[4 + ch][:, 0:128], ktok[i][:, blk, :], vdec[ch][s % 2])
                for (i, d) in chains:
                    h, ch, c, blk, cs = cvars(i, d)
                    ops = g.P[ch][:, 128:256]
                    kv = g.P[4 + ch][:, 0:128]
                    k.mm(ops, vtok[i][:, blk, :], attS[ch][s % 2], start=True, stop=(s == 0))
                    if s > 0:
                        k.mm(ops, Sb[ch], qd[ch][s % 2], start=False, stop=True)
                    k.copy(oacc[i][d][:, cs], ops, eng="act")
                    if s < NBLK - 1:
                        if s == 0:
                            k.copy(Sf[ch], kv)
                        else:
                            k.stt(Sf[ch], Sf[ch], cdec[:, c:c + 1], kv, ALU.mult, ALU.add)
                        k.copy(Sb[ch], Sf[ch], eng="act")
            for i in range(HP):
                h = hp * HP + i
                oa = oacc[i][0]
                k.tt(oa, oa, oacc[i][1], ALU.add)
                for c0 in range(0, T, 512):
                    cw = min(512, T - c0)
                    k.act(sq[:, 0:cw], oa[:, c0:c0 + cw], AF.Square)
                    ms = g.P[i][:, 0:cw]
                    k.mm(ms, g.ones_b, sq[:, 0:cw])
                    k.act(rstd[:, 0:cw], ms, AF.Sqrt, bias=epsc, scale=1.0 / 128)
                    k.recip(rstd[:, 0:cw], rstd[:, 0:cw])
                    k.tt(oa[:, c0:c0 + cw], oa[:, c0:c0 + cw], rstd[:, 0:cw], ALU.mult)
                k.tt(yo[i], oa, gd[i], ALU.mult, eng="pool")
                k.dma(g.MIXT[b][1024 + h * 128:1024 + (h + 1) * 128, :], yo[i], q="sp")


def host_consts_hy():
    c = {}
    for L in (4096, 256):
        t = np.linspace(0.0, 1.0, L, dtype=np.float32)[:, None]
        bands = 16
        w = (2.0 * np.float32(math.pi) * np.arange(L, dtype=np.float32)[:, None] / np.float32(L)).astype(np.float32)
        f = np.linspace(1e-4, bands - 1, bands, dtype=np.float32)[None]
        z = np.concatenate([t, np.cos(f * w), -np.sin(f * w)], axis=-1).astype(np.float32)
        c[f"zemb{L}"] = np.ascontiguousarray(z.T)
        c[f"tneg{L}"] = np.ascontiguousarray((-t[:, 0]).reshape(L // 128, 128).T)
        N = 2 * L
        kk = np.arange(L, dtype=np.int64)
        prod = (kk[:, None] * kk[None, :]) % N
        ang = prod.astype(np.float64) * (2.0 * math.pi / N)
        c[f"cm{L}"] = np.cos(ang).astype(np.float32).astype(BF)
        c[f"sm{L}"] = np.sin(ang).astype(np.float32).astype(BF)
    max_decay = math.log(1e-2) / 0.3
    min_decay = math.log(1e-2) / 1.5
    deltas = np.linspace(min_decay, max_decay, 1024, dtype=np.float32)
    c["dabs"] = np.broadcast_to(np.abs(deltas)[None, :], (128, 1024)).astype(np.float32).copy()
    alt = np.where(np.arange(128) % 2 == 0, 1.0, -1.0).astype(np.float32)
    c["altc"] = alt[:, None].copy()
    c["altrow"] = np.where(np.arange(256) % 2 == 0, 1.0, -1.0).astype(np.float32)[None, :].copy()
    return c


def sin_reduce(k, out, arg, tmp_i, tmp_f):
    k.ts(arg, arg, 1.0 / (2 * math.pi), 64.5, ALU.mult, ALU.add)
    k.copy(tmp_i, arg)
    k.copy(tmp_f, tmp_i)
    k.tt(arg, arg, tmp_f, ALU.subtract)
    k.stt(arg, arg, 0.0, arg, ALU.is_lt, ALU.add)
    k.ts(arg, arg, 2 * math.pi, -math.pi, ALU.mult, ALU.add)
    k.ts(arg, arg, -math.pi, math.pi, ALU.max, ALU.min)
    k.act(out, arg, AF.Sin)


def phase_hyfilt(g, o, L):
    k = g.k
    k.phase()
    zemb = k.sbt("zemb", [33, L])
    k.dma(zemb, g.cst[f"zemb{L}"], q="sp")
    w1 = k.sbt("w1", [33, 64])
    k.dma(w1, g.hy_w1[o], q="act")
    w2 = k.sbt("w2", [64, 64])
    k.dma(w2, g.hy_w2[o], q="act")
    w3 = k.sbt("w3", [64, 4096])
    k.dma(w3, g.hy_w3[o], q="sp")
    tneg = k.sbt("tneg", [128, L // 128])
    k.dma(tneg, g.cst[f"tneg{L}"], q="act")
    dabs = k.sbt("dabs", [128, 1024])
    k.dma(dabs, g.cst["dabs"], q="act")
    bias = k.sbt("hbias", [1, 2048])
    k.dma(bias, g.hy_bias[o:o + 1].re("a b c -> a (b c)"), q="act")
    hid1 = k.sbt("hid1", [64, L])
    hid2 = k.sbt("hid2", [64, L])
    arg = k.sbt("arg", [64, 512])
    ti = k.sbt("ti", [64, 512], I32)
    tf = k.sbt("tf", [64, 512])
    b1 = spcol(g, f"hy_b1{o}")[0:64]
    b2 = spcol(g, f"hy_b2{o}")[0:64]
    fr = spcol(g, f"hy_freq{o}")[0:64]
    for (src, wgt, bcol, dst) in ((zemb, w1, b1, hid1), (hid1, w2, b2, hid2)):
        for c0 in range(0, L, 512):
            cw = min(512, L - c0)
            ps = g.P[0][0:64, 0:cw]
            k.mm(ps, wgt, src[:, c0:c0 + cw])
            k.ts(arg[:, 0:cw], ps, bcol, fr, ALU.add, ALU.mult)
            sin_reduce(k, dst[:, c0:c0 + cw], arg[:, 0:cw], ti[:, 0:cw], tf[:, 0:cw])
    dec = k.sbt("dec", [128, 1024])
    hf = k.sbt("hf", [128, 512])
    hb = k.sbt("hb", [128, 512])
    hs = [k.sbt(f"hs{i}", [128, 512], BF16) for i in range(2)]
    hd = [k.sbt(f"hd{i}", [128, 512], BF16) for i in range(2)]
    it = 0
    HS, HD = g.HS[L], g.HD[L]
    for tb in range(L // 128):
        k.act(dec, dabs, AF.Exp, scale=tneg[:, tb:tb + 1])
        for o2 in range(2):
            for ch in range(2):
                pf = g.P[1 + it % 2]
                pb = g.P[3 + it % 2]
                cf = (o2 * 2 + 0) * 1024 + ch * 512
                cb = (o2 * 2 + 1) * 1024 + ch * 512
                k.mm(pf, hid2[:, tb * 128:(tb + 1) * 128], w3[:, cf:cf + 512])
                k.mm(pb, hid2[:, tb * 128:(tb + 1) * 128], w3[:, cb:cb + 512])
                k.tt(hf, pf, dec[:, ch * 512:(ch + 1) * 512], ALU.mult)
                k.tt(hb, pb, dec[:, ch * 512:(ch + 1) * 512], ALU.mult)
                if tb == 0:
                    k.memset(hb[0:1, :], 0.0)
                    bo = o2 * 1024 + ch * 512
                    k.tt(hf[0:1, :], hf[0:1, :], bias[:, bo:bo + 512], ALU.add)
                s_ = hs[it % 2]
                d_ = hd[it % 2]
                it += 1
                k.tt(s_, hf, hb, ALU.add)
                k.tt(d_, hf, hb, ALU.subtract, eng="pool")
                k.dma(HS[o2][tb * 128:(tb + 1) * 128, ch * 512:(ch + 1) * 512], s_, q="sp")
                k.dma(HD[o2][tb * 128:(tb + 1) * 128, ch * 512:(ch + 1) * 512], d_, q="act")


def hy_tables(g, L):
    k = g.k
    nch = L // 128
    altf = k.sbt("altf", [128, 1])
    k.dma(altf, g.cst["altc"], q="act")
    altc = k.sbt("altc", [128, 1], BF16)
    k.copy(altc, altf)
    arf = k.sbt("arf", [1, 256])
    k.dma(arf, g.cst["altrow"], q="act")
    altrow = k.sbt("altrow", [1, 256], BF16)
    k.copy(altrow, arf)
    Ct = [k.sbt(f"Ct{i}", [128, nch, 128], BF16) for i in range(2)]
    St = [k.sbt(f"St{i}", [128, nch, 128], BF16) for i in range(2)]
    return altc, altrow, Ct, St


def load_ft(g, L, kc, Ct, St, it):
    k = g.k
    c_ = Ct[it % 2]
    s_ = St[it % 2]
    k.dma(c_, g.cst[f"cm{L}"][:, kc * 128:(kc + 1) * 128].re("(tc p) k -> p tc k", p=128), q="sp")
    k.dma(s_, g.cst[f"sm{L}"][:, kc * 128:(kc + 1) * 128].re("(tc p) k -> p tc k", p=128), q="act")
    return c_, s_


def phase_hyspec(g, o, L):
    k = g.k
    k.phase()
    nch = L // 128
    N = 2 * L
    altc, altrow, Ct, St = hy_tables(g, L)
    hs = k.sbt("hs_sb", [128, nch, 512], BF16)
    hd = k.sbt("hd_sb", [128, nch, 512], BF16)
    kr = [k.sbt(f"kr{i}", [128, 512]) for i in range(2)]
    ki = [k.sbt(f"ki{i}", [128, 512]) for i in range(2)]
    kn = k.sbt("kn", [1, 512])
    it = 0
    for o2 in range(2):
        for ch in range(2):
            k.dma(hs, g.HS[L][o2][:, ch * 512:(ch + 1) * 512].re("(tc p) c -> p tc c", p=128), q="sp")
            k.dma(hd, g.HD[L][o2][:, ch * 512:(ch + 1) * 512].re("(tc p) c -> p tc c", p=128), q="act")
            pn = g.P[4][0:1, :]
            for tc in range(nch):
                k.mm(pn, altc, hs[:, tc, :], start=(tc == 0), stop=(tc == nch - 1))
            k.act(kn, pn, AF.Copy, scale=1.0 / N)
            k.dma(g.KN[L][o2][:, ch * 512:(ch + 1) * 512], kn, q="sp")
            for kc in range(nch):
                c_, s_ = load_ft(g, L, kc, Ct, St, it)
                pa = g.P[it % 2]
                pb = g.P[2 + it % 2]
                r_ = kr[it % 2]
                i_ = ki[it % 2]
                it += 1
                for tc in range(nch):
                    k.mm(pa, c_[:, tc, :], hs[:, tc, :], start=(tc == 0), stop=(tc == nch - 1))
                for tc in range(nch):
                    k.mm(pb, s_[:, tc, :], hd[:, tc, :], start=(tc == 0), stop=(tc == nch - 1))
                k.act(r_, pa, AF.Copy, scale=2.0 / N)
                k.act(i_, pb, AF.Copy, scale=2.0 / N)
                if kc == 0:
                    k.ts(r_[0:1, :], r_[0:1, :], 0.5, None, ALU.mult)
                k.dma(g.KR[L][o2][kc * 128:(kc + 1) * 128, ch * 512:(ch + 1) * 512], r_, q="sp")
                k.dma(g.KI[L][o2][kc * 128:(kc + 1) * 128, ch * 512:(ch + 1) * 512], i_, q="act")


def phase_hyprep(g, o):
    k = g.k
    k.phase()
    NBLK = T // 128
    segs = [(0, CTX), (CTX, T)]
    z = [k.sbt(f"z{i}", [128, T]) for i in range(2)]
    zc = [k.sbt(f"zc{i}", [128, T]) for i in range(2)]
    zb = k.sbt("zb", [128, T], BF16)
    tok = [k.sbt(f"tok{i}", [128, NBLK, 128], BF16) for i in range(2)]
    it = 0
    pc = 0
    for b in range(NB):
        for chk in range(24):
            z_ = z[it % 2]
            zc_ = zc[it % 2]
            t_ = tok[it % 2]
            it += 1
            k.dma(z_, g.Z[b][chk * 128:(chk + 1) * 128, :], q="sp")
            wc = [spcol(g, f"hy_cw{o}", kk * 24 + chk) for kk in range(3)]
            dwconv(k, zc_, z_, wc, spcol(g, f"hy_cb{o}", chk), 3, 1, segs)
            if chk < 8:
                k.copy(zb, zc_, eng="act")
                for blk in range(NBLK):
                    pt = g.P[pc % 4].bc(BF16)[:, 0:128]
                    pc += 1
                    k.tr(pt, zb[:, blk * 128:(blk + 1) * 128], g.ident_b)
                    k.copy(t_[:, blk, :], pt, eng="act" if blk % 2 else "dve")
                k.dma(g.U1T[b][:, chk * 128:(chk + 1) * 128].re("(n p) c -> p n c", p=128), t_, q="act")
            else:
                k.dma(g.ZC[b][(chk - 8) * 128:(chk - 7) * 128, :], zc_, q="act")


def phase_hyconv(g, l, o, b, seg, o2):
    k = g.k
    k.phase()
    L = CTX if seg == 0 else S
    t_off = 0 if seg == 0 else CTX
    nch = L // 128
    altc, altrow, Ct, St = hy_tables(g, L)
    usrc = g.U1T if o2 == 0 else g.U2T
    u = k.sbt("u_sb", [128, nch, 512], BF16)
    Yr = k.sbt("Yr", [128, nch, 512], BF16)
    Yi = k.sbt("Yi", [128, nch, 512], BF16)
    ynq = k.sbt("ynq", [1, 512], BF16)
    knq = k.sbt("knq", [1, 512])
    kr = [k.sbt(f"kr{i}", [128, 512]) for i in range(2)]
    ki = [k.sbt(f"ki{i}", [128, 512]) for i in range(2)]
    t1 = k.sbt("t1", [128, 512])
    t2 = k.sbt("t2", [128, 512])
    Cn = [k.sbt(f"Cn{i}", [128, nch, 256], BF16) for i in range(1)]
    Sn = [k.sbt(f"Sn{i}", [128, nch, 256], BF16) for i in range(1)]
    xm = [k.sbt(f"xm{i}", [128, 256]) for i in range(2)]
    gh = [k.sbt(f"gh{i}", [128, 256], BF16) for i in range(2)]
    yb = [k.sbt(f"yb{i}", [128, 256], BF16) for i in range(2)]
    ytok = [k.sbt(f"ytok{i}", [128, 2, 128], BF16) for i in range(2)]
    it = 0
    it2 = 0
    it3 = 0
    for ch in range(2):
        k.dma(u, usrc[b][t_off:t_off + L, ch * 512:(ch + 1) * 512].re("(tc p) c -> p tc c", p=128), q="sp")
        k.dma(knq, g.KN[L][o2][:, ch * 512:(ch + 1) * 512], q="act")
        pn = g.P[6][0:1, :]
        for tc in range(nch):
            k.mm(pn, altc, u[:, tc, :], start=(tc == 0), stop=(tc == nch - 1))
        k.tt(ynq, pn, knq, ALU.mult)
        for kc in range(nch):
            c_, s_ = load_ft(g, L, kc, Ct, St, it)
            pa = g.P[it % 2]
            pb = g.P[2 + it % 2]
            r_ = kr[it % 2]
            i_ = ki[it % 2]
            it += 1
            k.dma(r_, g.KR[L][o2][kc * 128:(kc + 1) * 128, ch * 512:(ch + 1) * 512], q="sp")
            k.dma(i_, g.KI[L][o2][kc * 128:(kc + 1) * 128, ch * 512:(ch + 1) * 512], q="act")
            for tc in range(nch):
                k.mm(pa, c_[:, tc, :], u[:, tc, :], start=(tc == 0), stop=(tc == nch - 1))
            for tc in range(nch):
                k.mm(pb, s_[:, tc, :], u[:, tc, :], start=(tc == 0), stop=(tc == nch - 1))
            k.tt(t1, pa, r_, ALU.mult)
            k.tt(t2, pb, i_, ALU.mult)
            k.tt(Yr[:, kc, :], t1, t2, ALU.subtract)
            k.tt(t1, pa, i_, ALU.mult)
            k.tt(t2, pb, r_, ALU.mult)
            k.tt(Yi[:, kc, :], t1, t2, ALU.add)
        for nb in range(L // 256):
            cn = Cn[0]
            sn = Sn[0]
            it2 += 1
            k.dma(cn, g.cst[f"cm{L}"][:, nb * 256:(nb + 1) * 256].re("(kc p) n -> p kc n", p=128), q="sp")
            k.dma(sn, g.cst[f"sm{L}"][:, nb * 256:(nb + 1) * 256].re("(kc p) n -> p kc n", p=128), q="act")
            n0 = t_off + nb * 256
            for cs in range(4):
                crow = ch * 512 + cs * 128
                x_ = xm[it3 % 2]
                g_ = gh[it3 % 2]
                y_ = yb[it3 % 2]
                yt = ytok[it3 % 2]
                ps = g.P[4 + it3 % 2][:, 0:256]
                it3 += 1
                k.dma(x_, g.ZC[b][o2 * 1024 + crow:o2 * 1024 + crow + 128, n0:n0 + 256], q="sp")
                for kc in range(nch):
                    k.mm(ps, Yr[:, kc, cs * 128:(cs + 1) * 128], cn[:, kc, :], start=(kc == 0), stop=False)
                    k.mm(ps, Yi[:, kc, cs * 128:(cs + 1) * 128], sn[:, kc, :], start=False, stop=False)
                k.mm(ps, ynq[:, cs * 128:(cs + 1) * 128], altrow, start=False, stop=True)
                if o2 == 0:
                    k.tt(y_, ps, x_, ALU.mult)
                    for sb in range(2):
                        pt = g.P[7].bc(BF16)[:, sb * 128:(sb + 1) * 128]
                        k.tr(pt, y_[:, sb * 128:(sb + 1) * 128], g.ident_b)
                        k.copy(yt[:, sb, :], pt, eng="act")
                    k.dma(g.U2T[b][n0:n0 + 256, crow:crow + 128].re("(s p) c -> p s c", p=128), yt, q="act")
                else:
                    k.dma(g_, g.GH[b][crow:crow + 128, n0:n0 + 256], q="act")
                    k.tt(x_, ps, x_, ALU.mult)
                    k.tt(y_, x_, g_, ALU.mult, eng="pool")
                    k.dma(g.MIXT[b][crow:crow + 128, n0:n0 + 256], y_, q="act")


def host_consts_fft():
    c = {}
    N = 8192
    tc = np.arange(32)[:, None]
    k1 = np.arange(64)[None, :]
    a = 2 * np.pi * (tc * k1 % 64) / 64.0
    c["f1tab"] = np.concatenate([np.cos(a), np.sin(a)], axis=1).astype(np.float32).astype(BF)
    p = np.arange(128)[:, None, None]
    kk = (np.arange(64)[None, :, None] + 64 * np.arange(64)[None, None, :])
    ang = 2 * np.pi * ((kk * p) % N) / N
    c["t2tab"] = np.concatenate([np.cos(ang), np.sin(ang), -np.sin(ang), np.cos(ang)], axis=2).astype(np.float32).astype(BF)
    k2 = np.arange(64)[:, None, None]
    k1b = np.arange(64)[None, :, None]
    n2 = np.arange(128)[None, None, :]
    ang3 = 2 * np.pi * (((k1b + 64 * k2) * n2) % N) / N
    top = np.stack([np.cos(ang3), np.sin(ang3), -np.cos(ang3)], axis=2)
    bot = np.stack([np.sin(ang3), -np.cos(ang3), -np.sin(ang3)], axis=2)
    c["t3tab"] = np.concatenate([top, bot], axis=0).astype(np.float32).astype(BF)
    j = np.arange(64)[:, None]
    n1 = np.arange(32)[None, :]
    ag = 2 * np.pi * ((j * n1) % 64) / 64.0
    c["gtab"] = np.concatenate([np.cos(ag), -np.sin(ag)], axis=0).astype(np.float32).astype(BF)
    c["pm1"] = np.concatenate([np.ones(32), -np.ones(32)])[None, :].astype(np.float32)
    return c


def fft_f1(g, src, ch, Bd, f1tab, ubuf, bst, cnt):
    k = g.k
    for pg in range(4):
        u = ubuf[cnt["u"] % 2]
        cnt["u"] += 1
        k.dma(u, src[:, ch * 512:(ch + 1) * 512].re("(tc p) c -> tc p c", p=128)[:, pg * 32:(pg + 1) * 32, :], q="sp")
        for pq in range(4):
            st = bst[cnt["b"] % 2]
            cnt["b"] += 1
            for i in range(8):
                ps_ = pq * 8 + i
                pp = g.P[cnt["p"] % 8]
                cnt["p"] += 1
                k.mm(pp, f1tab, u[:, ps_, :])
                if i % 2:
                    k.act(st[:, i, :], pp, AF.Copy)
                else:
                    k.copy(st[:, i, :], pp, eng="dve")
            p0 = pg * 32 + pq * 8
            k.dma(Bd[:, p0:p0 + 8, :], st, q="act")


def phase_fft_f1(g, srcs):
    k = g.k
    k.phase()
    f1tab = k.sbt("f1tab", [32, 128], BF16)
    k.dma(f1tab, g.cst["f1tab"], q="act")
    ubuf = [k.sbt(f"fu{i}", [32, 32, 512], BF16) for i in range(2)]
    bst = [k.sbt(f"fb{i}", [128, 8, 512], BF16) for i in range(2)]
    cnt = {"u": 0, "b": 0, "p": 0}
    for (src, ch, Bd) in srcs:
        fft_f1(g, src, ch, Bd, f1tab, ubuf, bst, cnt)


def phase_hyspec2(g, o):
    k = g.k
    N = 8192
    for o2 in range(2):
        phase_fft_f1(g, [(g.HS[S][o2], 0, g.Bd[0]), (g.HS[S][o2], 1, g.Bd[1]),
                         (g.HD[S][o2], 0, g.Bd[2]), (g.HD[S][o2], 1, g.Bd[3])])
        k.phase()
        t2 = k.sbt("t2", [128, 64, 256], BF16)
        k.dma(t2, g.cst["t2tab"], q="sp")
        altf = k.sbt("altf", [128, 1])
        k.dma(altf, g.cst["altc"], q="act")
        altc = k.sbt("altc", [128, 1], BF16)
        k.copy(altc, altf)
        br = [[k.sbt(f"br{s}_{i}", [128, 8, 512], BF16) for i in range(2)] for s in range(2)]
        bs = [[k.sbt(f"bs{s}_{i}", [128, 8, 512], BF16) for i in range(2)] for s in range(2)]
        kr = [k.sbt(f"kr{i}", [64, 512], BF16) for i in range(2)]
        ks = [k.sbt(f"ks{i}", [64, 512], BF16) for i in range(2)]
        kn = k.sbt("kn", [1, 512])
        it = 0
        groups = [(c_, kg_) for c_ in range(2) for kg_ in range(8)]

        def sp_load(i):
            c_, kg_ = groups[i]
            for s_, Bd in ((0, g.Bd[c_]), (1, g.Bd[2 + c_])):
                k.dma(br[s_][i % 2], Bd[kg_ * 8:(kg_ + 1) * 8].re("k p c -> p k c"), q="sp")
                k.dma(bs[s_][i % 2], Bd[64 + kg_ * 8:64 + (kg_ + 1) * 8].re("k p c -> p k c"), q="sp")
        sp_load(0)
        for gi, (ch, kg) in enumerate(groups):
            bb = gi % 2
            if gi + 1 < len(groups):
                sp_load(gi + 1)
            for kk in range(8):
                k1 = kg * 8 + kk
                pr = g.P[it % 2][0:64, :]
                pi = g.P[2 + it % 2][0:64, :]
                r_ = kr[it % 2]
                s2 = ks[it % 2]
                it += 1
                k.mm(pr, t2[:, k1, 0:64], br[0][bb][:, kk, :], start=True, stop=False)
                k.mm(pr, t2[:, k1, 128:192], bs[0][bb][:, kk, :], start=False, stop=True)
                k.mm(pi, t2[:, k1, 64:128], br[1][bb][:, kk, :], start=True, stop=False)
                k.mm(pi, t2[:, k1, 0:64], bs[1][bb][:, kk, :], start=False, stop=True)
                k.act(r_, pr, AF.Copy, scale=2.0 / N)
                k.ts(s2, pi, 2.0 / N, None, ALU.mult)
                if k1 == 0:
                    k.ts(r_[0:1, :], pr[0:1, :], 1.0 / N, None, ALU.mult)
                    pn = g.P[4][0:1, :]
                    k.mm(pn, altc, br[0][bb][:, 0, :])
                    k.act(kn, pn, AF.Copy, scale=1.0 / N)
                    k.dma(g.KN2[o2][:, ch * 512:(ch + 1) * 512], kn, q="act")
                k.dma(g.KR2[o2][k1][:, ch * 512:(ch + 1) * 512], r_, q="act")
                k.dma(g.KS2[o2][k1][:, ch * 512:(ch + 1) * 512], s2, q="act")


def phase_hyconv2(g, l, o, b, o2):
    k = g.k
    usrc = g.U1T if o2 == 0 else g.U2T
    uv = usrc[b][CTX:CTX + S, :]
    phase_fft_f1(g, [(uv, 0, g.Bd[0]), (uv, 1, g.Bd[1])])
    k.phase()
    t2 = k.sbt("t2", [128, 64, 256], BF16)
    k.dma(t2, g.cst["t2tab"], q="sp")
    t3 = k.sbt("t3", [128, 64, 3, 128], BF16)
    k.dma(t3, g.cst["t3tab"], q="act")
    altf = k.sbt("altf", [128, 1])
    k.dma(altf, g.cst["altc"], q="act")
    altc = k.sbt("altc", [128, 1], BF16)
    k.copy(altc, altf)
    br = [k.sbt(f"br{i}", [128, 4, 512], BF16) for i in range(2)]
    bs = [k.sbt(f"bs{i}", [128, 4, 512], BF16) for i in range(2)]
    kr = [k.sbt(f"kr{i}", [128, 4, 512], BF16) for i in range(2)]
    ks = [k.sbt(f"ks{i}", [128, 4, 512], BF16) for i in range(2)]
    knq = [k.sbt(f"knq{i}", [1, 512]) for i in range(2)]
    p1 = [k.sbt(f"p1_{i}", [128, 512], BF16) for i in range(2)]
    p2 = [k.sbt(f"p2_{i}", [128, 512], BF16) for i in range(2)]
    dst_ = [k.sbt(f"dst{i}", [128, 2, 512], BF16) for i in range(2)]
    it = 0
    groups = [(c_, kg_) for c_ in range(2) for kg_ in range(16)]

    def f2_load(i):
        c_, kg_ = groups[i]
        bb_ = i % 2
        if kg_ == 0:
            k.dma(knq[c_], g.KN2[o2][:, c_ * 512:(c_ + 1) * 512], q="sp")
        k.dma(br[bb_], g.Bd[c_][kg_ * 4:(kg_ + 1) * 4].re("k p c -> p k c"), q="sp")
        k.dma(bs[bb_], g.Bd[c_][64 + kg_ * 4:64 + (kg_ + 1) * 4].re("k p c -> p k c"), q="sp")
        krv = g.KR2[o2][kg_ * 4:(kg_ + 1) * 4][:, :, c_ * 512:(c_ + 1) * 512].re("k q c -> q k c")
        ksv = g.KS2[o2][kg_ * 4:(kg_ + 1) * 4][:, :, c_ * 512:(c_ + 1) * 512].re("k q c -> q k c")
        k.dma(kr[bb_][0:64], krv, q="sp")
        k.dma(kr[bb_][64:128], krv, q="sp")
        k.dma(ks[bb_][0:64], ksv, q="sp")
        k.dma(ks[bb_][64:128], ksv, q="sp")
    f2_load(0)
    for gi, (ch, kg) in enumerate(groups):
        Dd = g.Dd[ch]
        bb = gi % 2
        if gi + 1 < len(groups):
            f2_load(gi + 1)
        for kk in range(4):
            k1 = kg * 4 + kk
            px = g.P[it % 2]
            pdr = g.P[2 + it % 2]
            pdi = g.P[4 + it % 2]
            a_ = p1[it % 2]
            b_ = p2[it % 2]
            d_ = dst_[it % 2]
            it += 1
            k.mm(px, t2[:, k1, 0:128], br[bb][:, kk, :], start=True, stop=False)
            k.mm(px, t2[:, k1, 128:256], bs[bb][:, kk, :], start=False, stop=True)
            if k1 == 0:
                pn = g.P[6][0:1, :]
                k.mm(pn, altc, br[bb][:, 0, :])
                k.tt(g.ynq[:, ch, :], pn, knq[ch], ALU.mult)
            k.tt(a_, px, kr[bb][:, kk, :], ALU.mult)
            k.tt(b_, px, ks[bb][:, kk, :], ALU.mult)
            k.mm(pdr, t3[:, k1, 0, :], a_, start=True, stop=False)
            k.mm(pdr, t3[:, k1, 1, :], b_, start=False, stop=True)
            k.mm(pdi, t3[:, k1, 1, :], a_, start=True, stop=False)
            k.mm(pdi, t3[:, k1, 2, :], b_, start=False, stop=True)
            k.act(d_[:, 0, :], pdr, AF.Copy)
            k.copy(d_[:, 1, :], pdi, eng="dve")
            k.dma(Dd[k1], d_[:, 0, :], q="act")
            k.dma(Dd[64 + k1], d_[:, 1, :], q="act")
    k.phase()
    gtab = k.sbt("gtab", [128, 32], BF16)
    k.dma(gtab, g.cst["gtab"], q="act")
    pmf = k.sbt("pmf", [1, 64])
    k.dma(pmf, g.cst["pm1"], q="act")
    pm = k.sbt("pm", [1, 64], BF16)
    k.copy(pm, pmf)
    NBL = S // 128
    altpat = k.sbt("altpat", [128, S])
    k.memset(altpat, 1.0)
    k.memset(altpat.re("p (a two) -> p a two", two=2)[:, :, 1], -1.0)
    ycol = [k.sbt(f"ycol{i}", [128, 1]) for i in range(2)]
    dt_ = [k.sbt(f"dt{i}", [128, 16, 512], BF16) for i in range(2)]
    yT = [k.sbt(f"yT{i}", [128, 32, 128]) for i in range(2)]
    xT = [k.sbt(f"xT{i}", [128, S]) for i in range(2)]
    ob = [k.sbt(f"ob{i}", [128, S], BF16) for i in range(2)]
    ghT = [k.sbt(f"ghT{i}", [128, S], BF16) for i in range(2)]
    tok = [k.sbt(f"tok{i}", [128, NBL, 128], BF16) for i in range(2)] if o2 == 0 else None
    it = 0
    pc = 0
    for ch in range(2):
        Dd = g.Dd[ch]
        for cpair in range(2):
            for ci in range(2):
                crow = ch * 512 + (cpair * 2 + ci) * 128
                k.dma(xT[ci], g.ZC[b][o2 * 1024 + crow:o2 * 1024 + crow + 128, CTX:CTX + S], q="act")
                if o2 == 1:
                    k.dma(ghT[ci], g.GH[b][crow:crow + 128, CTX:CTX + S], q="act")
            for n2g in range(8):
                d_ = dt_[it % 2]
                it += 1
                k.dma(d_, Dd[:, n2g * 16:(n2g + 1) * 16, :], q="sp")
                for ci in range(2):
                    cs = cpair * 2 + ci
                    ps = g.P[pc % 6]
                    pc += 1
                    for j in range(16):
                        k.mm(ps[:, j * 32:(j + 1) * 32], d_[:, j, cs * 128:(cs + 1) * 128], gtab, start=True, stop=True)
                    k.copy(yT[ci][:, :, n2g * 16:(n2g + 1) * 16], ps.re("p (a b) -> p b a", a=16),
                           eng="act" if ci else "dve")
            for ci in range(2):
                crow = ch * 512 + (cpair * 2 + ci) * 128
                cs = cpair * 2 + ci
                yflat = yT[ci].re("p a b -> p (a b)")
                pcol = g.P[7][:, 0:1]
                k.mm(pcol, g.ynq[:, ch, cs * 128:(cs + 1) * 128], pm[:, 0:1])
                k.copy(ycol[ci], pcol, eng="dve")
                k.stt(yflat, altpat, ycol[ci], yflat, ALU.mult, ALU.add)
                if o2 == 0:
                    k.tt(ob[ci], yflat, xT[ci], ALU.mult)
                    t_ = tok[ci]
                    for blk in range(NBL):
                        pt = g.P[6 + blk % 2].bc(BF16)[:, 0:128]
                        k.tr(pt, ob[ci][:, blk * 128:(blk + 1) * 128], g.ident_b)
                        k.copy(t_[:, blk, :], pt, eng="act" if blk % 2 else "dve")
                    k.dma(g.U2T[b][CTX:CTX + S, crow:crow + 128].re("(n p) c -> p n c", p=128), t_, q="act")
                else:
                    k.tt(yflat, yflat, xT[ci], ALU.mult)
                    k.tt(ob[ci], yflat, ghT[ci], ALU.mult, eng="pool")
                    k.dma(g.MIXT[b][crow:crow + 128, CTX:CTX + S], ob[ci], q="act")
```

```python
import numpy as np
import concourse.bass as bass
import concourse.mybir as mybir

F32 = mybir.dt.float32
BF16 = mybir.dt.bfloat16
AF = mybir.ActivationFunctionType
ALU = mybir.AluOpType
AX = mybir.AxisListType

SEM_CHUNK = 20000
NDMA_SEMS = 12


class V:
    __slots__ = ("ap", "units")

    def __init__(self, ap, units):
        self.ap = ap
        self.units = tuple(units)

    def __getitem__(self, idx):
        return V(self.ap[idx], self.units)

    def re(self, pat, **kw):
        return V(self.ap.rearrange(pat, **kw), self.units)

    def bc(self, dt):
        return V(self.ap.bitcast(dt), self.units)


class Op:
    __slots__ = ("eng", "fn", "deps", "dma", "sig", "waits", "idx", "has_dep")

    def __init__(self, eng, fn, deps, dma):
        self.eng = eng
        self.fn = fn
        self.deps = deps
        self.dma = dma
        self.sig = None
        self.waits = None
        self.has_dep = False


class Builder:
    def __init__(self, nc):
        self.nc = nc
        self.ops = []
        self.units = {}
        self.last_dma = {"sp": [], "act": [], "pool": []}
        self.pending = {}
        self.psum_units = set()
        self.arena = None
        self.arena_off = 0
        self.arena_base = 0
        self.uid = 0

    def init_arena(self, nbytes):
        self.arena_bytes = nbytes
        self.arena = self.nc.alloc_sbuf_tensor("arena", [128, nbytes // 4], F32)

    def sbt(self, name, shape, dtype=F32, glob=False):
        esz = 2 if dtype == BF16 else 4
        n = 1
        for d in shape[1:]:
            n *= d
        nb = (n * esz + 31) // 32 * 32
        off = self.arena_off
        assert off + nb <= self.arena_bytes, (name, off, nb)
        self.arena_off += nb
        ap = self.arena[:, off // 4:(off + nb) // 4]
        if esz == 2:
            ap = ap.bitcast(BF16)
        elif dtype != F32:
            ap = ap.bitcast(dtype)
        ap = ap[:, 0:n]
        if len(shape) == 3:
            ap = ap.rearrange("p (a b) -> p a b", a=shape[1])
        elif len(shape) == 4:
            ap = ap.rearrange("p (a b c) -> p a b c", a=shape[1], b=shape[2])
        if shape[0] != 128:
            ap = ap[0:shape[0]]
        self.uid += 1
        return V(ap, (f"{name}#{self.uid}",))

    def phase(self):
        self.barrier()
        self.arena_off = self.arena_base

    def freeze_globals(self):
        self.arena_base = self.arena_off

    def barrier(self):
        deps = set()
        last = {}
        for i, op in enumerate(self.ops):
            if not op.dma:
                last[op.eng] = i
        deps.update(last.values())
        for q, lst in self.last_dma.items():
            deps.update(lst[-NDMA_SEMS:])
        for e in ("pe", "act", "dve", "pool", "sp"):
            self.pending[e] = set(deps)

    def sb(self, name, shape, dtype=F32, units=None):
        h = self.nc.alloc_sbuf_tensor(name, list(shape), dtype)
        return V(h[:], units if units is not None else (name,))

    def ps(self, name, shape, dtype=F32):
        h = self.nc.alloc_psum_tensor(name, list(shape), dtype)
        self.psum_units.add(name)
        return V(h[:], (name,))

    def dram(self, name, shape, dtype=F32, kind="Internal"):
        h = self.nc.dram_tensor(name, list(shape), dtype, kind=kind)
        return V(h.ap(), (name,))

    def add(self, eng, fn, r=(), w=(), dma=False):
        deps = set()
        ru = []
        wu = []
        for v in r:
            ru.extend(v.units if isinstance(v, V) else (v,))
        for v in w:
            wu.extend(v.units if isinstance(v, V) else (v,))
        pr = [u for u in ru if u in self.psum_units]
        if pr:
            ru = [u for u in ru if u not in self.psum_units]
            wu = wu + pr
        for u in ru:
            st = self.units.get(u)
            if st is not None and st[0] is not None:
                deps.add(st[0])
        for u in wu:
            st = self.units.get(u)
            if st is not None:
                if st[0] is not None:
                    deps.add(st[0])
                deps.update(st[1])
        opid = len(self.ops)
        pend = self.pending.pop(eng, None)
        if pend:
            deps.update(pend)
        if dma:
            q = self.last_dma[eng]
            if len(q) >= NDMA_SEMS:
                deps.add(q[-NDMA_SEMS])
            q.append(opid)
        if eng == "pe" and not dma:
            deps = {d for d in deps if not (self.ops[d].eng == "pe" and not self.ops[d].dma)}
        op = Op(eng, fn, deps, dma)
        self.ops.append(op)
        for d in deps:
            self.ops[d].has_dep = True
        for u in ru:
            st = self.units.get(u)
            if st is None:
                st = self.units[u] = [None, []]
            st[1].append(opid)
        for u in wu:
            self.units[u] = [opid, []]
        return opid

    def mm(self, out, lhsT, rhs, start=True, stop=True):
        self.add("pe", lambda e: e.matmul(out.ap, lhsT.ap, rhs.ap, start=start, stop=stop),
                 r=(lhsT, rhs), w=(out,))

    def tr(self, out, in_, ident):
        self.add("pe", lambda e: e.transpose(out.ap, in_.ap, ident.ap), r=(in_, ident), w=(out,))

    def act(self, out, in_, func, bias=None, scale=None, accum=None, eng="act"):
        kw = {}
        r = [in_]
        w = [out]
        if bias is not None:
            if isinstance(bias, V):
                kw["bias"] = bias.ap
                r.append(bias)
            else:
                kw["bias"] = bias
        if scale is not None:
            if isinstance(scale, V):
                kw["scale"] = scale.ap
                r.append(scale)
            else:
                kw["scale"] = scale
        if accum is not None:
            kw["accum_out"] = accum.ap
            w.append(accum)
        self.add("act", lambda e: e.activation(out.ap, in_.ap, func, **kw), r=r, w=w)

    def tt(self, out, a, b, op, eng="dve"):
        self.add(eng, lambda e: e.tensor_tensor(out.ap, a.ap, b.ap, op), r=(a, b), w=(out,))

    def ts(self, out, a, s1, s2, op0, op1=None, eng="dve", accum=None):
        r = [a]
        w = [out]
        a1 = s1
        a2 = s2
        if isinstance(s1, V):
            r.append(s1)
            a1 = s1.ap
        if isinstance(s2, V):
            r.append(s2)
            a2 = s2.ap
        kw = {}
        if a2 is None:
            a2 = 0.0
            op1 = ALU.add
        if op1 is not None:
            kw["op1"] = op1
        if accum is not None:
            kw["accum_out"] = accum.ap
            w.append(accum)
        self.add(eng, lambda e: e.tensor_scalar(out.ap, a.ap, a1, a2, op0, **kw), r=r, w=w)

    def stt(self, out, a, s, b, op0, op1, eng="dve"):
        r = [a, b]
        sa = s
        if isinstance(s, V):
            r.append(s)
            sa = s.ap
        self.add(eng, lambda e: e.scalar_tensor_tensor(out.ap, a.ap, sa, b.ap, op0, op1), r=r, w=(out,))

    def scan(self, out, d0, d1, init, op0=ALU.mult, op1=ALU.add):
        r = [d0, d1]
        ia = init
        if isinstance(init, V):
            r.append(init)
            ia = init.ap
        self.add("dve", lambda e: e.tensor_tensor_scan(out.ap, d0.ap, d1.ap, ia, op0, op1), r=r, w=(out,))

    def copy(self, out, in_, eng="dve"):
        if eng == "act":
            self.add("act", lambda e: e.copy(out.ap, in_.ap), r=(in_,), w=(out,))
        else:
            self.add(eng, lambda e: e.tensor_copy(out.ap, in_.ap), r=(in_,), w=(out,))

    def memset(self, out, val, eng="dve"):
        self.add(eng, lambda e: e.memset(out.ap, val), w=(out,))

    def recip(self, out, in_):
        self.add("dve", lambda e: e.reciprocal(out.ap, in_.ap), r=(in_,), w=(out,))

    def dma(self, out, in_, q="sp"):
        self.add(q, lambda e: e.dma_start(out=out.ap, in_=in_.ap), r=(in_,), w=(out,), dma=True)

    def emit(self, final_wait_units=()):
        nc = self.nc
        ops = self.ops
        engs = ("pe", "act", "dve", "pool", "sp")
        final_deps = set()
        for u in final_wait_units:
            st = self.units.get(u)
            if st is not None and st[0] is not None:
                final_deps.add(st[0])
        for d in final_deps:
            ops[d].has_dep = True
        n_sig = {e: 0 for e in engs}
        n_dma = {"sp": 0, "act": 0, "pool": 0}
        for op in ops:
            if op.dma:
                k = n_dma[op.eng]
                n_dma[op.eng] += 1
                op.sig = ("dma", op.eng, k % NDMA_SEMS, 16 * (k // NDMA_SEMS + 1))
            elif op.has_dep:
                k = n_sig[op.eng]
                n_sig[op.eng] += 1
                op.sig = ("cmp", op.eng, k // SEM_CHUNK, k % SEM_CHUNK + 1)
        sems = {}
        for e in engs:
            for c in range((n_sig[e] + SEM_CHUNK - 1) // SEM_CHUNK):
                sems[("cmp", e, c)] = nc.alloc_semaphore(name=f"s_{e}_{c}")
        for q, n in n_dma.items():
            for c in range(min(n, NDMA_SEMS)):
                sems[("dma", q, c)] = nc.alloc_semaphore(name=f"d_{q}_{c}")
        self.n_sems = len(sems)
        by_eng = {e: [] for e in engs}
        waited = {e: {} for e in engs}
        for op in ops:
            ws = {}
            for d in op.deps:
                s = ops[d].sig
                key = s[:3]
                if ws.get(key, 0) < s[3]:
                    ws[key] = s[3]
            wl = []
            wd = waited[op.eng]
            for key, val in ws.items():
                if wd.get(key, 0) < val:
                    wd[key] = val
                    wl.append((key, val))
            op.waits = wl
            by_eng[op.eng].append(op)
        fin = []
        fw = {}
        for d in final_deps:
            s = ops[d].sig
            if fw.get(s[:3], 0) < s[3]:
                fw[s[:3]] = s[3]
        fin = list(fw.items())

        def run(engine, name):
            for op in by_eng[name]:
                for key, val in op.waits:
                    engine.wait_ge(sems[key], val)
                ins = op.fn(engine)
                if op.sig is not None:
                    ins.then_inc(sems[op.sig[:3]], 16 if op.dma else 1)
            if name == "sp":
                for key, val in fin:
                    engine.wait_ge(sems[key], val)

        with nc.Block() as block:
            @block.tensor
            def _(e):
                run(e, "pe")

            @block.scalar
            def _(e):
                run(e, "act")

            @block.vector
            def _(e):
                run(e, "dve")

            @block.gpsimd
            def _(e):
                run(e, "pool")

            @block.sync
            def _(e):
                run(e, "sp")
        return {e: len(by_eng[e]) for e in engs}


import math
import ml_dtypes
from concourse.ap import AP
from concourse.bass_utils import run_bass_kernel_spmd

NB = 2
S = 4096
CTX = 256
T = S + CTX
D = 1024
TT = 256
NT = T // TT
DEPTH = 4
EPS = 1e-6
I32 = mybir.dt.int32
BF = ml_dtypes.bfloat16


def sp_layout():
    lay = {}
    off = 0

    def reg(name, n):
        nonlocal off
        lay[name] = (off, n)
        off += n
    reg("mod_b", 96)
    reg("norm_pre", 32)
    reg("norm_post", 32)
    for e in range(2):
        reg(f"lru_cw{e}", 32)
        reg(f"lru_cb{e}", 8)
        reg(f"lru_ba{e}", 16)
        reg(f"lru_bx{e}", 16)
        reg(f"lru_lam{e}", 16)
        reg(f"sink{e}", 8)
    for o in range(2):
        reg(f"hy_cw{o}", 72)
        reg(f"hy_cb{o}", 24)
        reg(f"hy_b1{o}", 1)
        reg(f"hy_freq{o}", 1)
        reg(f"hy_b2{o}", 1)
        reg(f"ret_logit{o}", 16)
    return lay, off


def chunked(v):
    v = np.asarray(v, np.float32)
    lead = v.shape[:-1]
    n = v.shape[-1] // 128
    a = v.reshape(lead + (n, 128))
    a = np.moveaxis(a, -1, 0)
    return a.reshape(128, -1)


def host_sp(inp):
    lay, n = sp_layout()
    sp = np.zeros((128, n), np.float32)

    def put(name, arr):
        o, c = lay[name]
        assert arr.shape == (128, c), (name, arr.shape, c)
        sp[:, o:o + c] = arr
    put("mod_b", chunked(inp["mod_b"].reshape(4, 3, 1024)))
    put("norm_pre", chunked(inp["norm_pre"]))
    put("norm_post", chunked(inp["norm_post"]))
    for e in range(2):
        put(f"lru_cw{e}", chunked(inp["lru_conv_w"][e]))
        put(f"lru_cb{e}", chunked(inp["lru_conv_b"][e]))
        put(f"lru_ba{e}", chunked(inp["lru_ba"][e]))
        put(f"lru_bx{e}", chunked(inp["lru_bx"][e]))
        put(f"lru_lam{e}", chunked(inp["lru_lambda"][e]))
        put(f"sink{e}", np.broadcast_to(inp["attn_sink"][e][None, :], (128, 8)))
    for o in range(2):
        put(f"hy_cw{o}", chunked(inp["hy_conv_w"][o]))
        put(f"hy_cb{o}", chunked(inp["hy_conv_b"][o]))
        for nm, key in (("hy_b1", "hy_b1"), ("hy_freq", "hy_freq"), ("hy_b2", "hy_b2")):
            col = np.zeros((128, 1), np.float32)
            col[:64, 0] = inp[key][o]
            put(f"{nm}{o}", col)
        put(f"ret_logit{o}", np.broadcast_to(inp["ret_decay_logit"][o].reshape(1, 16), (128, 16)))
    return sp


def host_consts():
    c = {}
    c["ident_f"] = np.eye(128, dtype=np.float32)
    half = 64
    inv = (10000.0 ** (-np.arange(0, half, 2, dtype=np.float32) / half)).astype(np.float32)
    tok = np.arange(S)
    row = (tok // 64).astype(np.float32)
    col = (tok % 64).astype(np.float32)
    ang_r = row[:, None] * inv[None]
    ang_c = col[:, None] * inv[None]
    cosT = np.ones((128, T), np.float32)
    sinT = np.zeros((128, T), np.float32)
    for d in range(128):
        ang = ang_r if d < 64 else ang_c
        j = d % 32
        first = (d % 64) < 32
        cosT[d, CTX:] = np.cos(ang[:, j])
        sinT[d, CTX:] = (-1.0 if first else 1.0) * np.sin(ang[:, j])
    c["ropec"] = cosT
    c["ropes"] = sinT
    pm = np.zeros((128, 128), np.float32)
    for d in range(128):
        partner = d + 32 if (d % 64) < 32 else d - 32
        pm[partner, d] = 1.0
    c["perm"] = pm
    jj = np.arange(128)[:, None]
    ii = np.arange(128)[None, :]
    mprev = (jj >= ii).astype(np.float32)
    mnext = (jj <= ii).astype(np.float32)
    c["mprev"] = np.tile(mprev[:, None, :], (1, 4, 1)).reshape(128, 512)
    c["mnext"] = np.tile(mnext[:, None, :], (1, 4, 1)).reshape(128, 512)
    c.update(host_consts_odd())
    c.update(host_consts_hy())
    c.update(host_consts_fft())
    return c


class Ctx:
    pass


def build_program(n_layers=DEPTH, debug=False, stop=99):
    nc = bass.Bass("TRN2", target_bir_lowering=False)
    k = Builder(nc)
    g = Ctx()
    g.k = k
    lay, nsp = sp_layout()
    g.lay = lay
    EI = "ExternalInput"
    g.xin = k.dram("xin", [NB, D, T], F32, kind=EI)
    g.cT = k.dram("cT", [128, 8, 3], F32, kind=EI)
    g.spd = k.dram("sp", [128, nsp], F32, kind=EI)
    g.mod_w = k.dram("mod_w", [4, D, 3 * D], F32, kind=EI)
    g.ev_w_in = k.dram("ev_w_in", [2, D, 4608], F32, kind=EI)
    g.ev_w_out = k.dram("ev_w_out", [2, 2048, D], F32, kind=EI)
    g.lru_wa = k.dram("lru_wa", [2, 2, 8, 128, 128], F32, kind=EI)
    g.lru_wx = k.dram("lru_wx", [2, 2, 8, 128, 128], F32, kind=EI)
    g.cst = {}
    for nm, shp in (("ident_f", [128, 128]), ("ropec", [128, T]), ("ropes", [128, T]), ("perm", [128, 128]),
                    ("mprev", [128, 512]), ("mnext", [128, 512])):
        g.cst[nm] = k.dram(nm, shp, F32, kind=EI)
    g.yout = k.dram("yout", [NB, D, T], F32, kind="ExternalOutput")
    g.XR = k.dram("XR", [NB, D, T], F32)
    g.XA = k.dram("XA", [NB, D, T], F32)
    g.GA = k.dram("GA", [NB, D, T], BF16)
    g.Q = k.dram("Q", [NB, D, T], BF16)
    g.K = k.dram("K", [NB, 256, T], BF16)
    g.VT = k.dram("VT", [NB, T, 256], BF16)
    g.GB = k.dram("GB", [NB, D, T], BF16)
    g.MIXT = k.dram("MIXT", [NB, 2048, T], BF16)
    g.od_w_in = k.dram("od_w_in", [2, D, 8192], F32, kind=EI)
    g.od_w_out = k.dram("od_w_out", [2, 2048, D], F32, kind=EI)
    g.hy_w1 = k.dram("hy_w1", [2, 33, 64], F32, kind=EI)
    g.hy_w2 = k.dram("hy_w2", [2, 64, 64], F32, kind=EI)
    g.hy_w3 = k.dram("hy_w3", [2, 64, 4096], F32, kind=EI)
    g.hy_bias = k.dram("hy_bias", [2, 2, 1024], F32, kind=EI)
    for nm, shp, dt_ in (("retc", [128, 770], F32), ("zemb4096", [33, 4096], F32), ("zemb256", [33, 256], F32),
                         ("tneg4096", [128, 32], F32), ("tneg256", [128, 2], F32), ("dabs", [128, 1024], F32),
                         ("altc", [128, 1], F32), ("altrow", [1, 256], F32),
                         ("cm4096", [4096, 4096], BF16), ("sm4096", [4096, 4096], BF16),
                         ("cm256", [256, 256], BF16), ("sm256", [256, 256], BF16)):
        g.cst[nm] = k.dram(nm, shp, dt_, kind=EI)
    for nm, shp, dt_ in (("f1tab", [32, 128], BF16), ("t2tab", [128, 64, 256], BF16), ("t3tab", [128, 64, 3, 128], BF16),
                         ("gtab", [128, 32], BF16), ("pm1", [1, 64], F32)):
        g.cst[nm] = k.dram(nm, shp, dt_, kind=EI)
    g.Bd = [k.dram(f"Bd{i}", [128, 128, 512], BF16) for i in range(4)]
    g.Dd = [k.dram(f"Dd{i}", [128, 128, 512], BF16) for i in range(2)]
    g.KR2 = k.dram("KR2", [2, 64, 64, D], BF16)
    g.KS2 = k.dram("KS2", [2, 64, 64, D], BF16)
    g.KN2 = k.dram("KN2", [2, 1, D], F32)
    g.Z = k.dram("Z", [NB, 3072, T], F32)
    g.ZC = k.dram("ZC", [NB, 2048, T], F32)
    g.GH = k.dram("GH", [NB, D, T], BF16)
    g.RQ = k.dram("RQ", [NB, D, T], BF16)
    g.RK = k.dram("RK", [NB, D, T], BF16)
    g.RKT = k.dram("RKT", [NB, T, D], BF16)
    g.RVT = k.dram("RVT", [NB, T, D], BF16)
    g.GD = k.dram("GD", [NB, D, T], BF16)
    g.U1T = k.dram("U1T", [NB, T, D], BF16)
    g.U2T = k.dram("U2T", [NB, T, D], BF16)
    g.HS = {L: k.dram(f"HS{L}", [2, L, D], BF16) for L in (S, CTX)}
    g.HD = {L: k.dram(f"HD{L}", [2, L, D], BF16) for L in (S, CTX)}
    g.KR = {L: k.dram(f"KR{L}", [2, L, D], F32) for L in (S, CTX)}
    g.KI = {L: k.dram(f"KI{L}", [2, L, D], F32) for L in (S, CTX)}
    g.KN = {L: k.dram(f"KN{L}", [2, 1, D], F32) for L in (S, CTX)}

    k.init_arena(200 * 1024)
    g.P = [k.ps(f"P{i}", [128, 512], F32) for i in range(8)]
    g.sp = k.sbt("sp", [128, nsp])
    k.dma(g.sp, g.spd)
    g.ident_f = k.sbt("ident_f", [128, 128])
    k.dma(g.ident_f, g.cst["ident_f"], q="act")
    g.ident_b = k.sbt("ident_b", [128, 128], BF16)
    k.copy(g.ident_b, g.ident_f)
    g.ones_b = k.sbt("ones_b", [128, 128], BF16)
    k.memset(g.ones_b, 1.0)
    g.modA = k.sbt("modA", [128, 4, 8, 3])
    g.modB = k.sbt("modB", [128, 4, 8, 3])
    g.modG = k.sbt("modG", [128, 4, 8, 3])
    g.ynq = k.sbt("ynq_g", [1, 2, 512], BF16)
    k.freeze_globals()

    phase_mod(g)
    for l in range(n_layers):
        last = (l == n_layers - 1)
        if l % 2 == 0:
            e = l // 2
            if stop >= 1:
                phase_proj(g, l, g.ev_w_in[e], 4608, even_specs(g))
            if stop >= 2:
                phase_lru(g, l, e)
            if stop >= 3:
                phase_attn(g, l, e)
            if stop >= 4:
                phase_out(g, l, g.ev_w_out[e], last)
        else:
            o = l // 2
            if stop >= 1:
                phase_proj(g, l, g.od_w_in[o][:, 0:4096], 4096, odd_specs1(g))
                phase_proj(g, l, g.od_w_in[o][:, 4096:8192], 4096, odd_specs2(g))
            if stop >= 2:
                phase_ret(g, l, o)
            if stop >= 3:
                phase_hyfilt(g, o, S)
                if not last:
                    phase_hyfilt(g, o, CTX)
                    phase_hyspec(g, o, CTX)
                phase_hyspec2(g, o)
                phase_hyprep(g, o)
            if stop >= 4:
                for b in range(NB):
                    for o2 in range(2):
                        if not last:
                            phase_hyconv(g, l, o, b, 0, o2)
                        phase_hyconv2(g, l, o, b, o2)
            if stop >= 5:
                phase_out(g, l, g.od_w_out[o], last)
    stats = k.emit(final_wait_units=("yout",))
    print("ops per engine", stats, "sems", k.n_sems, flush=True)
    return nc


def spcol(g, name, i=0, n=1):
    o, c = g.lay[name]
    return g.sp[:, o + i:o + i + n]


def phase_mod(g):
    k = g.k
    k.phase()
    c_sb = k.sbt("c_sb", [128, 8, 3])
    k.dma(c_sb, g.cT)
    sc = k.sbt("sc", [128, 8, 3])
    k.act(sc, c_sb, AF.Silu)
    raw = k.sbt("modraw", [128, 96, 3])
    wm = [k.sbt(f"wm{i}", [128, 8, 1024]) for i in range(2)]
    it = 0
    for l in range(DEPTH):
        for part in range(3):
            w = wm[it % 2]
            it += 1
            src = g.mod_w[l][:, part * 1024:(part + 1) * 1024].re("(dc p) f -> p dc f", p=128)
            k.dma(w, src, q="sp" if it % 2 else "act")
            ps = g.P[it % 2]
            for fc in range(8):
                for dc in range(8):
                    k.mm(ps[:, fc * 4:fc * 4 + 3], w[:, dc, fc * 128:(fc + 1) * 128], sc[:, dc, :],
                         start=(dc == 0), stop=(dc == 7))
            for fc in range(8):
                k.ts(raw[:, (l * 3 + part) * 8 + fc, :], ps[:, fc * 4:fc * 4 + 3],
                     spcol(g, "mod_b", (l * 3 + part) * 8 + fc), None, ALU.add)
    for l in range(DEPTH):
        for dc in range(8):
            k.ts(g.modA[:, l, dc, :], raw[:, (l * 3 + 1) * 8 + dc, :], 1.0, spcol(g, "norm_pre", l * 8 + dc), ALU.add, ALU.mult)
            k.copy(g.modB[:, l, dc, :], raw[:, (l * 3 + 0) * 8 + dc, :])
            k.ts(g.modG[:, l, dc, :], raw[:, (l * 3 + 2) * 8 + dc, :], spcol(g, "norm_post", l * 8 + dc), None, ALU.mult)


def even_specs(g):
    return [
        (0, 1024, "raw32", g.XA),
        (1024, 1024, "silu16", g.GA),
        (2048, 1024, "rope16", g.Q),
        (3072, 256, "rope16", g.K),
        (3328, 256, "tok16", g.VT),
        (3584, 1024, "silu16", g.GB),
    ]


def load_weight_bf16(g, name, w_dram, ncols, nchunks):
    k = g.k
    wb = []
    for dc in range(nchunks):
        t = k.sbt(f"{name}{dc}", [128, ncols], BF16)
        step = 2304 if ncols % 2304 == 0 else ncols
        for c0 in range(0, ncols, step):
            k.dma(t[:, c0:c0 + step], w_dram[dc * 128:(dc + 1) * 128, c0:c0 + step], q="pool")
        wb.append(t)
    return wb


def norm_load(g, b, tt, xsrc, xs):
    k = g.k
    t0 = tt * TT
    k.dma(xs, xsrc[b][:, t0:t0 + TT].re("(dc p) t -> p dc t", p=128), q="pool")


def norm_compute(g, l, b, tt, xs, hT, scr, msbank):
    k = g.k
    j = 2 if tt == 0 else b
    sq = scr["sq"]
    k.act(sq, xs, AF.Square)
    ms = msbank[:, 0:TT]
    for dc in range(8):
        k.mm(ms, g.ones_b, sq[:, dc, :], start=(dc == 0), stop=(dc == 7))
    rstd = scr["rstd"]
    k.act(rstd, ms, AF.Sqrt, bias=g.epsc, scale=1.0 / D)
    k.recip(rstd, rstd)
    tmp = scr["tmp"]
    for dc in range(8):
        k.tt(tmp[:, dc, :], xs[:, dc, :], rstd, ALU.mult)
        k.ts(hT[:, dc, :], tmp[:, dc, :], g.modA[:, l, dc, j:j + 1], g.modB[:, l, dc, j:j + 1],
             ALU.mult, ALU.add)


def phase_proj(g, l, w_dram, F, specs):
    k = g.k
    k.phase()
    wb = load_weight_bf16(g, "wb", w_dram, F, 8)
    xsrc = g.xin if l == 0 else g.XR
    g.epsc = k.sbt("epsc", [128, 1])
    k.memset(g.epsc, EPS)
    need_rope = any(s[2] == "rope16" for s in specs)
    if need_rope:
        ropec = k.sbt("ropec", [128, T])
        ropes = k.sbt("ropes", [128, T])
        k.dma(ropec, g.cst["ropec"], q="act")
        k.dma(ropes, g.cst["ropes"], q="act")
        permf = k.sbt("permf", [128, 128])
        k.dma(permf, g.cst["perm"], q="act")
        perm = k.sbt("perm", [128, 128], BF16)
        k.copy(perm, permf)
    xs = [k.sbt(f"xs{i}", [128, 8, TT]) for i in range(3)]
    hT = [k.sbt(f"hT{i}", [128, 8, TT], BF16) for i in range(2)]
    scr = {"sq": k.sbt("sq", [128, 8, TT], BF16), "rstd": k.sbt("rstd", [128, TT]),
           "tmp": k.sbt("tmp", [128, 8, TT])}
    st32 = [k.sbt(f"st32_{i}", [128, 8, TT]) for i in range(2)]
    st16 = [k.sbt(f"st16_{i}", [128, 8, TT], BF16) for i in range(2)]
    stt16 = [k.sbt(f"stt16_{i}", [128, 512], BF16) for i in range(2)]
    qb = [k.sbt(f"qb{i}", [128, TT], BF16) for i in range(2)]
    t1 = [k.sbt(f"rt1_{i}", [128, TT]) for i in range(2)]
    t2 = [k.sbt(f"rt2_{i}", [128, TT]) for i in range(2)]
    cnt = {"ps": 0, "s32": 0, "s16": 0, "t16": 0, "q": 0, "dq": 0}
    tiles = [(b_, t_) for b_ in range(NB) for t_ in range(NT)]
    norm_load(g, tiles[0][0], tiles[0][1], xsrc, xs[0])
    norm_load(g, tiles[1][0], tiles[1][1], xsrc, xs[1])
    norm_compute(g, l, tiles[0][0], tiles[0][1], xs[0], hT[0], scr, g.P[7])
    for it, (b, tt) in enumerate(tiles):
        if True:
            t0 = tt * TT
            h_ = hT[it % 2]
            if it + 2 < len(tiles):
                norm_load(g, tiles[it + 2][0], tiles[it + 2][1], xsrc, xs[(it + 2) % 3])
            if it + 1 < len(tiles):
                norm_compute(g, l, tiles[it + 1][0], tiles[it + 1][1], xs[(it + 1) % 3], hT[(it + 1) % 2], scr, g.P[7])
            for (c0, ncols, kind, dest) in specs:
                if kind == "tok16":
                    for sb in range(TT // 128):
                        for cc in range(0, ncols, 512):
                            cw = min(512, ncols - cc)
                            ps = g.P[6][:, 0:cw]
                            for dc in range(8):
                                k.mm(ps, h_[:, dc, sb * 128:(sb + 1) * 128], wb[dc][:, c0 + cc:c0 + cc + cw],
                                     start=(dc == 0), stop=(dc == 7))
                            st = stt16[cnt["t16"] % 2][:, 0:cw]
                            cnt["t16"] += 1
                            scale = dest_scale(kind, dest, g)
                            k.act(st, ps, AF.Copy, scale=scale)
                            dstt = dest[0] if isinstance(dest, tuple) else dest
                            r0t = c0 - spec_base(specs, dstt)
                            k.dma(dstt[b][t0 + sb * 128:t0 + (sb + 1) * 128, r0t + cc:r0t + cc + cw], st,
                                  q="sp")
                    continue
                dst = dest[0] if isinstance(dest, tuple) else dest
                for gc in range(0, ncols // 128, 8):
                    ng = min(8, ncols // 128 - gc)
                    if kind == "raw32":
                        stg = st32[cnt["s32"] % 2]
                        cnt["s32"] += 1
                    else:
                        stg = st16[cnt["s16"] % 2]
                        cnt["s16"] += 1
                    for ci in range(ng):
                        f0 = c0 + (gc + ci) * 128
                        ps = g.P[cnt["ps"] % 4][:, 0:TT]
                        cnt["ps"] += 1
                        for dc in range(8):
                            k.mm(ps, wb[dc][:, f0:f0 + 128], h_[:, dc, :], start=(dc == 0), stop=(dc == 7))
                        if kind == "raw32":
                            k.copy(stg[:, ci, :], ps, eng="dve")
                        elif kind == "silu16":
                            k.act(stg[:, ci, :], ps, AF.Silu)
                        elif kind == "copy16":
                            k.act(stg[:, ci, :], ps, AF.Copy, scale=dest_scale(kind, dest, g))
                        elif kind == "rope16":
                            q_ = qb[cnt["q"] % 2]
                            a_ = t1[cnt["q"] % 2]
                            b_ = t2[cnt["q"] % 2]
                            pp = g.P[4 + cnt["q"] % 2][:, 0:TT]
                            cnt["q"] += 1
                            k.act(q_, ps, AF.Copy)
                            k.mm(pp, perm, q_)
                            k.tt(a_, ps, ropec[:, t0:t0 + TT], ALU.mult)
                            k.tt(b_, pp, ropes[:, t0:t0 + TT], ALU.mult)
                            k.tt(stg[:, ci, :], a_, b_, ALU.add)
                    r0 = (c0 - spec_base(specs, dst)) + gc * 128
                    k.dma(dst[b][r0:r0 + ng * 128, t0:t0 + TT].re("(c p) t -> p c t", p=128),
                          stg[:, 0:ng, :], q="act" if cnt["dq"] % 2 else "sp")
                    cnt["dq"] += 1


def spec_base(specs, dst):
    for (c0, ncols, kind, dest) in specs:
        d = dest[0] if isinstance(dest, tuple) else dest
        if d is dst:
            return c0
    raise KeyError


def dest_scale(kind, dest, g):
    if isinstance(dest, tuple):
        return dest[1]
    return 1.0


def rev(v, lo, hi):
    a = v.ap[:, lo:hi]
    pat = [list(p) for p in a.ap]
    off = a.offset + (hi - lo - 1) * pat[-1][0]
    pat[-1][0] = -pat[-1][0]
    return V(AP(a.tensor, off, pat), v.units)


def dwconv(k, out, x, wcols, bcol, taps, left, segs, eng="dve"):
    k.ts(out, x, wcols[left], bcol, ALU.mult, ALU.add, eng=eng)
    for kk in range(taps):
        o = kk - left
        if o == 0:
            continue
        for (lo, hi) in segs:
            a = lo + max(0, -o)
            bnd = hi - max(0, o)
            k.stt(out[:, a:bnd], x[:, a + o:bnd + o], wcols[kk], out[:, a:bnd], ALU.mult, ALU.add, eng=eng)


def phase_lru(g, l, e):
    k = g.k
    k.phase()
    segs = [(0, CTX), (CTX, T)]
    lam = spcol(g, f"lru_lam{e}", 0, 16)
    cA = k.sbt("cA", [128, 16])
    cA2 = k.sbt("cA2", [128, 16])
    k.act(cA, lam, AF.Exp, scale=-1.0)
    k.act(cA, cA, AF.Ln, bias=1.0)
    k.ts(cA2, cA, -16.0, None, ALU.mult)
    k.ts(cA, cA, -8.0, None, ALU.mult)
    one_c = k.sbt("one_c", [128, 1])
    k.memset(one_c, 1.0)
    wa = k.sbt("wa", [128, 2, 8, 128], BF16)
    wx = k.sbt("wx", [128, 2, 8, 128], BF16)
    for d in range(2):
        k.dma(wa[:, d], g.lru_wa[e][d].re("n c d -> c n d"), q="pool")
        k.dma(wx[:, d], g.lru_wx[e][d].re("n c d -> c n d"), q="pool")
    xa = [k.sbt(f"xa{i}", [128, T]) for i in range(2)]
    ga = [k.sbt(f"ga{i}", [128, T], BF16) for i in range(2)]
    u = k.sbt("u", [128, T])
    ub = k.sbt("ub", [128, T], BF16)
    r_ = k.sbt("r", [128, T])
    i_ = k.sbt("i", [128, T])
    s_ = k.sbt("s", [128, T])
    hh = [k.sbt(f"h{d}", [128, T]) for d in range(2)]
    yo = [k.sbt(f"yo{i}", [128, T], BF16) for i in range(2)]
    pc = 0
    items = [(b_, n_) for b_ in range(NB) for n_ in range(8)]

    def lru_load(i):
        b_, n_ = items[i]
        k.dma(xa[i % 2], g.XA[b_][n_ * 128:(n_ + 1) * 128, :], q="sp")
        k.dma(ga[i % 2], g.GA[b_][n_ * 128:(n_ + 1) * 128, :], q="act")
    lru_load(0)
    for it, (b, n) in enumerate(items):
        if True:
            xa_ = xa[it % 2]
            ga_ = ga[it % 2]
            yo_ = yo[it % 2]
            if it + 1 < len(items):
                lru_load(it + 1)
            wc = [spcol(g, f"lru_cw{e}", kk * 8 + n) for kk in range(4)]
            dwconv(k, u, xa_, wc, spcol(g, f"lru_cb{e}", n), 4, 2, segs)
            k.copy(ub, u, eng="pool")
            for d in range(2):
                for c0 in range(0, T, 512):
                    cw = min(512, T - c0)
                    pr = g.P[pc % 8][:, 0:cw]
                    pi = g.P[(pc + 1) % 8][:, 0:cw]
                    pc += 2
                    k.mm(pr, wa[:, d, n, :], ub[:, c0:c0 + cw])
                    k.mm(pi, wx[:, d, n, :], ub[:, c0:c0 + cw])
                    k.act(r_[:, c0:c0 + cw], pr, AF.Sigmoid, bias=spcol(g, f"lru_ba{e}", d * 8 + n))
                    k.act(i_[:, c0:c0 + cw], pi, AF.Sigmoid, bias=spcol(g, f"lru_bx{e}", d * 8 + n))
                k.act(s_, r_, AF.Exp, scale=cA2[:, d * 8 + n:d * 8 + n + 1])
                k.act(r_, r_, AF.Exp, scale=cA[:, d * 8 + n:d * 8 + n + 1])
                k.ts(s_, s_, 1.0, None, ALU.min)
                k.act(s_, s_, AF.Sqrt, bias=one_c, scale=-1.0)
                k.tt(i_, i_, u, ALU.mult, eng="pool")
                k.tt(i_, i_, s_, ALU.mult)
                h_ = hh[d]
                if d == 0:
                    k.scan(h_, r_, i_, 0.0)
                else:
                    k.scan(rev(h_, 0, CTX), rev(r_, 0, CTX), rev(i_, 0, CTX), 0.0)
                    k.scan(rev(h_, CTX, T), rev(r_, CTX, T), rev(i_, CTX, T), h_[:, 0:1])
            k.tt(hh[0], hh[0], hh[1], ALU.add, eng="pool")
            k.tt(yo_, hh[0], ga_, ALU.mult, eng="pool")
            k.dma(g.MIXT[b][n * 128:(n + 1) * 128, :], yo_, q="pool")


def phase_attn(g, l, e):
    k = g.k
    k.phase()
    scale = 128 ** -0.5
    mprev_f = k.sbt("mprev_f", [128, 512])
    mnext_f = k.sbt("mnext_f", [128, 512])
    k.dma(mprev_f, g.cst["mprev"], q="act")
    k.dma(mnext_f, g.cst["mnext"], q="act")
    mprev = k.sbt("mprev", [128, 512], BF16)
    mnext = k.sbt("mnext", [128, 512], BF16)
    k.copy(mprev, mprev_f)
    k.copy(mnext, mnext_f)
    sinkexp = k.sbt("sinkexp", [128, 8])
    k.act(sinkexp, spcol(g, f"sink{e}", 0, 8), AF.Exp)
    onesf = k.sbt("onesf", [128, 128])
    k.memset(onesf, 1.0)
    sinkrow = k.sbt("sinkrow", [128, 1024])
    for h_ in range(8):
        k.ts(sinkrow[:, h_ * 128:(h_ + 1) * 128], onesf, sinkexp[:, h_:h_ + 1], None, ALU.mult)
    NBLK = T // 128
    Kt = [k.sbt(f"Kt{i}", [128, 2, T], BF16) for i in range(2)]
    Vt = [k.sbt(f"Vt{i}", [128, NBLK, 256], BF16) for i in range(2)]
    Qt = [k.sbt(f"Qt{i}", [128, 1024], BF16) for i in range(2)]
    Gt = [k.sbt(f"Gt{i}", [128, 1024], BF16) for i in range(2)]
    pT = [k.sbt(f"pT{i}", [128, 512], BF16) for i in range(8)]
    den = [k.sbt(f"den{i}", [128, 512]) for i in range(2)]
    ost = [k.sbt(f"ost{i}", [128, 1024], BF16) for i in range(2)]
    otmp = [k.sbt(f"otmp{i}", [128, 512]) for i in range(2)]
    pcnt = 0
    ppc = 0
    items = [(b_, k_) for b_ in range(NB) for k_ in range(NBLK)]

    def attn_load(i):
        b_, k_ = items[i]
        if k_ == 0:
            k.dma(Kt[b_], g.K[b_].re("(h p) t -> p h t", p=128), q="sp")
            k.dma(Vt[b_], g.VT[b_].re("(n p) c -> p n c", p=128), q="act")
        cc = k_ * 128
        k.dma(Qt[i % 2].re("p (h t) -> p h t", h=8), g.Q[b_][:, cc:cc + 128].re("(h p) t -> p h t", p=128), q="sp")
        k.dma(Gt[i % 2].re("p (h t) -> p h t", h=8), g.GB[b_][:, cc:cc + 128].re("(h p) t -> p h t", p=128), q="act")
    attn_load(0)
    for it, (b, blk) in enumerate(items):
        if True:
            Kb = Kt[b]
            Vb = Vt[b]
            q_ = Qt[it % 2]
            g_ = Gt[it % 2]
            o_ = ost[it % 2]
            c0 = blk * 128
            if it + 1 < len(items):
                attn_load(it + 1)
            if blk < 2:
                kbs = [(0, None), (1, None)]
            else:
                kbs = []
                if blk > 2:
                    kbs.append((blk - 1, mprev))
                kbs.append((blk, None))
                if blk < NBLK - 1:
                    kbs.append((blk + 1, mnext))
                kbs += [(0, None), (1, None)]
            for hh in range(2):
                po = g.P[4 + ppc % 2]
                pd = g.P[6 + ppc % 2]
                ppc += 1
                rhs = q_[:, hh * 512:(hh + 1) * 512]
                pts = []
                for (kb, msk) in kbs:
                    pst = g.P[pcnt % 4]
                    p_ = pT[pcnt % 8]
                    pcnt += 1
                    k.mm(pst, Kb[:, hh, kb * 128:(kb + 1) * 128], rhs)
                    k.act(p_, pst, AF.Exp, scale=scale)
                    if msk is not None:
                        k.tt(p_, p_, msk, ALU.mult, eng="pool")
                    pts.append((kb, p_))
                for ii, (kb, p_) in enumerate(pts):
                    first = (ii == 0)
                    lastk = (ii == len(pts) - 1)
                    k.mm(po, Vb[:, kb, hh * 128:(hh + 1) * 128], p_, start=first, stop=lastk)
                for ii, (kb, p_) in enumerate(pts):
                    first = (ii == 0)
                    lastk = (ii == len(pts) - 1)
                    k.mm(pd, g.ones_b, p_, start=first, stop=lastk)
                dn = den[ppc % 2]
                ot = otmp[ppc % 2]
                k.tt(dn, pd, sinkrow[:, hh * 512:(hh + 1) * 512], ALU.add)
                k.recip(dn, dn)
                k.tt(ot, po, dn, ALU.mult)
                k.tt(o_[:, hh * 512:(hh + 1) * 512], ot, g_[:, hh * 512:(hh + 1) * 512], ALU.mult, eng="pool")
            k.dma(g.MIXT[b][1024:2048, c0:c0 + 128].re("(h p) t -> p h t", p=128), o_.re("p (h t) -> p h t", h=8), q="pool")


def phase_out(g, l, w_dram, last):
    k = g.k
    k.phase()
    wo = load_weight_bf16(g, "wo", w_dram, D, 16)
    xsrc = g.xin if l == 0 else g.XR
    epsc = k.sbt("epsc", [128, 1])
    k.memset(epsc, EPS)
    mix = [k.sbt(f"mix{i}", [128, 16, TT], BF16) for i in range(2)]
    xs = [k.sbt(f"xs{i}", [128, 8, TT]) for i in range(2)]
    xn = [k.sbt(f"xn{i}", [128, 8, TT]) for i in range(2)]
    ysb = k.sbt("ysb", [128, 8, TT])
    sq = k.sbt("sq", [128, 8, TT], BF16)
    rstd = k.sbt("rstd", [128, TT])
    tmp = k.sbt("tmp", [128, TT])
    pc = 0
    tiles = [(b_, t_) for b_ in range(NB) for t_ in range(NT)]

    def out_load(i):
        b_, t_ = tiles[i]
        k.dma(mix[i % 2], g.MIXT[b_][:, t_ * TT:(t_ + 1) * TT].re("(c p) t -> p c t", p=128), q="pool")
        k.dma(xs[i % 2], xsrc[b_][:, t_ * TT:(t_ + 1) * TT].re("(c p) t -> p c t", p=128), q="pool")
    out_load(0)
    for it, (b, tt) in enumerate(tiles):
        if True:
            t0 = tt * TT
            j = 2 if tt == 0 else b
            m_ = mix[it % 2]
            x_ = xs[it % 2]
            n_ = xn[it % 2]
            if it + 1 < len(tiles):
                out_load(it + 1)
            ms = g.P[7][:, 0:TT]
            for dmc in range(8):
                ps = g.P[pc % 6][:, 0:TT]
                pc += 1
                for fc in range(16):
                    k.mm(ps, wo[fc][:, dmc * 128:(dmc + 1) * 128], m_[:, fc, :], start=(fc == 0), stop=(fc == 15))
                k.act(sq[:, dmc, :], ps, AF.Square)
                k.copy(ysb[:, dmc, :], ps, eng="dve")
            for dmc in range(8):
                k.mm(ms, g.ones_b, sq[:, dmc, :], start=(dmc == 0), stop=(dmc == 7))
            k.act(rstd, ms, AF.Sqrt, bias=epsc, scale=1.0 / D)
            k.recip(rstd, rstd)
            for dmc in range(8):
                k.tt(tmp, ysb[:, dmc, :], rstd, ALU.mult)
                k.stt(n_[:, dmc, :], tmp, g.modG[:, l, dmc, j:j + 1], x_[:, dmc, :], ALU.mult, ALU.add)
            dst = g.yout if last else g.XR
            k.dma(dst[b][:, t0:t0 + TT].re("(c p) t -> p c t", p=128), n_, q="sp")


def host_inputs(inp, core, consts, sp):
    b0 = core * NB
    xin = np.empty((NB, D, T), np.float32)
    for j in range(NB):
        xin[j, :, :CTX] = inp["ctx"][b0 + j].T
        xin[j, :, CTX:] = inp["x"][b0 + j].T
    cols = np.stack([inp["c"][b0], inp["c"][b0 + 1], inp["c_ctx"]], axis=-1)
    cT = np.ascontiguousarray(cols.reshape(8, 128, 3).transpose(1, 0, 2))
    m = {"xin": xin, "cT": cT, "sp": sp,
         "mod_w": inp["mod_w"], "ev_w_in": inp["ev_w_in"], "ev_w_out": inp["ev_w_out"],
         "lru_wa": inp["lru_wa"], "lru_wx": inp["lru_wx"],
         "od_w_in": inp["od_w_in"], "od_w_out": inp["od_w_out"], "hy_w1": inp["hy_w1"], "hy_w2": inp["hy_w2"],
         "hy_w3": inp["hy_w3"], "hy_bias": inp["hy_bias"]}
    m.update(consts)
    return m


_NC_CACHE = {}


def kernel(**inputs):
    inp = {k_: np.asarray(v, np.float32) for k_, v in inputs.items()}
    n_cores = 8
    if "nc" not in _NC_CACHE:
        _NC_CACHE["nc"] = build_program()
    nc = _NC_CACHE["nc"]
    consts = host_consts()
    sp = host_sp(inp)
    in_maps = [host_inputs(inp, c, consts, sp) for c in range(n_cores)]
    res = run_bass_kernel_spmd(nc, in_maps, core_ids=list(range(n_cores)))
    out = np.empty((16, S, D), np.float32)
    for c in range(n_cores):
        y = res.results[c]["yout"]
        for j in range(NB):
            out[c * NB + j] = y[j][:, CTX:].T
    return out


def odd_specs1(g):
    return [(0, 3072, "raw32", g.Z), (3072, 1024, "silu16", g.GH)]


def odd_specs2(g):
    ks = 128 ** -0.5
    return [(0, 1024, "copy16", g.RQ), (1024, 1024, "copy16", (g.RK, ks)), (1024, 1024, "tok16", (g.RKT, ks)),
            (2048, 1024, "tok16", g.RVT), (3072, 1024, "silu16", g.GD)]


def host_consts_odd():
    c = {}
    jj = np.arange(128, dtype=np.float32)[:, None]
    ii = np.arange(128, dtype=np.float32)[None, :]
    retc = np.zeros((128, 6 * 128 + 2), np.float32)
    retc[:, 0:128] = np.maximum(ii - jj, 0)
    retc[:, 128:256] = (ii >= jj)
    retc[:, 256:384] = np.maximum(jj - ii, 0)
    retc[:, 384:512] = (jj > ii)
    retc[:, 512:640] = ii + 1.0
    retc[:, 640:768] = 128.0 - ii
    retc[:, 768] = 127.0 - jj[:, 0]
    retc[:, 769] = jj[:, 0]
    c["retc"] = retc
    return c


def phase_ret(g, l, o):
    k = g.k
    k.phase()
    NBLK = T // 128
    retc = k.sbt("retc", [128, 770])
    k.dma(retc, g.cst["retc"], q="act")
    lg = k.sbt("lg", [128, 16])
    k.act(lg, spcol(g, f"ret_logit{o}", 0, 16), AF.Sigmoid)
    k.act(lg, lg, AF.Ln)
    inner = k.sbt("inner", [128, 16, 128])
    qdec = k.sbt("qdec", [128, 16, 128])
    kdec = k.sbt("kdec", [128, 16])
    cdec = k.sbt("cdec", [128, 16])
    for d in range(2):
        for h in range(8):
            c = d * 8 + h
            lgc = lg[:, c:c + 1]
            k.act(inner[:, c, :], retc[:, d * 256:d * 256 + 128], AF.Exp, scale=lgc)
            k.tt(inner[:, c, :], inner[:, c, :], retc[:, d * 256 + 128:d * 256 + 256], ALU.mult)
            k.act(qdec[:, c, :], retc[:, 512 + d * 128:640 + d * 128], AF.Exp, scale=lgc)
            k.act(kdec[:, c:c + 1], retc[:, 768 + d:769 + d], AF.Exp, scale=lgc)
    k.act(cdec, lg, AF.Exp, scale=128.0)
    epsc = k.sbt("epsc", [128, 1])
    k.memset(epsc, EPS)
    HP = 2
    qT = [k.sbt(f"qT{i}", [128, T], BF16) for i in range(HP)]
    kT = [k.sbt(f"kT{i}", [128, T], BF16) for i in range(HP)]
    ktok = [k.sbt(f"ktok{i}", [128, NBLK, 128], BF16) for i in range(HP)]
    vtok = [k.sbt(f"vtok{i}", [128, NBLK, 128], BF16) for i in range(HP)]
    gd = [k.sbt(f"gd{i}", [128, T], BF16) for i in range(HP)]
    oacc = [[k.sbt(f"oacc{i}_{d}", [128, T]) for d in range(2)] for i in range(HP)]
    Sf = [k.sbt(f"S{c}", [128, 128]) for c in range(4)]
    Sb = [k.sbt(f"Sb{c}", [128, 128], BF16) for c in range(4)]
    attS = [[k.sbt(f"attS{c}_{i}", [128, 128], BF16) for i in range(2)] for c in range(4)]
    qd = [[k.sbt(f"qd{c}_{i}", [128, 128], BF16) for i in range(2)] for c in range(4)]
    vdec = [[k.sbt(f"vdec{c}_{i}", [128, 128], BF16) for i in range(2)] for c in range(4)]
    sq = k.sbt("rsq", [128, 512], BF16)
    rstd = k.sbt("rrstd", [128, 512])
    yo = qT
    order = [list(range(NBLK)), [1, 0] + list(range(NBLK - 1, 1, -1))]
    for b in range(NB):
        for hp in range(8 // HP):
            for i in range(HP):
                h = hp * HP + i
                k.dma(qT[i], g.RQ[b][h * 128:(h + 1) * 128, :], q="sp")
                k.dma(kT[i], g.RK[b][h * 128:(h + 1) * 128, :], q="act")
                k.dma(ktok[i], g.RKT[b][:, h * 128:(h + 1) * 128].re("(n p) c -> p n c", p=128), q="sp")
                k.dma(vtok[i], g.RVT[b][:, h * 128:(h + 1) * 128].re("(n p) c -> p n c", p=128), q="act")
                k.dma(gd[i], g.GD[b][h * 128:(h + 1) * 128, :], q="sp")
            for s in range(NBLK):
                chains = [(i, d) for i in range(HP) for d in range(2)]

                def cvars(i, d):
                    h = hp * HP + i
                    ch = i * 2 + d
                    c = d * 8 + h
                    blk = order[d][s]
                    cs = slice(blk * 128, (blk + 1) * 128)
                    return h, ch, c, blk, cs
                for (i, d) in chains:
                    h, ch, c, blk, cs = cvars(i, d)
                    att = g.P[ch][:, 0:128]
                    a_ = attS[ch][s % 2]
                    k.mm(att, kT[i][:, cs], qT[i][:, cs])
                    k.tt(a_, att, inner[:, c, :], ALU.mult)
                    if s > 0:
                        k.tt(qd[ch][s % 2], qT[i][:, cs], qdec[:, c, :], ALU.mult, eng="pool")
                    if s < NBLK - 1:
                        k.ts(vdec[ch][s % 2], vtok[i][:, blk, :], kdec[:, c:c + 1], None, ALU.mult, eng="pool")
                if s < NBLK - 1:
                    for (i, d) in chains:
                        h, ch, c, blk, cs = cvars(i, d)
                        k.mm(g.P[4 + ch][:, 0:128], ktok[i][:, blk, :], vdec[ch][s % 2])
                for (i, d) in chains:
                    h, ch, c, blk, cs = cvars(i, d)
                    ops = g.P[ch][:, 128:256]
                    kv = g.P[4 + ch][:, 0:128]
                    k.mm(ops, vtok[i][:, blk, :], attS[ch][s % 2], start=True, stop=(s == 0))
                    if s > 0:
                        k.mm(ops, Sb[ch], qd[ch][s % 2], start=False, stop=True)
                    k.copy(oacc[i][d][:, cs], ops, eng="act")
                    if s < NBLK - 1:
                        if s == 0:
                            k.copy(Sf[ch], kv)
                        else:
                            k.stt(Sf[ch], Sf[ch], cdec[:, c:c + 1], kv, ALU.mult, ALU.add)
                        k.copy(Sb[ch], Sf[ch], eng="act")
            for i in range(HP):
                h = hp * HP + i
                oa = oacc[i][0]
                k.tt(oa, oa, oacc[i][1], ALU.add)
                for c0 in range(0, T, 512):
                    cw = min(512, T - c0)
                    k.act(sq[:, 0:cw], oa[:, c0:c0 + cw], AF.Square)
                    ms = g.P[i][:, 0:cw]
                    k.mm(ms, g.ones_b, sq[:, 0:cw])
                    k.act(rstd[:, 0:cw], ms, AF.Sqrt, bias=epsc, scale=1.0 / 128)
                    k.recip(rstd[:, 0:cw], rstd[:, 0:cw])
                    k.tt(oa[:, c0:c0 + cw], oa[:, c0:c0 + cw], rstd[:, 0:cw], ALU.mult)
                k.tt(yo[i], oa, gd[i], ALU.mult, eng="pool")
                k.dma(g.MIXT[b][1024 + h * 128:1024 + (h + 1) * 128, :], yo[i], q="sp")


def host_consts_hy():
    c = {}
    for L in (4096, 256):
        t = np.linspace(0.0, 1.0, L, dtype=np.float32)[:, None]
        bands = 16
        w = (2.0 * np.float32(math.pi) * np.arange(L, dtype=np.float32)[:, None] / np.float32(L)).astype(np.float32)
        f = np.linspace(1e-4, bands - 1, bands, dtype=np.float32)[None]
        z = np.concatenate([t, np.cos(f * w), -np.sin(f * w)], axis=-1).astype(np.float32)
        c[f"zemb{L}"] = np.ascontiguousarray(z.T)
        c[f"tneg{L}"] = np.ascontiguousarray((-t[:, 0]).reshape(L // 128, 128).T)
        N = 2 * L
        kk = np.arange(L, dtype=np.int64)
        prod = (kk[:, None] * kk[None, :]) % N
        ang = prod.astype(np.float64) * (2.0 * math.pi / N)
        c[f"cm{L}"] = np.cos(ang).astype(np.float32).astype(BF)
        c[f"sm{L}"] = np.sin(ang).astype(np.float32).astype(BF)
    max_decay = math.log(1e-2) / 0.3
    min_decay = math.log(1e-2) / 1.5
    deltas = np.linspace(min_decay, max_decay, 1024, dtype=np.float32)
    c["dabs"] = np.broadcast_to(np.abs(deltas)[None, :], (128, 1024)).astype(np.float32).copy()
    alt = np.where(np.arange(128) % 2 == 0, 1.0, -1.0).astype(np.float32)
    c["altc"] = alt[:, None].copy()
    c["altrow"] = np.where(np.arange(256) % 2 == 0, 1.0, -1.0).astype(np.float32)[None, :].copy()
    return c


def sin_reduce(k, out, arg, tmp_i, tmp_f):
    k.ts(arg, arg, 1.0 / (2 * math.pi), 64.5, ALU.mult, ALU.add)
    k.copy(tmp_i, arg)
    k.copy(tmp_f, tmp_i)
    k.tt(arg, arg, tmp_f, ALU.subtract)
    k.stt(arg, arg, 0.0, arg, ALU.is_lt, ALU.add)
    k.ts(arg, arg, 2 * math.pi, -math.pi, ALU.mult, ALU.add)
    k.ts(arg, arg, -math.pi, math.pi, ALU.max, ALU.min)
    k.act(out, arg, AF.Sin)


def phase_hyfilt(g, o, L):
    k = g.k
    k.phase()
    zemb = k.sbt("zemb", [33, L])
    k.dma(zemb, g.cst[f"zemb{L}"], q="sp")
    w1 = k.sbt("w1", [33, 64])
    k.dma(w1, g.hy_w1[o], q="act")
    w2 = k.sbt("w2", [64, 64])
    k.dma(w2, g.hy_w2[o], q="act")
    w3 = k.sbt("w3", [64, 4096])
    k.dma(w3, g.hy_w3[o], q="sp")
    tneg = k.sbt("tneg", [128, L // 128])
    k.dma(tneg, g.cst[f"tneg{L}"], q="act")
    dabs = k.sbt("dabs", [128, 1024])
    k.dma(dabs, g.cst["dabs"], q="act")
    bias = k.sbt("hbias", [1, 2048])
    k.dma(bias, g.hy_bias[o:o + 1].re("a b c -> a (b c)"), q="act")
    hid1 = k.sbt("hid1", [64, L])
    hid2 = k.sbt("hid2", [64, L])
    arg = k.sbt("arg", [64, 512])
    ti = k.sbt("ti", [64, 512], I32)
    tf = k.sbt("tf", [64, 512])
    b1 = spcol(g, f"hy_b1{o}")[0:64]
    b2 = spcol(g, f"hy_b2{o}")[0:64]
    fr = spcol(g, f"hy_freq{o}")[0:64]
    for (src, wgt, bcol, dst) in ((zemb, w1, b1, hid1), (hid1, w2, b2, hid2)):
        for c0 in range(0, L, 512):
            cw = min(512, L - c0)
            ps = g.P[0][0:64, 0:cw]
            k.mm(ps, wgt, src[:, c0:c0 + cw])
            k.ts(arg[:, 0:cw], ps, bcol, fr, ALU.add, ALU.mult)
            sin_reduce(k, dst[:, c0:c0 + cw], arg[:, 0:cw], ti[:, 0:cw], tf[:, 0:cw])
    dec = k.sbt("dec", [128, 1024])
    hf = k.sbt("hf", [128, 512])
    hb = k.sbt("hb", [128, 512])
    hs = [k.sbt(f"hs{i}", [128, 512], BF16) for i in range(2)]
    hd = [k.sbt(f"hd{i}", [128, 512], BF16) for i in range(2)]
    it = 0
    HS, HD = g.HS[L], g.HD[L]
    for tb in range(L // 128):
        k.act(dec, dabs, AF.Exp, scale=tneg[:, tb:tb + 1])
        for o2 in range(2):
            for ch in range(2):
                pf = g.P[1 + it % 2]
                pb = g.P[3 + it % 2]
                cf = (o2 * 2 + 0) * 1024 + ch * 512
                cb = (o2 * 2 + 1) * 1024 + ch * 512
                k.mm(pf, hid2[:, tb * 128:(tb + 1) * 128], w3[:, cf:cf + 512])
                k.mm(pb, hid2[:, tb * 128:(tb + 1) * 128], w3[:, cb:cb + 512])
                k.tt(hf, pf, dec[:, ch * 512:(ch + 1) * 512], ALU.mult)
                k.tt(hb, pb, dec[:, ch * 512:(ch + 1) * 512], ALU.mult)
                if tb == 0:
                    k.memset(hb[0:1, :], 0.0)
                    bo = o2 * 1024 + ch * 512
                    k.tt(hf[0:1, :], hf[0:1, :], bias[:, bo:bo + 512], ALU.add)
                s_ = hs[it % 2]
                d_ = hd[it % 2]
                it += 1
                k.tt(s_, hf, hb, ALU.add)
                k.tt(d_, hf, hb, ALU.subtract, eng="pool")
                k.dma(HS[o2][tb * 128:(tb + 1) * 128, ch * 512:(ch + 1) * 512], s_, q="sp")
                k.dma(HD[o2][tb * 128:(tb + 1) * 128, ch * 512:(ch + 1) * 512], d_, q="act")


def hy_tables(g, L):
    k = g.k
    nch = L // 128
    altf = k.sbt("altf", [128, 1])
    k.dma(altf, g.cst["altc"], q="act")
    altc = k.sbt("altc", [128, 1], BF16)
    k.copy(altc, altf)
    arf = k.sbt("arf", [1, 256])
    k.dma(arf, g.cst["altrow"], q="act")
    altrow = k.sbt("altrow", [1, 256], BF16)
    k.copy(altrow, arf)
    Ct = [k.sbt(f"Ct{i}", [128, nch, 128], BF16) for i in range(2)]
    St = [k.sbt(f"St{i}", [128, nch, 128], BF16) for i in range(2)]
    return altc, altrow, Ct, St


def load_ft(g, L, kc, Ct, St, it):
    k = g.k
    c_ = Ct[it % 2]
    s_ = St[it % 2]
    k.dma(c_, g.cst[f"cm{L}"][:, kc * 128:(kc + 1) * 128].re("(tc p) k -> p tc k", p=128), q="sp")
    k.dma(s_, g.cst[f"sm{L}"][:, kc * 128:(kc + 1) * 128].re("(tc p) k -> p tc k", p=128), q="act")
    return c_, s_


def phase_hyspec(g, o, L):
    k = g.k
    k.phase()
    nch = L // 128
    N = 2 * L
    altc, altrow, Ct, St = hy_tables(g, L)
    hs = k.sbt("hs_sb", [128, nch, 512], BF16)
    hd = k.sbt("hd_sb", [128, nch, 512], BF16)
    kr = [k.sbt(f"kr{i}", [128, 512]) for i in range(2)]
    ki = [k.sbt(f"ki{i}", [128, 512]) for i in range(2)]
    kn = k.sbt("kn", [1, 512])
    it = 0
    for o2 in range(2):
        for ch in range(2):
            k.dma(hs, g.HS[L][o2][:, ch * 512:(ch + 1) * 512].re("(tc p) c -> p tc c", p=128), q="sp")
            k.dma(hd, g.HD[L][o2][:, ch * 512:(ch + 1) * 512].re("(tc p) c -> p tc c", p=128), q="act")
            pn = g.P[4][0:1, :]
            for tc in range(nch):
                k.mm(pn, altc, hs[:, tc, :], start=(tc == 0), stop=(tc == nch - 1))
            k.act(kn, pn, AF.Copy, scale=1.0 / N)
            k.dma(g.KN[L][o2][:, ch * 512:(ch + 1) * 512], kn, q="sp")
            for kc in range(nch):
                c_, s_ = load_ft(g, L, kc, Ct, St, it)
                pa = g.P[it % 2]
                pb = g.P[2 + it % 2]
                r_ = kr[it % 2]
                i_ = ki[it % 2]
                it += 1
                for tc in range(nch):
                    k.mm(pa, c_[:, tc, :], hs[:, tc, :], start=(tc == 0), stop=(tc == nch - 1))
                for tc in range(nch):
                    k.mm(pb, s_[:, tc, :], hd[:, tc, :], start=(tc == 0), stop=(tc == nch - 1))
                k.act(r_, pa, AF.Copy, scale=2.0 / N)
                k.act(i_, pb, AF.Copy, scale=2.0 / N)
                if kc == 0:
                    k.ts(r_[0:1, :], r_[0:1, :], 0.5, None, ALU.mult)
                k.dma(g.KR[L][o2][kc * 128:(kc + 1) * 128, ch * 512:(ch + 1) * 512], r_, q="sp")
                k.dma(g.KI[L][o2][kc * 128:(kc + 1) * 128, ch * 512:(ch + 1) * 512], i_, q="act")


def phase_hyprep(g, o):
    k = g.k
    k.phase()
    NBLK = T // 128
    segs = [(0, CTX), (CTX, T)]
    z = [k.sbt(f"z{i}", [128, T]) for i in range(2)]
    zc = [k.sbt(f"zc{i}", [128, T]) for i in range(2)]
    zb = k.sbt("zb", [128, T], BF16)
    tok = [k.sbt(f"tok{i}", [128, NBLK, 128], BF16) for i in range(2)]
    it = 0
    pc = 0
    for b in range(NB):
        for chk in range(24):
            z_ = z[it % 2]
            zc_ = zc[it % 2]
            t_ = tok[it % 2]
            it += 1
            k.dma(z_, g.Z[b][chk * 128:(chk + 1) * 128, :], q="sp")
            wc = [spcol(g, f"hy_cw{o}", kk * 24 + chk) for kk in range(3)]
            dwconv(k, zc_, z_, wc, spcol(g, f"hy_cb{o}", chk), 3, 1, segs)
            if chk < 8:
                k.copy(zb, zc_, eng="act")
                for blk in range(NBLK):
                    pt = g.P[pc % 4].bc(BF16)[:, 0:128]
                    pc += 1
                    k.tr(pt, zb[:, blk * 128:(blk + 1) * 128], g.ident_b)
                    k.copy(t_[:, blk, :], pt, eng="act" if blk % 2 else "dve")
                k.dma(g.U1T[b][:, chk * 128:(chk + 1) * 128].re("(n p) c -> p n c", p=128), t_, q="act")
            else:
                k.dma(g.ZC[b][(chk - 8) * 128:(chk - 7) * 128, :], zc_, q="act")


def phase_hyconv(g, l, o, b, seg, o2):
    k = g.k
    k.phase()
    L = CTX if seg == 0 else S
    t_off = 0 if seg == 0 else CTX
    nch = L // 128
    altc, altrow, Ct, St = hy_tables(g, L)
    usrc = g.U1T if o2 == 0 else g.U2T
    u = k.sbt("u_sb", [128, nch, 512], BF16)
    Yr = k.sbt("Yr", [128, nch, 512], BF16)
    Yi = k.sbt("Yi", [128, nch, 512], BF16)
    ynq = k.sbt("ynq", [1, 512], BF16)
    knq = k.sbt("knq", [1, 512])
    kr = [k.sbt(f"kr{i}", [128, 512]) for i in range(2)]
    ki = [k.sbt(f"ki{i}", [128, 512]) for i in range(2)]
    t1 = k.sbt("t1", [128, 512])
    t2 = k.sbt("t2", [128, 512])
    Cn = [k.sbt(f"Cn{i}", [128, nch, 256], BF16) for i in range(1)]
    Sn = [k.sbt(f"Sn{i}", [128, nch, 256], BF16) for i in range(1)]
    xm = [k.sbt(f"xm{i}", [128, 256]) for i in range(2)]
    gh = [k.sbt(f"gh{i}", [128, 256], BF16) for i in range(2)]
    yb = [k.sbt(f"yb{i}", [128, 256], BF16) for i in range(2)]
    ytok = [k.sbt(f"ytok{i}", [128, 2, 128], BF16) for i in range(2)]
    it = 0
    it2 = 0
    it3 = 0
    for ch in range(2):
        k.dma(u, usrc[b][t_off:t_off + L, ch * 512:(ch + 1) * 512].re("(tc p) c -> p tc c", p=128), q="sp")
        k.dma(knq, g.KN[L][o2][:, ch * 512:(ch + 1) * 512], q="act")
        pn = g.P[6][0:1, :]
        for tc in range(nch):
            k.mm(pn, altc, u[:, tc, :], start=(tc == 0), stop=(tc == nch - 1))
        k.tt(ynq, pn, knq, ALU.mult)
        for kc in range(nch):
            c_, s_ = load_ft(g, L, kc, Ct, St, it)
            pa = g.P[it % 2]
            pb = g.P[2 + it % 2]
            r_ = kr[it % 2]
            i_ = ki[it % 2]
            it += 1
            k.dma(r_, g.KR[L][o2][kc * 128:(kc + 1) * 128, ch * 512:(ch + 1) * 512], q="sp")
            k.dma(i_, g.KI[L][o2][kc * 128:(kc + 1) * 128, ch * 512:(ch + 1) * 512], q="act")
            for tc in range(nch):
                k.mm(pa, c_[:, tc, :], u[:, tc, :], start=(tc == 0), stop=(tc == nch - 1))
            for tc in range(nch):
                k.mm(pb, s_[:, tc, :], u[:, tc, :], start=(tc == 0), stop=(tc == nch - 1))
            k.tt(t1, pa, r_, ALU.mult)
            k.tt(t2, pb, i_, ALU.mult)
            k.tt(Yr[:, kc, :], t1, t2, ALU.subtract)
            k.tt(t1, pa, i_, ALU.mult)
            k.tt(t2, pb, r_, ALU.mult)
            k.tt(Yi[:, kc, :], t1, t2, ALU.add)
        for nb in range(L // 256):
            cn = Cn[0]
            sn = Sn[0]
            it2 += 1
            k.dma(cn, g.cst[f"cm{L}"][:, nb * 256:(nb + 1) * 256].re("(kc p) n -> p kc n", p=128), q="sp")
            k.dma(sn, g.cst[f"sm{L}"][:, nb * 256:(nb + 1) * 256].re("(kc p) n -> p kc n", p=128), q="act")
            n0 = t_off + nb * 256
            for cs in range(4):
                crow = ch * 512 + cs * 128
                x_ = xm[it3 % 2]
                g_ = gh[it3 % 2]
                y_ = yb[it3 % 2]
                yt = ytok[it3 % 2]
                ps = g.P[4 + it3 % 2][:, 0:256]
                it3 += 1
                k.dma(x_, g.ZC[b][o2 * 1024 + crow:o2 * 1024 + crow + 128, n0:n0 + 256], q="sp")
                for kc in range(nch):
                    k.mm(ps, Yr[:, kc, cs * 128:(cs + 1) * 128], cn[:, kc, :], start=(kc == 0), stop=False)
                    k.mm(ps, Yi[:, kc, cs * 128:(cs + 1) * 128], sn[:, kc, :], start=False, stop=False)
                k.mm(ps, ynq[:, cs * 128:(cs + 1) * 128], altrow, start=False, stop=True)
                if o2 == 0:
                    k.tt(y_, ps, x_, ALU.mult)
                    for sb in range(2):
                        pt = g.P[7].bc(BF16)[:, sb * 128:(sb + 1) * 128]
                        k.tr(pt, y_[:, sb * 128:(sb + 1) * 128], g.ident_b)
                        k.copy(yt[:, sb, :], pt, eng="act")
                    k.dma(g.U2T[b][n0:n0 + 256, crow:crow + 128].re("(s p) c -> p s c", p=128), yt, q="act")
                else:
                    k.dma(g_, g.GH[b][crow:crow + 128, n0:n0 + 256], q="act")
                    k.tt(x_, ps, x_, ALU.mult)
                    k.tt(y_, x_, g_, ALU.mult, eng="pool")
                    k.dma(g.MIXT[b][crow:crow + 128, n0:n0 + 256], y_, q="act")


def host_consts_fft():
    c = {}
    N = 8192
    tc = np.arange(32)[:, None]
    k1 = np.arange(64)[None, :]
    a = 2 * np.pi * (tc * k1 % 64) / 64.0
    c["f1tab"] = np.concatenate([np.cos(a), np.sin(a)], axis=1).astype(np.float32).astype(BF)
    p = np.arange(128)[:, None, None]
    kk = (np.arange(64)[None, :, None] + 64 * np.arange(64)[None, None, :])
    ang = 2 * np.pi * ((kk * p) % N) / N
    c["t2tab"] = np.concatenate([np.cos(ang), np.sin(ang), -np.sin(ang), np.cos(ang)], axis=2).astype(np.float32).astype(BF)
    k2 = np.arange(64)[:, None, None]
    k1b = np.arange(64)[None, :, None]
    n2 = np.arange(128)[None, None, :]
    ang3 = 2 * np.pi * (((k1b + 64 * k2) * n2) % N) / N
    top = np.stack([np.cos(ang3), np.sin(ang3), -np.cos(ang3)], axis=2)
    bot = np.stack([np.sin(ang3), -np.cos(ang3), -np.sin(ang3)], axis=2)
    c["t3tab"] = np.concatenate([top, bot], axis=0).astype(np.float32).astype(BF)
    j = np.arange(64)[:, None]
    n1 = np.arange(32)[None, :]
    ag = 2 * np.pi * ((j * n1) % 64) / 64.0
    c["gtab"] = np.concatenate([np.cos(ag), -np.sin(ag)], axis=0).astype(np.float32).astype(BF)
    c["pm1"] = np.concatenate([np.ones(32), -np.ones(32)])[None, :].astype(np.float32)
    return c


def fft_f1(g, src, ch, Bd, f1tab, ubuf, bst, cnt):
    k = g.k
    for pg in range(4):
        u = ubuf[cnt["u"] % 2]
        cnt["u"] += 1
        k.dma(u, src[:, ch * 512:(ch + 1) * 512].re("(tc p) c -> tc p c", p=128)[:, pg * 32:(pg + 1) * 32, :], q="sp")
        for pq in range(4):
            st = bst[cnt["b"] % 2]
            cnt["b"] += 1
            for i in range(8):
                ps_ = pq * 8 + i
                pp = g.P[cnt["p"] % 8]
                cnt["p"] += 1
                k.mm(pp, f1tab, u[:, ps_, :])
                if i % 2:
                    k.act(st[:, i, :], pp, AF.Copy)
                else:
                    k.copy(st[:, i, :], pp, eng="dve")
            p0 = pg * 32 + pq * 8
            k.dma(Bd[:, p0:p0 + 8, :], st, q="act")


def phase_fft_f1(g, srcs):
    k = g.k
    k.phase()
    f1tab = k.sbt("f1tab", [32, 128], BF16)
    k.dma(f1tab, g.cst["f1tab"], q="act")
    ubuf = [k.sbt(f"fu{i}", [32, 32, 512], BF16) for i in range(2)]
    bst = [k.sbt(f"fb{i}", [128, 8, 512], BF16) for i in range(2)]
    cnt = {"u": 0, "b": 0, "p": 0}
    for (src, ch, Bd) in srcs:
        fft_f1(g, src, ch, Bd, f1tab, ubuf, bst, cnt)


def phase_hyspec2(g, o):
    k = g.k
    N = 8192
    for o2 in range(2):
        phase_fft_f1(g, [(g.HS[S][o2], 0, g.Bd[0]), (g.HS[S][o2], 1, g.Bd[1]),
                         (g.HD[S][o2], 0, g.Bd[2]), (g.HD[S][o2], 1, g.Bd[3])])
        k.phase()
        t2 = k.sbt("t2", [128, 64, 256], BF16)
        k.dma(t2, g.cst["t2tab"], q="sp")
        altf = k.sbt("altf", [128, 1])
        k.dma(altf, g.cst["altc"], q="act")
        altc = k.sbt("altc", [128, 1], BF16)
        k.copy(altc, altf)
        br = [[k.sbt(f"br{s}_{i}", [128, 8, 512], BF16) for i in range(2)] for s in range(2)]
        bs = [[k.sbt(f"bs{s}_{i}", [128, 8, 512], BF16) for i in range(2)] for s in range(2)]
        kr = [k.sbt(f"kr{i}", [64, 512], BF16) for i in range(2)]
        ks = [k.sbt(f"ks{i}", [64, 512], BF16) for i in range(2)]
        kn = k.sbt("kn", [1, 512])
        it = 0
        groups = [(c_, kg_) for c_ in range(2) for kg_ in range(8)]

        def sp_load(i):
            c_, kg_ = groups[i]
            for s_, Bd in ((0, g.Bd[c_]), (1, g.Bd[2 + c_])):
                k.dma(br[s_][i % 2], Bd[kg_ * 8:(kg_ + 1) * 8].re("k p c -> p k c"), q="sp")
                k.dma(bs[s_][i % 2], Bd[64 + kg_ * 8:64 + (kg_ + 1) * 8].re("k p c -> p k c"), q="sp")
        sp_load(0)
        for gi, (ch, kg) in enumerate(groups):
            bb = gi % 2
            if gi + 1 < len(groups):
                sp_load(gi + 1)
            for kk in range(8):
                k1 = kg * 8 + kk
                pr = g.P[it % 2][0:64, :]
                pi = g.P[2 + it % 2][0:64, :]
                r_ = kr[it % 2]
                s2 = ks[it % 2]
                it += 1
                k.mm(pr, t2[:, k1, 0:64], br[0][bb][:, kk, :], start=True, stop=False)
                k.mm(pr, t2[:, k1, 128:192], bs[0][bb][:, kk, :], start=False, stop=True)
                k.mm(pi, t2[:, k1, 64:128], br[1][bb][:, kk, :], start=True, stop=False)
                k.mm(pi, t2[:, k1, 0:64], bs[1][bb][:, kk, :], start=False, stop=True)
                k.act(r_, pr, AF.Copy, scale=2.0 / N)
                k.ts(s2, pi, 2.0 / N, None, ALU.mult)
                if k1 == 0:
                    k.ts(r_[0:1, :], pr[0:1, :], 1.0 / N, None, ALU.mult)
                    pn = g.P[4][0:1, :]
                    k.mm(pn, altc, br[0][bb][:, 0, :])
                    k.act(kn, pn, AF.Copy, scale=1.0 / N)
                    k.dma(g.KN2[o2][:, ch * 512:(ch + 1) * 512], kn, q="act")
                k.dma(g.KR2[o2][k1][:, ch * 512:(ch + 1) * 512], r_, q="act")
                k.dma(g.KS2[o2][k1][:, ch * 512:(ch + 1) * 512], s2, q="act")


def phase_hyconv2(g, l, o, b, o2):
    k = g.k
    usrc = g.U1T if o2 == 0 else g.U2T
    uv = usrc[b][CTX:CTX + S, :]
    phase_fft_f1(g, [(uv, 0, g.Bd[0]), (uv, 1, g.Bd[1])])
    k.phase()
    t2 = k.sbt("t2", [128, 64, 256], BF16)
    k.dma(t2, g.cst["t2tab"], q="sp")
    t3 = k.sbt("t3", [128, 64, 3, 128], BF16)
    k.dma(t3, g.cst["t3tab"], q="act")
    altf = k.sbt("altf", [128, 1])
    k.dma(altf, g.cst["altc"], q="act")
    altc = k.sbt("altc", [128, 1], BF16)
    k.copy(altc, altf)
    br = [k.sbt(f"br{i}", [128, 4, 512], BF16) for i in range(2)]
    bs = [k.sbt(f"bs{i}", [128, 4, 512], BF16) for i in range(2)]
    kr = [k.sbt(f"kr{i}", [128, 4, 512], BF16) for i in range(2)]
    ks = [k.sbt(f"ks{i}", [128, 4, 512], BF16) for i in range(2)]
    knq = [k.sbt(f"knq{i}", [1, 512]) for i in range(2)]
    p1 = [k.sbt(f"p1_{i}", [128, 512], BF16) for i in range(2)]
    p2 = [k.sbt(f"p2_{i}", [128, 512], BF16) for i in range(2)]
    dst_ = [k.sbt(f"dst{i}", [128, 2, 512], BF16) for i in range(2)]
    it = 0
    groups = [(c_, kg_) for c_ in range(2) for kg_ in range(16)]

    def f2_load(i):
        c_, kg_ = groups[i]
        bb_ = i % 2
        if kg_ == 0:
            k.dma(knq[c_], g.KN2[o2][:, c_ * 512:(c_ + 1) * 512], q="sp")
        k.dma(br[bb_], g.Bd[c_][kg_ * 4:(kg_ + 1) * 4].re("k p c -> p k c"), q="sp")
        k.dma(bs[bb_], g.Bd[c_][64 + kg_ * 4:64 + (kg_ + 1) * 4].re("k p c -> p k c"), q="sp")
        krv = g.KR2[o2][kg_ * 4:(kg_ + 1) * 4][:, :, c_ * 512:(c_ + 1) * 512].re("k q c -> q k c")
        ksv = g.KS2[o2][kg_ * 4:(kg_ + 1) * 4][:, :, c_ * 512:(c_ + 1) * 512].re("k q c -> q k c")
        k.dma(kr[bb_][0:64], krv, q="sp")
        k.dma(kr[bb_][64:128], krv, q="sp")
        k.dma(ks[bb_][0:64], ksv, q="sp")
        k.dma(ks[bb_][64:128], ksv, q="sp")
    f2_load(0)
    for gi, (ch, kg) in enumerate(groups):
        Dd = g.Dd[ch]
        bb = gi % 2
        if gi + 1 < len(groups):
            f2_load(gi + 1)
        for kk in range(4):
            k1 = kg * 4 + kk
            px = g.P[it % 2]
            pdr = g.P[2 + it % 2]
            pdi = g.P[4 + it % 2]
            a_ = p1[it % 2]
            b_ = p2[it % 2]
            d_ = dst_[it % 2]
            it += 1
            k.mm(px, t2[:, k1, 0:128], br[bb][:, kk, :], start=True, stop=False)
            k.mm(px, t2[:, k1, 128:256], bs[bb][:, kk, :], start=False, stop=True)
            if k1 == 0:
                pn = g.P[6][0:1, :]
                k.mm(pn, altc, br[bb][:, 0, :])
                k.tt(g.ynq[:, ch, :], pn, knq[ch], ALU.mult)
            k.tt(a_, px, kr[bb][:, kk, :], ALU.mult)
            k.tt(b_, px, ks[bb][:, kk, :], ALU.mult)
            k.mm(pdr, t3[:, k1, 0, :], a_, start=True, stop=False)
            k.mm(pdr, t3[:, k1, 1, :], b_, start=False, stop=True)
            k.mm(pdi, t3[:, k1, 1, :], a_, start=True, stop=False)
            k.mm(pdi, t3[:, k1, 2, :], b_, start=False, stop=True)
            k.act(d_[:, 0, :], pdr, AF.Copy)
            k.copy(d_[:, 1, :], pdi, eng="dve")
            k.dma(Dd[k1], d_[:, 0, :], q="act")
            k.dma(Dd[64 + k1], d_[:, 1, :], q="act")
    k.phase()
    gtab = k.sbt("gtab", [128, 32], BF16)
    k.dma(gtab, g.cst["gtab"], q="act")
    pmf = k.sbt("pmf", [1, 64])
    k.dma(pmf, g.cst["pm1"], q="act")
    pm = k.sbt("pm", [1, 64], BF16)
    k.copy(pm, pmf)
    NBL = S // 128
    altpat = k.sbt("altpat", [128, S])
    k.memset(altpat, 1.0)
    k.memset(altpat.re("p (a two) -> p a two", two=2)[:, :, 1], -1.0)
    ycol = [k.sbt(f"ycol{i}", [128, 1]) for i in range(2)]
    dt_ = [k.sbt(f"dt{i}", [128, 16, 512], BF16) for i in range(2)]
    yT = [k.sbt(f"yT{i}", [128, 32, 128]) for i in range(4)]
    xT = [k.sbt(f"xT{i}", [128, S]) for i in range(2)]
    ob = [k.sbt(f"ob{i}", [128, S], BF16) for i in range(2)]
    ghT = [k.sbt(f"ghT{i}", [128, S], BF16) for i in range(2)] if o2 == 1 else None
    tok = [k.sbt(f"tok{i}", [128, NBL, 128], BF16) for i in range(1)] if o2 == 0 else None
    it = 0
    pc = 0

    def x_load(ch_, cs_):
        crow_ = ch_ * 512 + cs_ * 128
        k.dma(xT[cs_ % 2], g.ZC[b][o2 * 1024 + crow_:o2 * 1024 + crow_ + 128, CTX:CTX + S], q="act")
        if o2 == 1:
            k.dma(ghT[cs_ % 2], g.GH[b][crow_:crow_ + 128, CTX:CTX + S], q="act")
    for ch in range(2):
        Dd = g.Dd[ch]
        x_load(ch, 0)
        x_load(ch, 1)
        for n2g in range(8):
            d_ = dt_[it % 2]
            it += 1
            k.dma(d_, Dd[:, n2g * 16:(n2g + 1) * 16, :], q="sp")
            for cs in range(4):
                ps = g.P[pc % 6]
                pc += 1
                for j in range(16):
                    k.mm(ps[:, j * 32:(j + 1) * 32], d_[:, j, cs * 128:(cs + 1) * 128], gtab, start=True, stop=True)
                k.copy(yT[cs][:, :, n2g * 16:(n2g + 1) * 16], ps.re("p (a b) -> p b a", a=16),
                       eng="act" if cs % 2 else "dve")
        for cs in range(4):
            ci = cs % 2
            crow = ch * 512 + cs * 128
            yflat = yT[cs].re("p a b -> p (a b)")
            pcol = g.P[7][:, 0:1]
            k.mm(pcol, g.ynq[:, ch, cs * 128:(cs + 1) * 128], pm[:, 0:1])
            k.copy(ycol[ci], pcol, eng="dve")
            k.stt(yflat, altpat, ycol[ci], yflat, ALU.mult, ALU.add)
            if o2 == 0:
                k.tt(ob[ci], yflat, xT[ci], ALU.mult)
                t_ = tok[0]
                for blk in range(NBL):
                    pt = g.P[6 + blk % 2].bc(BF16)[:, 0:128]
                    k.tr(pt, ob[ci][:, blk * 128:(blk + 1) * 128], g.ident_b)
                    k.copy(t_[:, blk, :], pt, eng="act" if blk % 2 else "dve")
                k.dma(g.U2T[b][CTX:CTX + S, crow:crow + 128].re("(n p) c -> p n c", p=128), t_, q="act")
            else:
                k.tt(yflat, yflat, xT[ci], ALU.mult)
                k.tt(ob[ci], yflat, ghT[ci], ALU.mult, eng="pool")
                k.dma(g.MIXT[b][crow:crow + 128, CTX:CTX + S], ob[ci], q="act")
            if cs + 2 < 4:
                x_load(ch, cs + 2)
```

```python
import numpy as np
import concourse.bass as bass
import concourse.mybir as mybir

F32 = mybir.dt.float32
BF16 = mybir.dt.bfloat16
AF = mybir.ActivationFunctionType
ALU = mybir.AluOpType
AX = mybir.AxisListType

SEM_CHUNK = 20000
NDMA_SEMS = 12


class V:
    __slots__ = ("ap", "units")

    def __init__(self, ap, units):
        self.ap = ap
        self.units = tuple(units)

    def __getitem__(self, idx):
        return V(self.ap[idx], self.units)

    def re(self, pat, **kw):
        return V(self.ap.rearrange(pat, **kw), self.units)

    def bc(self, dt):
        return V(self.ap.bitcast(dt), self.units)


class Op:
    __slots__ = ("eng", "fn", "deps", "dma", "sig", "waits", "idx", "has_dep")

    def __init__(self, eng, fn, deps, dma):
        self.eng = eng
        self.fn = fn
        self.deps = deps
        self.dma = dma
        self.sig = None
        self.waits = None
        self.has_dep = False


class Builder:
    def __init__(self, nc):
        self.nc = nc
        self.ops = []
        self.units = {}
        self.last_dma = {"sp": [], "act": [], "pool": []}
        self.pending = {}
        self.psum_units = set()
        self.arena = None
        self.arena_off = 0
        self.arena_base = 0
        self.uid = 0

    def init_arena(self, nbytes):
        self.arena_bytes = nbytes
        self.arena = self.nc.alloc_sbuf_tensor("arena", [128, nbytes // 4], F32)

    def sbt(self, name, shape, dtype=F32, glob=False):
        esz = 2 if dtype == BF16 else 4
        n = 1
        for d in shape[1:]:
            n *= d
        nb = (n * esz + 31) // 32 * 32
        off = self.arena_off
        assert off + nb <= self.arena_bytes, (name, off, nb)
        self.arena_off += nb
        ap = self.arena[:, off // 4:(off + nb) // 4]
        if esz == 2:
            ap = ap.bitcast(BF16)
        elif dtype != F32:
            ap = ap.bitcast(dtype)
        ap = ap[:, 0:n]
        if len(shape) == 3:
            ap = ap.rearrange("p (a b) -> p a b", a=shape[1])
        elif len(shape) == 4:
            ap = ap.rearrange("p (a b c) -> p a b c", a=shape[1], b=shape[2])
        if shape[0] != 128:
            ap = ap[0:shape[0]]
        self.uid += 1
        return V(ap, (f"{name}#{self.uid}",))

    def phase(self):
        self.barrier()
        self.arena_off = self.arena_base

    def freeze_globals(self):
        self.arena_base = self.arena_off

    def barrier(self):
        deps = set()
        last = {}
        for i, op in enumerate(self.ops):
            if not op.dma:
                last[op.eng] = i
        deps.update(last.values())
        for q, lst in self.last_dma.items():
            deps.update(lst[-NDMA_SEMS:])
        for e in ("pe", "act", "dve", "pool", "sp"):
            self.pending[e] = set(deps)

    def sb(self, name, shape, dtype=F32, units=None):
        h = self.nc.alloc_sbuf_tensor(name, list(shape), dtype)
        return V(h[:], units if units is not None else (name,))

    def ps(self, name, shape, dtype=F32):
        h = self.nc.alloc_psum_tensor(name, list(shape), dtype)
        self.psum_units.add(name)
        return V(h[:], (name,))

    def dram(self, name, shape, dtype=F32, kind="Internal"):
        h = self.nc.dram_tensor(name, list(shape), dtype, kind=kind)
        return V(h.ap(), (name,))

    def add(self, eng, fn, r=(), w=(), dma=False):
        deps = set()
        ru = []
        wu = []
        for v in r:
            ru.extend(v.units if isinstance(v, V) else (v,))
        for v in w:
            wu.extend(v.units if isinstance(v, V) else (v,))
        pr = [u for u in ru if u in self.psum_units]
        if pr:
            ru = [u for u in ru if u not in self.psum_units]
            wu = wu + pr
        for u in ru:
            st = self.units.get(u)
            if st is not None and st[0] is not None:
                deps.add(st[0])
        for u in wu:
            st = self.units.get(u)
            if st is not None:
                if st[0] is not None:
                    deps.add(st[0])
                deps.update(st[1])
        opid = len(self.ops)
        pend = self.pending.pop(eng, None)
        if pend:
            deps.update(pend)
        if dma:
            q = self.last_dma[eng]
            if len(q) >= NDMA_SEMS:
                deps.add(q[-NDMA_SEMS])
            q.append(opid)
        if eng == "pe" and not dma:
            deps = {d for d in deps if not (self.ops[d].eng == "pe" and not self.ops[d].dma)}
        op = Op(eng, fn, deps, dma)
        self.ops.append(op)
        for d in deps:
            self.ops[d].has_dep = True
        for u in ru:
            st = self.units.get(u)
            if st is None:
                st = self.units[u] = [None, []]
            st[1].append(opid)
        for u in wu:
            self.units[u] = [opid, []]
        return opid

    def mm(self, out, lhsT, rhs, start=True, stop=True):
        self.add("pe", lambda e: e.matmul(out.ap, lhsT.ap, rhs.ap, start=start, stop=stop),
                 r=(lhsT, rhs), w=(out,))

    def tr(self, out, in_, ident):
        self.add("pe", lambda e: e.transpose(out.ap, in_.ap, ident.ap), r=(in_, ident), w=(out,))

    def act(self, out, in_, func, bias=None, scale=None, accum=None, eng="act"):
        kw = {}
        r = [in_]
        w = [out]
        if bias is not None:
            if isinstance(bias, V):
                kw["bias"] = bias.ap
                r.append(bias)
            else:
                kw["bias"] = bias
        if scale is not None:
            if isinstance(scale, V):
                kw["scale"] = scale.ap
                r.append(scale)
            else:
                kw["scale"] = scale
        if accum is not None:
            kw["accum_out"] = accum.ap
            w.append(accum)
        self.add("act", lambda e: e.activation(out.ap, in_.ap, func, **kw), r=r, w=w)

    def tt(self, out, a, b, op, eng="dve"):
        self.add(eng, lambda e: e.tensor_tensor(out.ap, a.ap, b.ap, op), r=(a, b), w=(out,))

    def ts(self, out, a, s1, s2, op0, op1=None, eng="dve", accum=None):
        r = [a]
        w = [out]
        a1 = s1
        a2 = s2
        if isinstance(s1, V):
            r.append(s1)
            a1 = s1.ap
        if isinstance(s2, V):
            r.append(s2)
            a2 = s2.ap
        kw = {}
        if a2 is None:
            a2 = 0.0
            op1 = ALU.add
        if op1 is not None:
            kw["op1"] = op1
        if accum is not None:
            kw["accum_out"] = accum.ap
            w.append(accum)
        self.add(eng, lambda e: e.tensor_scalar(out.ap, a.ap, a1, a2, op0, **kw), r=r, w=w)

    def stt(self, out, a, s, b, op0, op1, eng="dve"):
        r = [a, b]
        sa = s
        if isinstance(s, V):
            r.append(s)
            sa = s.ap
        self.add(eng, lambda e: e.scalar_tensor_tensor(out.ap, a.ap, sa, b.ap, op0, op1), r=r, w=(out,))

    def scan(self, out, d0, d1, init, op0=ALU.mult, op1=ALU.add):
        r = [d0, d1]
        ia = init
        if isinstance(init, V):
            r.append(init)
            ia = init.ap
        self.add("dve", lambda e: e.tensor_tensor_scan(out.ap, d0.ap, d1.ap, ia, op0, op1), r=r, w=(out,))

    def copy(self, out, in_, eng="dve"):
        if eng == "act":
            self.add("act", lambda e: e.copy(out.ap, in_.ap), r=(in_,), w=(out,))
        else:
            self.add(eng, lambda e: e.tensor_copy(out.ap, in_.ap), r=(in_,), w=(out,))

    def memset(self, out, val, eng="dve"):
        self.add(eng, lambda e: e.memset(out.ap, val), w=(out,))

    def recip(self, out, in_):
        self.add("dve", lambda e: e.reciprocal(out.ap, in_.ap), r=(in_,), w=(out,))

    def dma(self, out, in_, q="sp"):
        self.add(q, lambda e: e.dma_start(out=out.ap, in_=in_.ap), r=(in_,), w=(out,), dma=True)

    def emit(self, final_wait_units=()):
        nc = self.nc
        ops = self.ops
        engs = ("pe", "act", "dve", "pool", "sp")
        final_deps = set()
        for u in final_wait_units:
            st = self.units.get(u)
            if st is not None and st[0] is not None:
                final_deps.add(st[0])
        for d in final_deps:
            ops[d].has_dep = True
        n_sig = {e: 0 for e in engs}
        n_dma = {"sp": 0, "act": 0, "pool": 0}
        for op in ops:
            if op.dma:
                k = n_dma[op.eng]
                n_dma[op.eng] += 1
                op.sig = ("dma", op.eng, k % NDMA_SEMS, 16 * (k // NDMA_SEMS + 1))
            elif op.has_dep:
                k = n_sig[op.eng]
                n_sig[op.eng] += 1
                op.sig = ("cmp", op.eng, k // SEM_CHUNK, k % SEM_CHUNK + 1)
        sems = {}
        for e in engs:
            for c in range((n_sig[e] + SEM_CHUNK - 1) // SEM_CHUNK):
                sems[("cmp", e, c)] = nc.alloc_semaphore(name=f"s_{e}_{c}")
        for q, n in n_dma.items():
            for c in range(min(n, NDMA_SEMS)):
                sems[("dma", q, c)] = nc.alloc_semaphore(name=f"d_{q}_{c}")
        self.n_sems = len(sems)
        by_eng = {e: [] for e in engs}
        waited = {e: {} for e in engs}
        for op in ops:
            ws = {}
            for d in op.deps:
                s = ops[d].sig
                key = s[:3]
                if ws.get(key, 0) < s[3]:
                    ws[key] = s[3]
            wl = []
            wd = waited[op.eng]
            for key, val in ws.items():
                if wd.get(key, 0) < val:
                    wd[key] = val
                    wl.append((key, val))
            op.waits = wl
            by_eng[op.eng].append(op)
        fin = []
        fw = {}
        for d in final_deps:
            s = ops[d].sig
            if fw.get(s[:3], 0) < s[3]:
                fw[s[:3]] = s[3]
        fin = list(fw.items())

        def run(engine, name):
            for op in by_eng[name]:
                for key, val in op.waits:
                    engine.wait_ge(sems[key], val)
                ins = op.fn(engine)
                if op.sig is not None:
                    ins.then_inc(sems[op.sig[:3]], 16 if op.dma else 1)
            if name == "sp":
                for key, val in fin:
                    engine.wait_ge(sems[key], val)

        with nc.Block() as block:
            @block.tensor
            def _(e):
                run(e, "pe")

            @block.scalar
            def _(e):
                run(e, "act")

            @block.vector
            def _(e):
                run(e, "dve")

            @block.gpsimd
            def _(e):
                run(e, "pool")

            @block.sync
            def _(e):
                run(e, "sp")
        return {e: len(by_eng[e]) for e in engs}


import math
import ml_dtypes
from concourse.ap import AP
from concourse.bass_utils import run_bass_kernel_spmd

NB = 2
S = 4096
CTX = 256
T = S + CTX
D = 1024
TT = 256
NT = T // TT
DEPTH = 4
EPS = 1e-6
I32 = mybir.dt.int32
BF = ml_dtypes.bfloat16


def sp_layout():
    lay = {}
    off = 0

    def reg(name, n):
        nonlocal off
        lay[name] = (off, n)
        off += n
    reg("mod_b", 96)
    reg("norm_pre", 32)
    reg("norm_post", 32)
    for e in range(2):
        reg(f"lru_cw{e}", 32)
        reg(f"lru_cb{e}", 8)
        reg(f"lru_ba{e}", 16)
        reg(f"lru_bx{e}", 16)
        reg(f"lru_lam{e}", 16)
        reg(f"sink{e}", 8)
    for o in range(2):
        reg(f"hy_cw{o}", 72)
        reg(f"hy_cb{o}", 24)
        reg(f"hy_b1{o}", 1)
        reg(f"hy_freq{o}", 1)
        reg(f"hy_b2{o}", 1)
        reg(f"ret_logit{o}", 16)
    return lay, off


def chunked(v):
    v = np.asarray(v, np.float32)
    lead = v.shape[:-1]
    n = v.shape[-1] // 128
    a = v.reshape(lead + (n, 128))
    a = np.moveaxis(a, -1, 0)
    return a.reshape(128, -1)


def host_sp(inp):
    lay, n = sp_layout()
    sp = np.zeros((128, n), np.float32)

    def put(name, arr):
        o, c = lay[name]
        assert arr.shape == (128, c), (name, arr.shape, c)
        sp[:, o:o + c] = arr
    put("mod_b", chunked(inp["mod_b"].reshape(4, 3, 1024)))
    put("norm_pre", chunked(inp["norm_pre"]))
    put("norm_post", chunked(inp["norm_post"]))
    for e in range(2):
        put(f"lru_cw{e}", chunked(inp["lru_conv_w"][e]))
        put(f"lru_cb{e}", chunked(inp["lru_conv_b"][e]))
        put(f"lru_ba{e}", chunked(inp["lru_ba"][e]))
        put(f"lru_bx{e}", chunked(inp["lru_bx"][e]))
        put(f"lru_lam{e}", chunked(inp["lru_lambda"][e]))
        put(f"sink{e}", np.broadcast_to(inp["attn_sink"][e][None, :], (128, 8)))
    for o in range(2):
        put(f"hy_cw{o}", chunked(inp["hy_conv_w"][o]))
        put(f"hy_cb{o}", chunked(inp["hy_conv_b"][o]))
        for nm, key in (("hy_b1", "hy_b1"), ("hy_freq", "hy_freq"), ("hy_b2", "hy_b2")):
            col = np.zeros((128, 1), np.float32)
            col[:64, 0] = inp[key][o]
            put(f"{nm}{o}", col)
        put(f"ret_logit{o}", np.broadcast_to(inp["ret_decay_logit"][o].reshape(1, 16), (128, 16)))
    return sp


def host_consts():
    c = {}
    c["ident_f"] = np.eye(128, dtype=np.float32)
    half = 64
    inv = (10000.0 ** (-np.arange(0, half, 2, dtype=np.float32) / half)).astype(np.float32)
    tok = np.arange(S)
    row = (tok // 64).astype(np.float32)
    col = (tok % 64).astype(np.float32)
    ang_r = row[:, None] * inv[None]
    ang_c = col[:, None] * inv[None]
    cosT = np.ones((128, T), np.float32)
    sinT = np.zeros((128, T), np.float32)
    for d in range(128):
        ang = ang_r if d < 64 else ang_c
        j = d % 32
        first = (d % 64) < 32
        cosT[d, CTX:] = np.cos(ang[:, j])
        sinT[d, CTX:] = (-1.0 if first else 1.0) * np.sin(ang[:, j])
    c["ropec"] = cosT
    c["ropes"] = sinT
    pm = np.zeros((128, 128), np.float32)
    for d in range(128):
        partner = d + 32 if (d % 64) < 32 else d - 32
        pm[partner, d] = 1.0
    c["perm"] = pm
    jj = np.arange(128)[:, None]
    ii = np.arange(128)[None, :]
    mprev = (jj >= ii).astype(np.float32)
    mnext = (jj <= ii).astype(np.float32)
    c["mprev"] = np.tile(mprev[:, None, :], (1, 4, 1)).reshape(128, 512)
    c["mnext"] = np.tile(mnext[:, None, :], (1, 4, 1)).reshape(128, 512)
    c.update(host_consts_odd())
    c.update(host_consts_hy())
    c.update(host_consts_fft())
    return c


class Ctx:
    pass


def build_program(n_layers=DEPTH, debug=False, stop=99):
    nc = bass.Bass("TRN2", target_bir_lowering=False)
    k = Builder(nc)
    g = Ctx()
    g.k = k
    lay, nsp = sp_layout()
    g.lay = lay
    EI = "ExternalInput"
    g.xin = k.dram("xin", [NB, D, T], F32, kind=EI)
    g.cT = k.dram("cT", [128, 8, 3], F32, kind=EI)
    g.spd = k.dram("sp", [128, nsp], F32, kind=EI)
    g.mod_w = k.dram("mod_w", [4, D, 3 * D], F32, kind=EI)
    g.ev_w_in = k.dram("ev_w_in", [2, D, 4608], F32, kind=EI)
    g.ev_w_out = k.dram("ev_w_out", [2, 2048, D], F32, kind=EI)
    g.lru_wa = k.dram("lru_wa", [2, 2, 8, 128, 128], F32, kind=EI)
    g.lru_wx = k.dram("lru_wx", [2, 2, 8, 128, 128], F32, kind=EI)
    g.cst = {}
    for nm, shp in (("ident_f", [128, 128]), ("ropec", [128, T]), ("ropes", [128, T]), ("perm", [128, 128]),
                    ("mprev", [128, 512]), ("mnext", [128, 512])):
        g.cst[nm] = k.dram(nm, shp, F32, kind=EI)
    g.yout = k.dram("yout", [NB, D, T], F32, kind="ExternalOutput")
    g.XR = k.dram("XR", [NB, D, T], F32)
    g.XA = k.dram("XA", [NB, D, T], F32)
    g.GA = k.dram("GA", [NB, D, T], BF16)
    g.Q = k.dram("Q", [NB, D, T], BF16)
    g.K = k.dram("K", [NB, 256, T], BF16)
    g.VT = k.dram("VT", [NB, T, 256], BF16)
    g.GB = k.dram("GB", [NB, D, T], BF16)
    g.MIXT = k.dram("MIXT", [NB, 2048, T], BF16)
    g.od_w_in = k.dram("od_w_in", [2, D, 8192], F32, kind=EI)
    g.od_w_out = k.dram("od_w_out", [2, 2048, D], F32, kind=EI)
    g.hy_w1 = k.dram("hy_w1", [2, 33, 64], F32, kind=EI)
    g.hy_w2 = k.dram("hy_w2", [2, 64, 64], F32, kind=EI)
    g.hy_w3 = k.dram("hy_w3", [2, 64, 4096], F32, kind=EI)
    g.hy_bias = k.dram("hy_bias", [2, 2, 1024], F32, kind=EI)
    for nm, shp, dt_ in (("retc", [128, 770], F32), ("zemb4096", [33, 4096], F32), ("zemb256", [33, 256], F32),
                         ("tneg4096", [128, 32], F32), ("tneg256", [128, 2], F32), ("dabs", [128, 1024], F32),
                         ("altc", [128, 1], F32), ("altrow", [1, 256], F32),
                         ("cm4096", [4096, 4096], BF16), ("sm4096", [4096, 4096], BF16),
                         ("cm256", [256, 256], BF16), ("sm256", [256, 256], BF16)):
        g.cst[nm] = k.dram(nm, shp, dt_, kind=EI)
    for nm, shp, dt_ in (("f1tab", [32, 128], BF16), ("t2tab", [128, 64, 256], BF16), ("t3tab", [128, 64, 3, 128], BF16),
                         ("gtab", [128, 32], BF16), ("pm1", [1, 64], F32)):
        g.cst[nm] = k.dram(nm, shp, dt_, kind=EI)
    g.Bd = [k.dram(f"Bd{i}", [128, 128, 512], BF16) for i in range(4)]
    g.Dd = [k.dram(f"Dd{i}", [128, 128, 512], BF16) for i in range(2)]
    g.KR2 = k.dram("KR2", [2, 64, 64, D], BF16)
    g.KS2 = k.dram("KS2", [2, 64, 64, D], BF16)
    g.KN2 = k.dram("KN2", [2, 1, D], F32)
    g.Z = k.dram("Z", [NB, 3072, T], F32)
    g.ZC = k.dram("ZC", [NB, 2048, T], F32)
    g.GH = k.dram("GH", [NB, D, T], BF16)
    g.RQ = k.dram("RQ", [NB, D, T], BF16)
    g.RK = k.dram("RK", [NB, D, T], BF16)
    g.RKT = k.dram("RKT", [NB, T, D], BF16)
    g.RVT = k.dram("RVT", [NB, T, D], BF16)
    g.GD = k.dram("GD", [NB, D, T], BF16)
    g.U1T = k.dram("U1T", [NB, T, D], BF16)
    g.U2T = k.dram("U2T", [NB, T, D], BF16)
    g.HS = {L: k.dram(f"HS{L}", [2, L, D], BF16) for L in (S, CTX)}
    g.HD = {L: k.dram(f"HD{L}", [2, L, D], BF16) for L in (S, CTX)}
    g.KR = {L: k.dram(f"KR{L}", [2, L, D], F32) for L in (S, CTX)}
    g.KI = {L: k.dram(f"KI{L}", [2, L, D], F32) for L in (S, CTX)}
    g.KN = {L: k.dram(f"KN{L}", [2, 1, D], F32) for L in (S, CTX)}

    k.init_arena(200 * 1024)
    g.P = [k.ps(f"P{i}", [128, 512], F32) for i in range(8)]
    g.sp = k.sbt("sp", [128, nsp])
    k.dma(g.sp, g.spd)
    g.ident_f = k.sbt("ident_f", [128, 128])
    k.dma(g.ident_f, g.cst["ident_f"], q="act")
    g.ident_b = k.sbt("ident_b", [128, 128], BF16)
    k.copy(g.ident_b, g.ident_f)
    g.ones_b = k.sbt("ones_b", [128, 128], BF16)
    k.memset(g.ones_b, 1.0)
    g.modA = k.sbt("modA", [128, 4, 8, 3])
    g.modB = k.sbt("modB", [128, 4, 8, 3])
    g.modG = k.sbt("modG", [128, 4, 8, 3])
    g.ynq = k.sbt("ynq_g", [1, 2, 512], BF16)
    k.freeze_globals()

    phase_mod(g)
    for l in range(n_layers):
        last = (l == n_layers - 1)
        if l % 2 == 0:
            e = l // 2
            if stop >= 1:
                phase_proj(g, l, g.ev_w_in[e], 4608, even_specs(g))
            if stop >= 2:
                phase_lru(g, l, e)
            if stop >= 3:
                phase_attn(g, l, e)
            if stop >= 4:
                phase_out(g, l, g.ev_w_out[e], last)
        else:
            o = l // 2
            if stop >= 1:
                phase_proj(g, l, g.od_w_in[o][:, 0:4096], 4096, odd_specs1(g))
                phase_proj(g, l, g.od_w_in[o][:, 4096:8192], 4096, odd_specs2(g))
            if stop >= 2:
                phase_ret(g, l, o)
            if stop >= 3:
                phase_hyfilt(g, o, S)
                if not last:
                    phase_hyfilt(g, o, CTX)
                    phase_hyspec(g, o, CTX)
                phase_hyspec2(g, o)
                phase_hyprep(g, o)
            if stop >= 4:
                for b in range(NB):
                    for o2 in range(2):
                        if not last:
                            phase_hyconv(g, l, o, b, 0, o2)
                        phase_hyconv2(g, l, o, b, o2)
            if stop >= 5:
                phase_out(g, l, g.od_w_out[o], last)
    stats = k.emit(final_wait_units=("yout",))
    print("ops per engine", stats, "sems", k.n_sems, flush=True)
    return nc


def spcol(g, name, i=0, n=1):
    o, c = g.lay[name]
    return g.sp[:, o + i:o + i + n]


def phase_mod(g):
    k = g.k
    k.phase()
    c_sb = k.sbt("c_sb", [128, 8, 3])
    k.dma(c_sb, g.cT)
    sc = k.sbt("sc", [128, 8, 3])
    k.act(sc, c_sb, AF.Silu)
    raw = k.sbt("modraw", [128, 96, 3])
    wm = [k.sbt(f"wm{i}", [128, 8, 1024]) for i in range(2)]
    it = 0
    for l in range(DEPTH):
        for part in range(3):
            w = wm[it % 2]
            it += 1
            src = g.mod_w[l][:, part * 1024:(part + 1) * 1024].re("(dc p) f -> p dc f", p=128)
            k.dma(w, src, q="sp" if it % 2 else "act")
            ps = g.P[it % 2]
            for fc in range(8):
                for dc in range(8):
                    k.mm(ps[:, fc * 4:fc * 4 + 3], w[:, dc, fc * 128:(fc + 1) * 128], sc[:, dc, :],
                         start=(dc == 0), stop=(dc == 7))
            for fc in range(8):
                k.ts(raw[:, (l * 3 + part) * 8 + fc, :], ps[:, fc * 4:fc * 4 + 3],
                     spcol(g, "mod_b", (l * 3 + part) * 8 + fc), None, ALU.add)
    for l in range(DEPTH):
        for dc in range(8):
            k.ts(g.modA[:, l, dc, :], raw[:, (l * 3 + 1) * 8 + dc, :], 1.0, spcol(g, "norm_pre", l * 8 + dc), ALU.add, ALU.mult)
            k.copy(g.modB[:, l, dc, :], raw[:, (l * 3 + 0) * 8 + dc, :])
            k.ts(g.modG[:, l, dc, :], raw[:, (l * 3 + 2) * 8 + dc, :], spcol(g, "norm_post", l * 8 + dc), None, ALU.mult)


def even_specs(g):
    return [
        (0, 1024, "raw32", g.XA),
        (1024, 1024, "silu16", g.GA),
        (2048, 1024, "rope16", g.Q),
        (3072, 256, "rope16", g.K),
        (3328, 256, "tok16", g.VT),
        (3584, 1024, "silu16", g.GB),
    ]


def load_weight_bf16(g, name, w_dram, ncols, nchunks):
    k = g.k
    wb = []
    for dc in range(nchunks):
        t = k.sbt(f"{name}{dc}", [128, ncols], BF16)
        step = 2304 if ncols % 2304 == 0 else ncols
        for c0 in range(0, ncols, step):
            k.dma(t[:, c0:c0 + step], w_dram[dc * 128:(dc + 1) * 128, c0:c0 + step], q="pool")
        wb.append(t)
    return wb


def norm_load(g, b, tt, xsrc, xs):
    k = g.k
    t0 = tt * TT
    k.dma(xs, xsrc[b][:, t0:t0 + TT].re("(dc p) t -> p dc t", p=128), q="pool")


def norm_compute(g, l, b, tt, xs, hT, scr, msbank):
    k = g.k
    j = 2 if tt == 0 else b
    sq = scr["sq"]
    k.act(sq, xs, AF.Square)
    ms = msbank[:, 0:TT]
    for dc in range(8):
        k.mm(ms, g.ones_b, sq[:, dc, :], start=(dc == 0), stop=(dc == 7))
    rstd = scr["rstd"]
    k.act(rstd, ms, AF.Sqrt, bias=g.epsc, scale=1.0 / D)
    k.recip(rstd, rstd)
    tmp = scr["tmp"]
    for dc in range(8):
        k.tt(tmp[:, dc, :], xs[:, dc, :], rstd, ALU.mult)
        k.ts(hT[:, dc, :], tmp[:, dc, :], g.modA[:, l, dc, j:j + 1], g.modB[:, l, dc, j:j + 1],
             ALU.mult, ALU.add)


def phase_proj(g, l, w_dram, F, specs):
    k = g.k
    k.phase()
    wb = load_weight_bf16(g, "wb", w_dram, F, 8)
    xsrc = g.xin if l == 0 else g.XR
    g.epsc = k.sbt("epsc", [128, 1])
    k.memset(g.epsc, EPS)
    need_rope = any(s[2] == "rope16" for s in specs)
    if need_rope:
        ropec = k.sbt("ropec", [128, T])
        ropes = k.sbt("ropes", [128, T])
        k.dma(ropec, g.cst["ropec"], q="act")
        k.dma(ropes, g.cst["ropes"], q="act")
        permf = k.sbt("permf", [128, 128])
        k.dma(permf, g.cst["perm"], q="act")
        perm = k.sbt("perm", [128, 128], BF16)
        k.copy(perm, permf)
    xs = [k.sbt(f"xs{i}", [128, 8, TT]) for i in range(3)]
    hT = [k.sbt(f"hT{i}", [128, 8, TT], BF16) for i in range(2)]
    scr = {"sq": k.sbt("sq", [128, 8, TT], BF16), "rstd": k.sbt("rstd", [128, TT]),
           "tmp": k.sbt("tmp", [128, 8, TT])}
    st32 = [k.sbt(f"st32_{i}", [128, 8, TT]) for i in range(2)]
    st16 = [k.sbt(f"st16_{i}", [128, 8, TT], BF16) for i in range(2)]
    stt16 = [k.sbt(f"stt16_{i}", [128, 512], BF16) for i in range(2)]
    qb = [k.sbt(f"qb{i}", [128, TT], BF16) for i in range(2)]
    t1 = [k.sbt(f"rt1_{i}", [128, TT]) for i in range(2)]
    t2 = [k.sbt(f"rt2_{i}", [128, TT]) for i in range(2)]
    cnt = {"ps": 0, "s32": 0, "s16": 0, "t16": 0, "q": 0, "dq": 0}
    pend_rope = []
    tiles = [(b_, t_) for b_ in range(NB) for t_ in range(NT)]
    norm_load(g, tiles[0][0], tiles[0][1], xsrc, xs[0])
    norm_load(g, tiles[1][0], tiles[1][1], xsrc, xs[1])
    norm_compute(g, l, tiles[0][0], tiles[0][1], xs[0], hT[0], scr, g.P[7])
    for it, (b, tt) in enumerate(tiles):
        if True:
            t0 = tt * TT
            h_ = hT[it % 2]
            if it + 2 < len(tiles):
                norm_load(g, tiles[it + 2][0], tiles[it + 2][1], xsrc, xs[(it + 2) % 3])
            if it + 1 < len(tiles):
                norm_compute(g, l, tiles[it + 1][0], tiles[it + 1][1], xs[(it + 1) % 3], hT[(it + 1) % 2], scr, g.P[7])
            for (c0, ncols, kind, dest) in specs:
                if kind == "tok16":
                    for sb in range(TT // 128):
                        for cc in range(0, ncols, 512):
                            cw = min(512, ncols - cc)
                            ps = g.P[6][:, 0:cw]
                            for dc in range(8):
                                k.mm(ps, h_[:, dc, sb * 128:(sb + 1) * 128], wb[dc][:, c0 + cc:c0 + cc + cw],
                                     start=(dc == 0), stop=(dc == 7))
                            st = stt16[cnt["t16"] % 2][:, 0:cw]
                            cnt["t16"] += 1
                            scale = dest_scale(kind, dest, g)
                            k.act(st, ps, AF.Copy, scale=scale)
                            dstt = dest[0] if isinstance(dest, tuple) else dest
                            r0t = c0 - spec_base(specs, dstt)
                            k.dma(dstt[b][t0 + sb * 128:t0 + (sb + 1) * 128, r0t + cc:r0t + cc + cw], st,
                                  q="sp")
                    continue
                dst = dest[0] if isinstance(dest, tuple) else dest
                for gc in range(0, ncols // 128, 8):
                    ng = min(8, ncols // 128 - gc)
                    if kind == "raw32":
                        stg = st32[cnt["s32"] % 2]
                        cnt["s32"] += 1
                    else:
                        stg = st16[cnt["s16"] % 2]
                        cnt["s16"] += 1
                    for ci in range(ng):
                        f0 = c0 + (gc + ci) * 128
                        ps = g.P[cnt["ps"] % 4][:, 0:TT]
                        cnt["ps"] += 1
                        for dc in range(8):
                            k.mm(ps, wb[dc][:, f0:f0 + 128], h_[:, dc, :], start=(dc == 0), stop=(dc == 7))
                        while pend_rope:
                            pend_rope.pop(0)()
                        if kind == "raw32":
                            k.copy(stg[:, ci, :], ps, eng="dve")
                        elif kind == "silu16":
                            k.act(stg[:, ci, :], ps, AF.Silu)
                        elif kind == "copy16":
                            k.act(stg[:, ci, :], ps, AF.Copy, scale=dest_scale(kind, dest, g))
                        elif kind == "rope16":
                            q_ = qb[cnt["q"] % 2]
                            a_ = t1[cnt["q"] % 2]
                            b_ = t2[cnt["q"] % 2]
                            pp = g.P[4 + cnt["q"] % 2][:, 0:TT]
                            cnt["q"] += 1
                            k.act(q_, ps, AF.Copy)

                            def fin(q_=q_, a_=a_, b_=b_, pp=pp, ps=ps, dst_=stg[:, ci, :]):
                                k.mm(pp, perm, q_)
                                k.tt(a_, ps, ropec[:, t0:t0 + TT], ALU.mult)
                                k.tt(b_, pp, ropes[:, t0:t0 + TT], ALU.mult)
                                k.tt(dst_, a_, b_, ALU.add)
                            pend_rope.append(fin)
                    while pend_rope:
                        pend_rope.pop(0)()
                    r0 = (c0 - spec_base(specs, dst)) + gc * 128
                    k.dma(dst[b][r0:r0 + ng * 128, t0:t0 + TT].re("(c p) t -> p c t", p=128),
                          stg[:, 0:ng, :], q="act" if cnt["dq"] % 2 else "sp")
                    cnt["dq"] += 1


def spec_base(specs, dst):
    for (c0, ncols, kind, dest) in specs:
        d = dest[0] if isinstance(dest, tuple) else dest
        if d is dst:
            return c0
    raise KeyError


def dest_scale(kind, dest, g):
    if isinstance(dest, tuple):
        return dest[1]
    return 1.0


def rev(v, lo, hi):
    a = v.ap[:, lo:hi]
    pat = [list(p) for p in a.ap]
    off = a.offset + (hi - lo - 1) * pat[-1][0]
    pat[-1][0] = -pat[-1][0]
    return V(AP(a.tensor, off, pat), v.units)


def dwconv(k, out, x, wcols, bcol, taps, left, segs, eng="dve"):
    k.ts(out, x, wcols[left], bcol, ALU.mult, ALU.add, eng=eng)
    for kk in range(taps):
        o = kk - left
        if o == 0:
            continue
        for (lo, hi) in segs:
            a = lo + max(0, -o)
            bnd = hi - max(0, o)
            k.stt(out[:, a:bnd], x[:, a + o:bnd + o], wcols[kk], out[:, a:bnd], ALU.mult, ALU.add, eng=eng)


def phase_lru(g, l, e):
    k = g.k
    k.phase()
    segs = [(0, CTX), (CTX, T)]
    lam = spcol(g, f"lru_lam{e}", 0, 16)
    cA = k.sbt("cA", [128, 16])
    cA2 = k.sbt("cA2", [128, 16])
    k.act(cA, lam, AF.Exp, scale=-1.0)
    k.act(cA, cA, AF.Ln, bias=1.0)
    k.ts(cA2, cA, -16.0, None, ALU.mult)
    k.ts(cA, cA, -8.0, None, ALU.mult)
    one_c = k.sbt("one_c", [128, 1])
    k.memset(one_c, 1.0)
    wa = k.sbt("wa", [128, 2, 8, 128], BF16)
    wx = k.sbt("wx", [128, 2, 8, 128], BF16)
    for d in range(2):
        k.dma(wa[:, d], g.lru_wa[e][d].re("n c d -> c n d"), q="pool")
        k.dma(wx[:, d], g.lru_wx[e][d].re("n c d -> c n d"), q="pool")
    xa = [k.sbt(f"xa{i}", [128, T]) for i in range(2)]
    ga = [k.sbt(f"ga{i}", [128, T], BF16) for i in range(2)]
    u = k.sbt("u", [128, T])
    ub = k.sbt("ub", [128, T], BF16)
    r_ = k.sbt("r", [128, T])
    i_ = k.sbt("i", [128, T])
    s_ = k.sbt("s", [128, T])
    hh = [k.sbt(f"h{d}", [128, T]) for d in range(2)]
    yo = [k.sbt(f"yo{i}", [128, T], BF16) for i in range(2)]
    pc = 0
    items = [(b_, n_) for b_ in range(NB) for n_ in range(8)]

    def lru_load(i):
        b_, n_ = items[i]
        k.dma(xa[i % 2], g.XA[b_][n_ * 128:(n_ + 1) * 128, :], q="sp")
        k.dma(ga[i % 2], g.GA[b_][n_ * 128:(n_ + 1) * 128, :], q="act")
    lru_load(0)
    for it, (b, n) in enumerate(items):
        if True:
            xa_ = xa[it % 2]
            ga_ = ga[it % 2]
            yo_ = yo[it % 2]
            if it + 1 < len(items):
                lru_load(it + 1)
            wc = [spcol(g, f"lru_cw{e}", kk * 8 + n) for kk in range(4)]
            dwconv(k, u, xa_, wc, spcol(g, f"lru_cb{e}", n), 4, 2, segs)
            k.copy(ub, u, eng="pool")
            for d in range(2):
                for c0 in range(0, T, 512):
                    cw = min(512, T - c0)
                    pr = g.P[pc % 8][:, 0:cw]
                    pi = g.P[(pc + 1) % 8][:, 0:cw]
                    pc += 2
                    k.mm(pr, wa[:, d, n, :], ub[:, c0:c0 + cw])
                    k.mm(pi, wx[:, d, n, :], ub[:, c0:c0 + cw])
                    k.act(r_[:, c0:c0 + cw], pr, AF.Sigmoid, bias=spcol(g, f"lru_ba{e}", d * 8 + n))
                    k.act(i_[:, c0:c0 + cw], pi, AF.Sigmoid, bias=spcol(g, f"lru_bx{e}", d * 8 + n))
                k.act(s_, r_, AF.Exp, scale=cA2[:, d * 8 + n:d * 8 + n + 1])
                k.act(r_, r_, AF.Exp, scale=cA[:, d * 8 + n:d * 8 + n + 1])
                k.ts(s_, s_, 1.0, None, ALU.min)
                k.act(s_, s_, AF.Sqrt, bias=one_c, scale=-1.0)
                k.tt(i_, i_, u, ALU.mult, eng="pool")
                k.tt(i_, i_, s_, ALU.mult)
                h_ = hh[d]
                if d == 0:
                    k.scan(h_, r_, i_, 0.0)
                else:
                    k.scan(rev(h_, 0, CTX), rev(r_, 0, CTX), rev(i_, 0, CTX), 0.0)
                    k.scan(rev(h_, CTX, T), rev(r_, CTX, T), rev(i_, CTX, T), h_[:, 0:1])
            k.tt(hh[0], hh[0], hh[1], ALU.add, eng="pool")
            k.tt(yo_, hh[0], ga_, ALU.mult, eng="pool")
            k.dma(g.MIXT[b][n * 128:(n + 1) * 128, :], yo_, q="pool")


def phase_attn(g, l, e):
    k = g.k
    k.phase()
    scale = 128 ** -0.5
    mprev_f = k.sbt("mprev_f", [128, 512])
    mnext_f = k.sbt("mnext_f", [128, 512])
    k.dma(mprev_f, g.cst["mprev"], q="act")
    k.dma(mnext_f, g.cst["mnext"], q="act")
    mprev = k.sbt("mprev", [128, 512], BF16)
    mnext = k.sbt("mnext", [128, 512], BF16)
    k.copy(mprev, mprev_f)
    k.copy(mnext, mnext_f)
    sinkexp = k.sbt("sinkexp", [128, 8])
    k.act(sinkexp, spcol(g, f"sink{e}", 0, 8), AF.Exp)
    onesf = k.sbt("onesf", [128, 128])
    k.memset(onesf, 1.0)
    sinkrow = k.sbt("sinkrow", [128, 1024])
    for h_ in range(8):
        k.ts(sinkrow[:, h_ * 128:(h_ + 1) * 128], onesf, sinkexp[:, h_:h_ + 1], None, ALU.mult)
    NBLK = T // 128
    Kt = [k.sbt(f"Kt{i}", [128, 2, T], BF16) for i in range(2)]
    Vt = [k.sbt(f"Vt{i}", [128, NBLK, 256], BF16) for i in range(2)]
    Qt = [k.sbt(f"Qt{i}", [128, 1024], BF16) for i in range(2)]
    Gt = [k.sbt(f"Gt{i}", [128, 1024], BF16) for i in range(2)]
    pT = [k.sbt(f"pT{i}", [128, 512], BF16) for i in range(12)]
    den = [k.sbt(f"den{i}", [128, 512]) for i in range(2)]
    ost = [k.sbt(f"ost{i}", [128, 1024], BF16) for i in range(2)]
    otmp = [k.sbt(f"otmp{i}", [128, 512]) for i in range(2)]
    pcnt = 0
    ppc = 0
    items = [(b_, k_) for b_ in range(NB) for k_ in range(NBLK)]

    def attn_load(i):
        b_, k_ = items[i]
        if k_ == 0:
            k.dma(Kt[b_], g.K[b_].re("(h p) t -> p h t", p=128), q="sp")
            k.dma(Vt[b_], g.VT[b_].re("(n p) c -> p n c", p=128), q="act")
        cc = k_ * 128
        k.dma(Qt[i % 2].re("p (h t) -> p h t", h=8), g.Q[b_][:, cc:cc + 128].re("(h p) t -> p h t", p=128), q="sp")
        k.dma(Gt[i % 2].re("p (h t) -> p h t", h=8), g.GB[b_][:, cc:cc + 128].re("(h p) t -> p h t", p=128), q="act")
    iters = [(bi_, hh_) for bi_ in range(len(items)) for hh_ in range(2)]

    def emit_qk(j):
        nonlocal pcnt
        bi, hh = iters[j]
        b, blk = items[bi]
        Kb = Kt[b]
        q_ = Qt[bi % 2]
        if blk < 2:
            kbs = [(0, None), (1, None)]
        else:
            kbs = []
            if blk > 2:
                kbs.append((blk - 1, mprev))
            kbs.append((blk, None))
            if blk < NBLK - 1:
                kbs.append((blk + 1, mnext))
            kbs += [(0, None), (1, None)]
        rhs = q_[:, hh * 512:(hh + 1) * 512]
        pts = []
        for (kb, msk) in kbs:
            pst = g.P[pcnt % 4]
            p_ = pT[pcnt % 12]
            pcnt += 1
            k.mm(pst, Kb[:, hh, kb * 128:(kb + 1) * 128], rhs)
            k.act(p_, pst, AF.Exp, scale=scale)
            if msk is not None:
                k.tt(p_, p_, msk, ALU.mult, eng="pool")
            pts.append((kb, p_))
        return pts

    def emit_pv(j, pts):
        bi, hh = iters[j]
        b, blk = items[bi]
        Vb = Vt[b]
        g_ = Gt[bi % 2]
        o_ = ost[bi % 2]
        po = g.P[4 + j % 2]
        pd = g.P[6 + j % 2]
        for ii, (kb, p_) in enumerate(pts):
            k.mm(po, Vb[:, kb, hh * 128:(hh + 1) * 128], p_, start=(ii == 0), stop=(ii == len(pts) - 1))
        for ii, (kb, p_) in enumerate(pts):
            k.mm(pd, g.ones_b, p_, start=(ii == 0), stop=(ii == len(pts) - 1))
        dn = den[j % 2]
        ot = otmp[j % 2]
        k.tt(dn, pd, sinkrow[:, hh * 512:(hh + 1) * 512], ALU.add)
        k.recip(dn, dn)
        k.tt(ot, po, dn, ALU.mult)
        k.tt(o_[:, hh * 512:(hh + 1) * 512], ot, g_[:, hh * 512:(hh + 1) * 512], ALU.mult, eng="pool")
        if hh == 1:
            c0 = blk * 128
            k.dma(g.MIXT[b][1024:2048, c0:c0 + 128].re("(h p) t -> p h t", p=128),
                  o_.re("p (h t) -> p h t", h=8), q="pool")
            if bi + 2 < len(items):
                attn_load(bi + 2)

    attn_load(0)
    attn_load(1)
    pts_next = emit_qk(0)
    for j in range(len(iters)):
        pts = pts_next
        if j + 1 < len(iters):
            pts_next = emit_qk(j + 1)
        emit_pv(j, pts)


def phase_out(g, l, w_dram, last):
    k = g.k
    k.phase()
    wo = load_weight_bf16(g, "wo", w_dram, D, 16)
    xsrc = g.xin if l == 0 else g.XR
    epsc = k.sbt("epsc", [128, 1])
    k.memset(epsc, EPS)
    mix = [k.sbt(f"mix{i}", [128, 16, TT], BF16) for i in range(2)]
    xs = [k.sbt(f"xs{i}", [128, 8, TT]) for i in range(2)]
    xn = [k.sbt(f"xn{i}", [128, 8, TT]) for i in range(2)]
    ysb = k.sbt("ysb", [128, 8, TT])
    sq = k.sbt("sq", [128, 8, TT], BF16)
    rstd = k.sbt("rstd", [128, TT])
    tmp = k.sbt("tmp", [128, TT])
    pc = 0
    tiles = [(b_, t_) for b_ in range(NB) for t_ in range(NT)]

    def out_load(i):
        b_, t_ = tiles[i]
        k.dma(mix[i % 2], g.MIXT[b_][:, t_ * TT:(t_ + 1) * TT].re("(c p) t -> p c t", p=128), q="pool")
        k.dma(xs[i % 2], xsrc[b_][:, t_ * TT:(t_ + 1) * TT].re("(c p) t -> p c t", p=128), q="pool")
    out_load(0)
    for it, (b, tt) in enumerate(tiles):
        if True:
            t0 = tt * TT
            j = 2 if tt == 0 else b
            m_ = mix[it % 2]
            x_ = xs[it % 2]
            n_ = xn[it % 2]
            if it + 1 < len(tiles):
                out_load(it + 1)
            ms = g.P[7][:, 0:TT]
            for dmc in range(8):
                ps = g.P[pc % 6][:, 0:TT]
                pc += 1
                for fc in range(16):
                    k.mm(ps, wo[fc][:, dmc * 128:(dmc + 1) * 128], m_[:, fc, :], start=(fc == 0), stop=(fc == 15))
                k.act(sq[:, dmc, :], ps, AF.Square)
                k.copy(ysb[:, dmc, :], ps, eng="dve")
            for dmc in range(8):
                k.mm(ms, g.ones_b, sq[:, dmc, :], start=(dmc == 0), stop=(dmc == 7))
            k.act(rstd, ms, AF.Sqrt, bias=epsc, scale=1.0 / D)
            k.recip(rstd, rstd)
            for dmc in range(8):
                k.tt(tmp, ysb[:, dmc, :], rstd, ALU.mult)
                k.stt(n_[:, dmc, :], tmp, g.modG[:, l, dmc, j:j + 1], x_[:, dmc, :], ALU.mult, ALU.add)
            dst = g.yout if last else g.XR
            k.dma(dst[b][:, t0:t0 + TT].re("(c p) t -> p c t", p=128), n_, q="sp")


def host_inputs(inp, core, consts, sp):
    b0 = core * NB
    xin = np.empty((NB, D, T), np.float32)
    for j in range(NB):
        xin[j, :, :CTX] = inp["ctx"][b0 + j].T
        xin[j, :, CTX:] = inp["x"][b0 + j].T
    cols = np.stack([inp["c"][b0], inp["c"][b0 + 1], inp["c_ctx"]], axis=-1)
    cT = np.ascontiguousarray(cols.reshape(8, 128, 3).transpose(1, 0, 2))
    m = {"xin": xin, "cT": cT, "sp": sp,
         "mod_w": inp["mod_w"], "ev_w_in": inp["ev_w_in"], "ev_w_out": inp["ev_w_out"],
         "lru_wa": inp["lru_wa"], "lru_wx": inp["lru_wx"],
         "od_w_in": inp["od_w_in"], "od_w_out": inp["od_w_out"], "hy_w1": inp["hy_w1"], "hy_w2": inp["hy_w2"],
         "hy_w3": inp["hy_w3"], "hy_bias": inp["hy_bias"]}
    m.update(consts)
    return m


_NC_CACHE = {}


def kernel(**inputs):
    inp = {k_: np.asarray(v, np.float32) for k_, v in inputs.items()}
    n_cores = 8
    if "nc" not in _NC_CACHE:
        _NC_CACHE["nc"] = build_program()
    nc = _NC_CACHE["nc"]
    consts = host_consts()
    sp = host_sp(inp)
    in_maps = [host_inputs(inp, c, consts, sp) for c in range(n_cores)]
    res = run_bass_kernel_spmd(nc, in_maps, core_ids=list(range(n_cores)))
    out = np.empty((16, S, D), np.float32)
    for c in range(n_cores):
        y = res.results[c]["yout"]
        for j in range(NB):
            out[c * NB + j] = y[j][:, CTX:].T
    return out


def odd_specs1(g):
    return [(0, 3072, "raw32", g.Z), (3072, 1024, "silu16", g.GH)]


def odd_specs2(g):
    ks = 128 ** -0.5
    return [(0, 1024, "copy16", g.RQ), (1024, 1024, "copy16", (g.RK, ks)), (1024, 1024, "tok16", (g.RKT, ks)),
            (2048, 1024, "tok16", g.RVT), (3072, 1024, "silu16", g.GD)]


def host_consts_odd():
    c = {}
    jj = np.arange(128, dtype=np.float32)[:, None]
    ii = np.arange(128, dtype=np.float32)[None, :]
    retc = np.zeros((128, 6 * 128 + 2), np.float32)
    retc[:, 0:128] = np.maximum(ii - jj, 0)
    retc[:, 128:256] = (ii >= jj)
    retc[:, 256:384] = np.maximum(jj - ii, 0)
    retc[:, 384:512] = (jj > ii)
    retc[:, 512:640] = ii + 1.0
    retc[:, 640:768] = 128.0 - ii
    retc[:, 768] = 127.0 - jj[:, 0]
    retc[:, 769] = jj[:, 0]
    c["retc"] = retc
    return c


def phase_ret(g, l, o):
    k = g.k
    k.phase()
    NBLK = T // 128
    retc = k.sbt("retc", [128, 770])
    k.dma(retc, g.cst["retc"], q="act")
    lg = k.sbt("lg", [128, 16])
    k.act(lg, spcol(g, f"ret_logit{o}", 0, 16), AF.Sigmoid)
    k.act(lg, lg, AF.Ln)
    inner = k.sbt("inner", [128, 16, 128])
    qdec = k.sbt("qdec", [128, 16, 128])
    kdec = k.sbt("kdec", [128, 16])
    cdec = k.sbt("cdec", [128, 16])
    for d in range(2):
        for h in range(8):
            c = d * 8 + h
            lgc = lg[:, c:c + 1]
            k.act(inner[:, c, :], retc[:, d * 256:d * 256 + 128], AF.Exp, scale=lgc)
            k.tt(inner[:, c, :], inner[:, c, :], retc[:, d * 256 + 128:d * 256 + 256], ALU.mult)
            k.act(qdec[:, c, :], retc[:, 512 + d * 128:640 + d * 128], AF.Exp, scale=lgc)
            k.act(kdec[:, c:c + 1], retc[:, 768 + d:769 + d], AF.Exp, scale=lgc)
    k.act(cdec, lg, AF.Exp, scale=128.0)
    epsc = k.sbt("epsc", [128, 1])
    k.memset(epsc, EPS)
    HP = 2
    qT = [k.sbt(f"qT{i}", [128, T], BF16) for i in range(HP)]
    kT = [k.sbt(f"kT{i}", [128, T], BF16) for i in range(HP)]
    ktok = [k.sbt(f"ktok{i}", [128, NBLK, 128], BF16) for i in range(HP)]
    vtok = [k.sbt(f"vtok{i}", [128, NBLK, 128], BF16) for i in range(HP)]
    gd = [k.sbt(f"gd{i}", [128, T], BF16) for i in range(HP)]
    oacc = [[k.sbt(f"oacc{i}_{d}", [128, T]) for d in range(2)] for i in range(HP)]
    Sf = [k.sbt(f"S{c}", [128, 128]) for c in range(4)]
    Sb = [k.sbt(f"Sb{c}", [128, 128], BF16) for c in range(4)]
    attS = [[k.sbt(f"attS{c}_{i}", [128, 128], BF16) for i in range(2)] for c in range(4)]
    qd = [[k.sbt(f"qd{c}_{i}", [128, 128], BF16) for i in range(2)] for c in range(4)]
    vdec = [[k.sbt(f"vdec{c}_{i}", [128, 128], BF16) for i in range(2)] for c in range(4)]
    sq = k.sbt("rsq", [128, 512], BF16)
    rstd = k.sbt("rrstd", [128, 512])
    yo = qT
    order = [list(range(NBLK)), [1, 0] + list(range(NBLK - 1, 1, -1))]
    for b in range(NB):
        for hp in range(8 // HP):
            for i in range(HP):
                h = hp * HP + i
                k.dma(qT[i], g.RQ[b][h * 128:(h + 1) * 128, :], q="sp")
                k.dma(kT[i], g.RK[b][h * 128:(h + 1) * 128, :], q="act")
                k.dma(ktok[i], g.RKT[b][:, h * 128:(h + 1) * 128].re("(n p) c -> p n c", p=128), q="sp")
                k.dma(vtok[i], g.RVT[b][:, h * 128:(h + 1) * 128].re("(n p) c -> p n c", p=128), q="act")
                k.dma(gd[i], g.GD[b][h * 128:(h + 1) * 128, :], q="sp")
            for s in range(NBLK):
                chains = [(i, d) for i in range(HP) for d in range(2)]

                def cvars(i, d):
                    h = hp * HP + i
                    ch = i * 2 + d
                    c = d * 8 + h
                    blk = order[d][s]
                    cs = slice(blk * 128, (blk + 1) * 128)
                    return h, ch, c, blk, cs
                for (i, d) in chains:
                    h, ch, c, blk, cs = cvars(i, d)
                    att = g.P[ch][:, 0:128]
                    a_ = attS[ch][s % 2]
                    k.mm(att, kT[i][:, cs], qT[i][:, cs])
                    k.tt(a_, att, inner[:, c, :], ALU.mult)
                    if s > 0:
                        k.tt(qd[ch][s % 2], qT[i][:, cs], qdec[:, c, :], ALU.mult, eng="pool")
                    if s < NBLK - 1:
                        k.ts(vdec[ch][s % 2], vtok[i][:, blk, :], kdec[:, c:c + 1], None, ALU.mult, eng="pool")
                if s < NBLK - 1:
                    for (i, d) in chains:
                        h, ch, c, blk, cs = cvars(i, d)
                        k.mm(g.P[4 + ch][:, 0:128], ktok[i][:, blk, :], vdec[ch][s % 2])
                for (i, d) in chains:
                    h, ch, c, blk, cs = cvars(i, d)
                    ops = g.P[ch][:, 128:256]
                    kv = g.P[4 + ch][:, 0:128]
                    k.mm(ops, vtok[i][:, blk, :], attS[ch][s % 2], start=True, stop=(s == 0))
                    if s > 0:
                        k.mm(ops, Sb[ch], qd[ch][s % 2], start=False, stop=True)
                    k.copy(oacc[i][d][:, cs], ops, eng="act")
                    if s < NBLK - 1:
                        if s == 0:
                            k.copy(Sf[ch], kv)
                        else:
                            k.stt(Sf[ch], Sf[ch], cdec[:, c:c + 1], kv, ALU.mult, ALU.add)
                        k.copy(Sb[ch], Sf[ch], eng="act")
            for i in range(HP):
                h = hp * HP + i
                oa = oacc[i][0]
                k.tt(oa, oa, oacc[i][1], ALU.add)
                for c0 in range(0, T, 512):
                    cw = min(512, T - c0)
                    k.act(sq[:, 0:cw], oa[:, c0:c0 + cw], AF.Square)
                    ms = g.P[i][:, 0:cw]
                    k.mm(ms, g.ones_b, sq[:, 0:cw])
                    k.act(rstd[:, 0:cw], ms, AF.Sqrt, bias=epsc, scale=1.0 / 128)
                    k.recip(rstd[:, 0:cw], rstd[:, 0:cw])
                    k.tt(oa[:, c0:c0 + cw], oa[:, c0:c0 + cw], rstd[:, 0:cw], ALU.mult)
                k.tt(yo[i], oa, gd[i], ALU.mult, eng="pool")
                k.dma(g.MIXT[b][1024 + h * 128:1024 + (h + 1) * 128, :], yo[i], q="sp")


def host_consts_hy():
    c = {}
    for L in (4096, 256):
        t = np.linspace(0.0, 1.0, L, dtype=np.float32)[:, None]
        bands = 16
        w = (2.0 * np.float32(math.pi) * np.arange(L, dtype=np.float32)[:, None] / np.float32(L)).astype(np.float32)
        f = np.linspace(1e-4, bands - 1, bands, dtype=np.float32)[None]
        z = np.concatenate([t, np.cos(f * w), -np.sin(f * w)], axis=-1).astype(np.float32)
        c[f"zemb{L}"] = np.ascontiguousarray(z.T)
        c[f"tneg{L}"] = np.ascontiguousarray((-t[:, 0]).reshape(L // 128, 128).T)
        N = 2 * L
        kk = np.arange(L, dtype=np.int64)
        prod = (kk[:, None] * kk[None, :]) % N
        ang = prod.astype(np.float64) * (2.0 * math.pi / N)
        c[f"cm{L}"] = np.cos(ang).astype(np.float32).astype(BF)
        c[f"sm{L}"] = np.sin(ang).astype(np.float32).astype(BF)
    max_decay = math.log(1e-2) / 0.3
    min_decay = math.log(1e-2) / 1.5
    deltas = np.linspace(min_decay, max_decay, 1024, dtype=np.float32)
    c["dabs"] = np.broadcast_to(np.abs(deltas)[None, :], (128, 1024)).astype(np.float32).copy()
    alt = np.where(np.arange(128) % 2 == 0, 1.0, -1.0).astype(np.float32)
    c["altc"] = alt[:, None].copy()
    c["altrow"] = np.where(np.arange(256) % 2 == 0, 1.0, -1.0).astype(np.float32)[None, :].copy()
    return c


def sin_reduce(k, out, arg, tmp_i, tmp_f):
    k.ts(arg, arg, 1.0 / (2 * math.pi), 64.5, ALU.mult, ALU.add)
    k.copy(tmp_i, arg)
    k.copy(tmp_f, tmp_i)
    k.tt(arg, arg, tmp_f, ALU.subtract)
    k.stt(arg, arg, 0.0, arg, ALU.is_lt, ALU.add)
    k.ts(arg, arg, 2 * math.pi, -math.pi, ALU.mult, ALU.add)
    k.ts(arg, arg, -math.pi, math.pi, ALU.max, ALU.min)
    k.act(out, arg, AF.Sin)


def phase_hyfilt(g, o, L):
    k = g.k
    k.phase()
    zemb = k.sbt("zemb", [33, L])
    k.dma(zemb, g.cst[f"zemb{L}"], q="sp")
    w1 = k.sbt("w1", [33, 64])
    k.dma(w1, g.hy_w1[o], q="act")
    w2 = k.sbt("w2", [64, 64])
    k.dma(w2, g.hy_w2[o], q="act")
    w3 = k.sbt("w3", [64, 4096])
    k.dma(w3, g.hy_w3[o], q="sp")
    tneg = k.sbt("tneg", [128, L // 128])
    k.dma(tneg, g.cst[f"tneg{L}"], q="act")
    dabs = k.sbt("dabs", [128, 1024])
    k.dma(dabs, g.cst["dabs"], q="act")
    bias = k.sbt("hbias", [1, 2048])
    k.dma(bias, g.hy_bias[o:o + 1].re("a b c -> a (b c)"), q="act")
    hid1 = k.sbt("hid1", [64, L])
    hid2 = k.sbt("hid2", [64, L])
    arg = k.sbt("arg", [64, 512])
    ti = k.sbt("ti", [64, 512], I32)
    tf = k.sbt("tf", [64, 512])
    b1 = spcol(g, f"hy_b1{o}")[0:64]
    b2 = spcol(g, f"hy_b2{o}")[0:64]
    fr = spcol(g, f"hy_freq{o}")[0:64]
    for (src, wgt, bcol, dst) in ((zemb, w1, b1, hid1), (hid1, w2, b2, hid2)):
        for c0 in range(0, L, 512):
            cw = min(512, L - c0)
            ps = g.P[0][0:64, 0:cw]
            k.mm(ps, wgt, src[:, c0:c0 + cw])
            k.ts(arg[:, 0:cw], ps, bcol, fr, ALU.add, ALU.mult)
            sin_reduce(k, dst[:, c0:c0 + cw], arg[:, 0:cw], ti[:, 0:cw], tf[:, 0:cw])
    dec = k.sbt("dec", [128, 1024])
    hf = k.sbt("hf", [128, 512])
    hb = k.sbt("hb", [128, 512])
    hs = [k.sbt(f"hs{i}", [128, 512], BF16) for i in range(2)]
    hd = [k.sbt(f"hd{i}", [128, 512], BF16) for i in range(2)]
    it = 0
    HS, HD = g.HS[L], g.HD[L]
    for tb in range(L // 128):
        k.act(dec, dabs, AF.Exp, scale=tneg[:, tb:tb + 1])
        for o2 in range(2):
            for ch in range(2):
                pf = g.P[1 + it % 2]
                pb = g.P[3 + it % 2]
                cf = (o2 * 2 + 0) * 1024 + ch * 512
                cb = (o2 * 2 + 1) * 1024 + ch * 512
                k.mm(pf, hid2[:, tb * 128:(tb + 1) * 128], w3[:, cf:cf + 512])
                k.mm(pb, hid2[:, tb * 128:(tb + 1) * 128], w3[:, cb:cb + 512])
                k.tt(hf, pf, dec[:, ch * 512:(ch + 1) * 512], ALU.mult)
                k.tt(hb, pb, dec[:, ch * 512:(ch + 1) * 512], ALU.mult)
                if tb == 0:
                    k.memset(hb[0:1, :], 0.0)
                    bo = o2 * 1024 + ch * 512
                    k.tt(hf[0:1, :], hf[0:1, :], bias[:, bo:bo + 512], ALU.add)
                s_ = hs[it % 2]
                d_ = hd[it % 2]
                it += 1
                k.tt(s_, hf, hb, ALU.add)
                k.tt(d_, hf, hb, ALU.subtract, eng="pool")
                k.dma(HS[o2][tb * 128:(tb + 1) * 128, ch * 512:(ch + 1) * 512], s_, q="sp")
                k.dma(HD[o2][tb * 128:(tb + 1) * 128, ch * 512:(ch + 1) * 512], d_, q="act")


def hy_tables(g, L):
    k = g.k
    nch = L // 128
    altf = k.sbt("altf", [128, 1])
    k.dma(altf, g.cst["altc"], q="act")
    altc = k.sbt("altc", [128, 1], BF16)
    k.copy(altc, altf)
    arf = k.sbt("arf", [1, 256])
    k.dma(arf, g.cst["altrow"], q="act")
    altrow = k.sbt("altrow", [1, 256], BF16)
    k.copy(altrow, arf)
    Ct = [k.sbt(f"Ct{i}", [128, nch, 128], BF16) for i in range(2)]
    St = [k.sbt(f"St{i}", [128, nch, 128], BF16) for i in range(2)]
    return altc, altrow, Ct, St


def load_ft(g, L, kc, Ct, St, it):
    k = g.k
    c_ = Ct[it % 2]
    s_ = St[it % 2]
    k.dma(c_, g.cst[f"cm{L}"][:, kc * 128:(kc + 1) * 128].re("(tc p) k -> p tc k", p=128), q="sp")
    k.dma(s_, g.cst[f"sm{L}"][:, kc * 128:(kc + 1) * 128].re("(tc p) k -> p tc k", p=128), q="act")
    return c_, s_


def phase_hyspec(g, o, L):
    k = g.k
    k.phase()
    nch = L // 128
    N = 2 * L
    altc, altrow, Ct, St = hy_tables(g, L)
    hs = k.sbt("hs_sb", [128, nch, 512], BF16)
    hd = k.sbt("hd_sb", [128, nch, 512], BF16)
    kr = [k.sbt(f"kr{i}", [128, 512]) for i in range(2)]
    ki = [k.sbt(f"ki{i}", [128, 512]) for i in range(2)]
    kn = k.sbt("kn", [1, 512])
    it = 0
    for o2 in range(2):
        for ch in range(2):
            k.dma(hs, g.HS[L][o2][:, ch * 512:(ch + 1) * 512].re("(tc p) c -> p tc c", p=128), q="sp")
            k.dma(hd, g.HD[L][o2][:, ch * 512:(ch + 1) * 512].re("(tc p) c -> p tc c", p=128), q="act")
            pn = g.P[4][0:1, :]
            for tc in range(nch):
                k.mm(pn, altc, hs[:, tc, :], start=(tc == 0), stop=(tc == nch - 1))
            k.act(kn, pn, AF.Copy, scale=1.0 / N)
            k.dma(g.KN[L][o2][:, ch * 512:(ch + 1) * 512], kn, q="sp")
            for kc in range(nch):
                c_, s_ = load_ft(g, L, kc, Ct, St, it)
                pa = g.P[it % 2]
                pb = g.P[2 + it % 2]
                r_ = kr[it % 2]
                i_ = ki[it % 2]
                it += 1
                for tc in range(nch):
                    k.mm(pa, c_[:, tc, :], hs[:, tc, :], start=(tc == 0), stop=(tc == nch - 1))
                for tc in range(nch):
                    k.mm(pb, s_[:, tc, :], hd[:, tc, :], start=(tc == 0), stop=(tc == nch - 1))
                k.act(r_, pa, AF.Copy, scale=2.0 / N)
                k.act(i_, pb, AF.Copy, scale=2.0 / N)
                if kc == 0:
                    k.ts(r_[0:1, :], r_[0:1, :], 0.5, None, ALU.mult)
                k.dma(g.KR[L][o2][kc * 128:(kc + 1) * 128, ch * 512:(ch + 1) * 512], r_, q="sp")
                k.dma(g.KI[L][o2][kc * 128:(kc + 1) * 128, ch * 512:(ch + 1) * 512], i_, q="act")


def phase_hyprep(g, o):
    k = g.k
    k.phase()
    NBLK = T // 128
    segs = [(0, CTX), (CTX, T)]
    z = [k.sbt(f"z{i}", [128, T]) for i in range(2)]
    zc = [k.sbt(f"zc{i}", [128, T]) for i in range(2)]
    zb = k.sbt("zb", [128, T], BF16)
    tok = [k.sbt(f"tok{i}", [128, NBLK, 128], BF16) for i in range(2)]
    it = 0
    pc = 0
    for b in range(NB):
        for chk in range(24):
            z_ = z[it % 2]
            zc_ = zc[it % 2]
            t_ = tok[it % 2]
            it += 1
            k.dma(z_, g.Z[b][chk * 128:(chk + 1) * 128, :], q="sp")
            wc = [spcol(g, f"hy_cw{o}", kk * 24 + chk) for kk in range(3)]
            dwconv(k, zc_, z_, wc, spcol(g, f"hy_cb{o}", chk), 3, 1, segs)
            if chk < 8:
                k.copy(zb, zc_, eng="act")
                for blk in range(NBLK):
                    pt = g.P[pc % 4].bc(BF16)[:, 0:128]
                    pc += 1
                    k.tr(pt, zb[:, blk * 128:(blk + 1) * 128], g.ident_b)
                    k.copy(t_[:, blk, :], pt, eng="act" if blk % 2 else "dve")
                k.dma(g.U1T[b][:, chk * 128:(chk + 1) * 128].re("(n p) c -> p n c", p=128), t_, q="act")
            else:
                k.dma(g.ZC[b][(chk - 8) * 128:(chk - 7) * 128, :], zc_, q="act")


def phase_hyconv(g, l, o, b, seg, o2):
    k = g.k
    k.phase()
    L = CTX if seg == 0 else S
    t_off = 0 if seg == 0 else CTX
    nch = L // 128
    altc, altrow, Ct, St = hy_tables(g, L)
    usrc = g.U1T if o2 == 0 else g.U2T
    u = k.sbt("u_sb", [128, nch, 512], BF16)
    Yr = k.sbt("Yr", [128, nch, 512], BF16)
    Yi = k.sbt("Yi", [128, nch, 512], BF16)
    ynq = k.sbt("ynq", [1, 512], BF16)
    knq = k.sbt("knq", [1, 512])
    kr = [k.sbt(f"kr{i}", [128, 512]) for i in range(2)]
    ki = [k.sbt(f"ki{i}", [128, 512]) for i in range(2)]
    t1 = k.sbt("t1", [128, 512])
    t2 = k.sbt("t2", [128, 512])
    Cn = [k.sbt(f"Cn{i}", [128, nch, 256], BF16) for i in range(1)]
    Sn = [k.sbt(f"Sn{i}", [128, nch, 256], BF16) for i in range(1)]
    xm = [k.sbt(f"xm{i}", [128, 256]) for i in range(2)]
    gh = [k.sbt(f"gh{i}", [128, 256], BF16) for i in range(2)]
    yb = [k.sbt(f"yb{i}", [128, 256], BF16) for i in range(2)]
    ytok = [k.sbt(f"ytok{i}", [128, 2, 128], BF16) for i in range(2)]
    it = 0
    it2 = 0
    it3 = 0
    for ch in range(2):
        k.dma(u, usrc[b][t_off:t_off + L, ch * 512:(ch + 1) * 512].re("(tc p) c -> p tc c", p=128), q="sp")
        k.dma(knq, g.KN[L][o2][:, ch * 512:(ch + 1) * 512], q="act")
        pn = g.P[6][0:1, :]
        for tc in range(nch):
            k.mm(pn, altc, u[:, tc, :], start=(tc == 0), stop=(tc == nch - 1))
        k.tt(ynq, pn, knq, ALU.mult)
        for kc in range(nch):
            c_, s_ = load_ft(g, L, kc, Ct, St, it)
            pa = g.P[it % 2]
            pb = g.P[2 + it % 2]
            r_ = kr[it % 2]
            i_ = ki[it % 2]
            it += 1
            k.dma(r_, g.KR[L][o2][kc * 128:(kc + 1) * 128, ch * 512:(ch + 1) * 512], q="sp")
            k.dma(i_, g.KI[L][o2][kc * 128:(kc + 1) * 128, ch * 512:(ch + 1) * 512], q="act")
            for tc in range(nch):
                k.mm(pa, c_[:, tc, :], u[:, tc, :], start=(tc == 0), stop=(tc == nch - 1))
            for tc in range(nch):
                k.mm(pb, s_[:, tc, :], u[:, tc, :], start=(tc == 0), stop=(tc == nch - 1))
            k.tt(t1, pa, r_, ALU.mult)
            k.tt(t2, pb, i_, ALU.mult)
            k.tt(Yr[:, kc, :], t1, t2, ALU.subtract)
            k.tt(t1, pa, i_, ALU.mult)
            k.tt(t2, pb, r_, ALU.mult)
            k.tt(Yi[:, kc, :], t1, t2, ALU.add)
        for nb in range(L // 256):
            cn = Cn[0]
            sn = Sn[0]
            it2 += 1
            k.dma(cn, g.cst[f"cm{L}"][:, nb * 256:(nb + 1) * 256].re("(kc p) n -> p kc n", p=128), q="sp")
            k.dma(sn, g.cst[f"sm{L}"][:, nb * 256:(nb + 1) * 256].re("(kc p) n -> p kc n", p=128), q="act")
            n0 = t_off + nb * 256
            for cs in range(4):
                crow = ch * 512 + cs * 128
                x_ = xm[it3 % 2]
                g_ = gh[it3 % 2]
                y_ = yb[it3 % 2]
                yt = ytok[it3 % 2]
                ps = g.P[4 + it3 % 2][:, 0:256]
                it3 += 1
                k.dma(x_, g.ZC[b][o2 * 1024 + crow:o2 * 1024 + crow + 128, n0:n0 + 256], q="sp")
                for kc in range(nch):
                    k.mm(ps, Yr[:, kc, cs * 128:(cs + 1) * 128], cn[:, kc, :], start=(kc == 0), stop=False)
                    k.mm(ps, Yi[:, kc, cs * 128:(cs + 1) * 128], sn[:, kc, :], start=False, stop=False)
                k.mm(ps, ynq[:, cs * 128:(cs + 1) * 128], altrow, start=False, stop=True)
                if o2 == 0:
                    k.tt(y_, ps, x_, ALU.mult)
                    for sb in range(2):
                        pt = g.P[7].bc(BF16)[:, sb * 128:(sb + 1) * 128]
                        k.tr(pt, y_[:, sb * 128:(sb + 1) * 128], g.ident_b)
                        k.copy(yt[:, sb, :], pt, eng="act")
                    k.dma(g.U2T[b][n0:n0 + 256, crow:crow + 128].re("(s p) c -> p s c", p=128), yt, q="act")
                else:
                    k.dma(g_, g.GH[b][crow:crow + 128, n0:n0 + 256], q="act")
                    k.tt(x_, ps, x_, ALU.mult)
                    k.tt(y_, x_, g_, ALU.mult, eng="pool")
                    k.dma(g.MIXT[b][crow:crow + 128, n0:n0 + 256], y_, q="act")


def host_consts_fft():
    c = {}
    N = 8192
    tc = np.arange(32)[:, None]
    k1 = np.arange(64)[None, :]
    a = 2 * np.pi * (tc * k1 % 64) / 64.0
    c["f1tab"] = np.concatenate([np.cos(a), np.sin(a)], axis=1).astype(np.float32).astype(BF)
    p = np.arange(128)[:, None, None]
    kk = (np.arange(64)[None, :, None] + 64 * np.arange(64)[None, None, :])
    ang = 2 * np.pi * ((kk * p) % N) / N
    c["t2tab"] = np.concatenate([np.cos(ang), np.sin(ang), -np.sin(ang), np.cos(ang)], axis=2).astype(np.float32).astype(BF)
    k2 = np.arange(64)[:, None, None]
    k1b = np.arange(64)[None, :, None]
    n2 = np.arange(128)[None, None, :]
    ang3 = 2 * np.pi * (((k1b + 64 * k2) * n2) % N) / N
    top = np.stack([np.cos(ang3), np.sin(ang3), -np.cos(ang3)], axis=2)
    bot = np.stack([np.sin(ang3), -np.cos(ang3), -np.sin(ang3)], axis=2)
    c["t3tab"] = np.concatenate([top, bot], axis=0).astype(np.float32).astype(BF)
    j = np.arange(64)[:, None]
    n1 = np.arange(32)[None, :]
    ag = 2 * np.pi * ((j * n1) % 64) / 64.0
    c["gtab"] = np.concatenate([np.cos(ag), -np.sin(ag)], axis=0).astype(np.float32).astype(BF)
    c["pm1"] = np.concatenate([np.ones(32), -np.ones(32)])[None, :].astype(np.float32)
    return c


def fft_f1(g, src, ch, Bd, f1tab, ubuf, bst, cnt):
    k = g.k
    for pg in range(4):
        u = ubuf[cnt["u"] % 2]
        cnt["u"] += 1
        k.dma(u, src[:, ch * 512:(ch + 1) * 512].re("(tc p) c -> tc p c", p=128)[:, pg * 32:(pg + 1) * 32, :], q="sp")
        for pq in range(4):
            st = bst[cnt["b"] % 2]
            cnt["b"] += 1
            for i in range(8):
                ps_ = pq * 8 + i
                pp = g.P[cnt["p"] % 8]
                cnt["p"] += 1
                k.mm(pp, f1tab, u[:, ps_, :])
                if i % 2:
                    k.act(st[:, i, :], pp, AF.Copy)
                else:
                    k.copy(st[:, i, :], pp, eng="dve")
            p0 = pg * 32 + pq * 8
            k.dma(Bd[:, p0:p0 + 8, :], st, q="act")


def phase_fft_f1(g, srcs):
    k = g.k
    k.phase()
    f1tab = k.sbt("f1tab", [32, 128], BF16)
    k.dma(f1tab, g.cst["f1tab"], q="act")
    ubuf = [k.sbt(f"fu{i}", [32, 32, 512], BF16) for i in range(2)]
    bst = [k.sbt(f"fb{i}", [128, 8, 512], BF16) for i in range(2)]
    cnt = {"u": 0, "b": 0, "p": 0}
    for (src, ch, Bd) in srcs:
        fft_f1(g, src, ch, Bd, f1tab, ubuf, bst, cnt)


def phase_hyspec2(g, o):
    k = g.k
    N = 8192
    for o2 in range(2):
        phase_fft_f1(g, [(g.HS[S][o2], 0, g.Bd[0]), (g.HS[S][o2], 1, g.Bd[1]),
                         (g.HD[S][o2], 0, g.Bd[2]), (g.HD[S][o2], 1, g.Bd[3])])
        k.phase()
        t2 = k.sbt("t2", [128, 64, 256], BF16)
        k.dma(t2, g.cst["t2tab"], q="sp")
        altf = k.sbt("altf", [128, 1])
        k.dma(altf, g.cst["altc"], q="act")
        altc = k.sbt("altc", [128, 1], BF16)
        k.copy(altc, altf)
        br = [[k.sbt(f"br{s}_{i}", [128, 8, 512], BF16) for i in range(2)] for s in range(2)]
        bs = [[k.sbt(f"bs{s}_{i}", [128, 8, 512], BF16) for i in range(2)] for s in range(2)]
        kr = [k.sbt(f"kr{i}", [64, 512], BF16) for i in range(2)]
        ks = [k.sbt(f"ks{i}", [64, 512], BF16) for i in range(2)]
        kn = k.sbt("kn", [1, 512])
        it = 0
        groups = [(c_, kg_) for c_ in range(2) for kg_ in range(8)]

        def sp_load(i):
            c_, kg_ = groups[i]
            for s_, Bd in ((0, g.Bd[c_]), (1, g.Bd[2 + c_])):
                k.dma(br[s_][i % 2], Bd[kg_ * 8:(kg_ + 1) * 8].re("k p c -> p k c"), q="sp")
                k.dma(bs[s_][i % 2], Bd[64 + kg_ * 8:64 + (kg_ + 1) * 8].re("k p c -> p k c"), q="sp")
        sp_load(0)
        for gi, (ch, kg) in enumerate(groups):
            bb = gi % 2
            if gi + 1 < len(groups):
                sp_load(gi + 1)
            for kk in range(8):
                k1 = kg * 8 + kk
                pr = g.P[it % 2][0:64, :]
                pi = g.P[2 + it % 2][0:64, :]
                r_ = kr[it % 2]
                s2 = ks[it % 2]
                it += 1
                k.mm(pr, t2[:, k1, 0:64], br[0][bb][:, kk, :], start=True, stop=False)
                k.mm(pr, t2[:, k1, 128:192], bs[0][bb][:, kk, :], start=False, stop=True)
                k.mm(pi, t2[:, k1, 64:128], br[1][bb][:, kk, :], start=True, stop=False)
                k.mm(pi, t2[:, k1, 0:64], bs[1][bb][:, kk, :], start=False, stop=True)
                k.act(r_, pr, AF.Copy, scale=2.0 / N)
                k.ts(s2, pi, 2.0 / N, None, ALU.mult)
                if k1 == 0:
                    k.ts(r_[0:1, :], pr[0:1, :], 1.0 / N, None, ALU.mult)
                    pn = g.P[4][0:1, :]
                    k.mm(pn, altc, br[0][bb][:, 0, :])
                    k.act(kn, pn, AF.Copy, scale=1.0 / N)
                    k.dma(g.KN2[o2][:, ch * 512:(ch + 1) * 512], kn, q="act")
                k.dma(g.KR2[o2][k1][:, ch * 512:(ch + 1) * 512], r_, q="act")
                k.dma(g.KS2[o2][k1][:, ch * 512:(ch + 1) * 512], s2, q="act")


def phase_hyconv2(g, l, o, b, o2):
    k = g.k
    usrc = g.U1T if o2 == 0 else g.U2T
    uv = usrc[b][CTX:CTX + S, :]
    phase_fft_f1(g, [(uv, 0, g.Bd[0]), (uv, 1, g.Bd[1])])
    k.phase()
    t2 = k.sbt("t2", [128, 64, 256], BF16)
    k.dma(t2, g.cst["t2tab"], q="sp")
    t3 = k.sbt("t3", [128, 64, 3, 128], BF16)
    k.dma(t3, g.cst["t3tab"], q="act")
    altf = k.sbt("altf", [128, 1])
    k.dma(altf, g.cst["altc"], q="act")
    altc = k.sbt("altc", [128, 1], BF16)
    k.copy(altc, altf)
    br = [k.sbt(f"br{i}", [128, 4, 512], BF16) for i in range(2)]
    bs = [k.sbt(f"bs{i}", [128, 4, 512], BF16) for i in range(2)]
    kr = [k.sbt(f"kr{i}", [128, 4, 512], BF16) for i in range(2)]
    ks = [k.sbt(f"ks{i}", [128, 4, 512], BF16) for i in range(2)]
    knq = [k.sbt(f"knq{i}", [1, 512]) for i in range(2)]
    p1 = [k.sbt(f"p1_{i}", [128, 512], BF16) for i in range(2)]
    p2 = [k.sbt(f"p2_{i}", [128, 512], BF16) for i in range(2)]
    dst_ = [k.sbt(f"dst{i}", [128, 2, 512], BF16) for i in range(2)]
    it = 0
    groups = [(c_, kg_) for c_ in range(2) for kg_ in range(16)]

    def f2_load(i):
        c_, kg_ = groups[i]
        bb_ = i % 2
        if kg_ == 0:
            k.dma(knq[c_], g.KN2[o2][:, c_ * 512:(c_ + 1) * 512], q="sp")
        k.dma(br[bb_], g.Bd[c_][kg_ * 4:(kg_ + 1) * 4].re("k p c -> p k c"), q="sp")
        k.dma(bs[bb_], g.Bd[c_][64 + kg_ * 4:64 + (kg_ + 1) * 4].re("k p c -> p k c"), q="sp")
        krv = g.KR2[o2][kg_ * 4:(kg_ + 1) * 4][:, :, c_ * 512:(c_ + 1) * 512].re("k q c -> q k c")
        ksv = g.KS2[o2][kg_ * 4:(kg_ + 1) * 4][:, :, c_ * 512:(c_ + 1) * 512].re("k q c -> q k c")
        k.dma(kr[bb_][0:64], krv, q="sp")
        k.dma(kr[bb_][64:128], krv, q="sp")
        k.dma(ks[bb_][0:64], ksv, q="sp")
        k.dma(ks[bb_][64:128], ksv, q="sp")
    f2_load(0)
    for gi, (ch, kg) in enumerate(groups):
        Dd = g.Dd[ch]
        bb = gi % 2
        if gi + 1 < len(groups):
            f2_load(gi + 1)
        for kk in range(4):
            k1 = kg * 4 + kk
            px = g.P[it % 2]
            pdr = g.P[2 + it % 2]
            pdi = g.P[4 + it % 2]
            a_ = p1[it % 2]
            b_ = p2[it % 2]
            d_ = dst_[it % 2]
            it += 1
            k.mm(px, t2[:, k1, 0:128], br[bb][:, kk, :], start=True, stop=False)
            k.mm(px, t2[:, k1, 128:256], bs[bb][:, kk, :], start=False, stop=True)
            if k1 == 0:
                pn = g.P[6][0:1, :]
                k.mm(pn, altc, br[bb][:, 0, :])
                k.tt(g.ynq[:, ch, :], pn, knq[ch], ALU.mult)
            k.tt(a_, px, kr[bb][:, kk, :], ALU.mult)
            k.tt(b_, px, ks[bb][:, kk, :], ALU.mult)
            k.mm(pdr, t3[:, k1, 0, :], a_, start=True, stop=False)
            k.mm(pdr, t3[:, k1, 1, :], b_, start=False, stop=True)
            k.mm(pdi, t3[:, k1, 1, :], a_, start=True, stop=False)
            k.mm(pdi, t3[:, k1, 2, :], b_, start=False, stop=True)
            k.act(d_[:, 0, :], pdr, AF.Copy)
            k.copy(d_[:, 1, :], pdi, eng="dve")
            k.dma(Dd[k1], d_[:, 0, :], q="act")
            k.dma(Dd[64 + k1], d_[:, 1, :], q="act")
    k.phase()
    gtab = k.sbt("gtab", [128, 32], BF16)
    k.dma(gtab, g.cst["gtab"], q="act")
    pmf = k.sbt("pmf", [1, 64])
    k.dma(pmf, g.cst["pm1"], q="act")
    pm = k.sbt("pm", [1, 64], BF16)
    k.copy(pm, pmf)
    NBL = S // 128
    altpat = k.sbt("altpat", [128, S])
    k.memset(altpat, 1.0)
    k.memset(altpat.re("p (a two) -> p a two", two=2)[:, :, 1], -1.0)
    ycol = [k.sbt(f"ycol{i}", [128, 1]) for i in range(2)]
    dt_ = [k.sbt(f"dt{i}", [128, 16, 512], BF16) for i in range(2)]
    yT = [k.sbt(f"yT{i}", [128, 32, 128]) for i in range(4)]
    xT = [k.sbt(f"xT{i}", [128, S]) for i in range(2)]
    ob = [k.sbt(f"ob{i}", [128, S], BF16) for i in range(2)]
    ghT = [k.sbt(f"ghT{i}", [128, S], BF16) for i in range(2)] if o2 == 1 else None
    tok = [k.sbt(f"tok{i}", [128, NBL, 128], BF16) for i in range(1)] if o2 == 0 else None
    it = 0
    pc = 0

    def x_load(ch_, cs_):
        crow_ = ch_ * 512 + cs_ * 128
        k.dma(xT[cs_ % 2], g.ZC[b][o2 * 1024 + crow_:o2 * 1024 + crow_ + 128, CTX:CTX + S], q="act")
        if o2 == 1:
            k.dma(ghT[cs_ % 2], g.GH[b][crow_:crow_ + 128, CTX:CTX + S], q="act")
    for ch in range(2):
        Dd = g.Dd[ch]
        x_load(ch, 0)
        x_load(ch, 1)
        for n2g in range(8):
            d_ = dt_[it % 2]
            it += 1
            k.dma(d_, Dd[:, n2g * 16:(n2g + 1) * 16, :], q="sp")
            for cs in range(4):
                ps = g.P[pc % 6]
                pc += 1
                for j in range(16):
                    k.mm(ps[:, j * 32:(j + 1) * 32], d_[:, j, cs * 128:(cs + 1) * 128], gtab, start=True, stop=True)
                k.copy(yT[cs][:, :, n2g * 16:(n2g + 1) * 16], ps.re("p (a b) -> p b a", a=16),
                       eng="act" if cs % 2 else "dve")
        for cs in range(4):
            ci = cs % 2
            crow = ch * 512 + cs * 128
            yflat = yT[cs].re("p a b -> p (a b)")
            pcol = g.P[7][:, 0:1]
            k.mm(pcol, g.ynq[:, ch, cs * 128:(cs + 1) * 128], pm[:, 0:1])
            k.copy(ycol[ci], pcol, eng="dve")
            k.stt(yflat, altpat, ycol[ci], yflat, ALU.mult, ALU.add)
            if o2 == 0:
                k.tt(ob[ci], yflat, xT[ci], ALU.mult)
                t_ = tok[0]
                for blk in range(NBL):
                    pt = g.P[6 + blk % 2].bc(BF16)[:, 0:128]
                    k.tr(pt, ob[ci][:, blk * 128:(blk + 1) * 128], g.ident_b)
                    k.copy(t_[:, blk, :], pt, eng="act" if blk % 2 else "dve")
                k.dma(g.U2T[b][CTX:CTX + S, crow:crow + 128].re("(n p) c -> p n c", p=128), t_, q="act")
            else:
                k.tt(yflat, yflat, xT[ci], ALU.mult)
                k.tt(ob[ci], yflat, ghT[ci], ALU.mult, eng="pool")
                k.dma(g.MIXT[b][crow:crow + 128, CTX:CTX + S], ob[ci], q="act")
            if cs + 2 < 4:
                x_load(ch, cs + 2)
```
